# Optimizing a Trainium2 kernel written in Bass

```python
import jax, jax.numpy as jnp
from jax import lax
import numpy as np

D_MODEL = 1024
BATCH = 4
SEQ = 4096
DEPTH = 1

D_MIX = D_MODEL
D_ATT = D_MIX // 2
D_POOL = D_MIX - D_ATT
N_ATT_HEADS = 8
HEAD_DIM = D_ATT // N_ATT_HEADS
POOL_WINDOWS = (2, 4, 8, 16)
N_POOL_GROUPS = len(POOL_WINDOWS)
POOL_GROUP_DIM = D_POOL // N_POOL_GROUPS
Q_BLOCK = 128
LN_EPS = 1e-5
FORGET_BIAS_INIT = 3.0
DEEPNORM_ALPHA = (2.0 * DEPTH) ** 0.25
DEEPNORM_BETA = (8.0 * DEPTH) ** -0.25
D_IN = 3 * D_ATT + N_ATT_HEADS + D_POOL + D_ATT + D_POOL
SPLIT_POINTS = (D_ATT, 2 * D_ATT, 3 * D_ATT, 3 * D_ATT + N_ATT_HEADS,
                3 * D_ATT + N_ATT_HEADS + D_POOL, 3 * D_ATT + N_ATT_HEADS + D_POOL + D_ATT)

kernel_name = "hymba_fox_poolformer_deepnorm_adaln"


def _layer_norm(h, g, b):
    h32 = h.astype(jnp.float32)
    mu = jnp.mean(h32, axis=-1, keepdims=True)
    var = jnp.mean(jnp.square(h32 - mu), axis=-1, keepdims=True)
    y = (h32 - mu) * lax.rsqrt(var + LN_EPS)
    return (y * g.astype(jnp.float32) + b.astype(jnp.float32)).astype(h.dtype)


def _forgetting_attention(q, k, v, f_logit):
    B, S, _ = q.shape
    q = q.reshape(B, S, N_ATT_HEADS, HEAD_DIM)
    k = k.reshape(B, S, N_ATT_HEADS, HEAD_DIM)
    v = v.reshape(B, S, N_ATT_HEADS, HEAD_DIM)
    log_f = jax.nn.log_sigmoid(f_logit.astype(jnp.float32))
    cum = jnp.cumsum(log_f, axis=1)
    nb = S // Q_BLOCK
    q_blocks = q.reshape(B, nb, Q_BLOCK, N_ATT_HEADS, HEAD_DIM).transpose(1, 0, 2, 3, 4)
    cum_blocks = cum.reshape(B, nb, Q_BLOCK, N_ATT_HEADS).transpose(1, 0, 2, 3)
    cum_k = cum.transpose(0, 2, 1)[:, :, None, :]
    k_pos = jnp.arange(S)
    scale = HEAD_DIM ** -0.5

    def one_block(args):
        q_blk, cum_q, idx = args
        s = jnp.einsum('bqhd,bkhd->bhqk', q_blk, k).astype(jnp.float32) * scale
        s = s + cum_q.transpose(0, 2, 1)[..., None] - cum_k
        q_pos = idx * Q_BLOCK + jnp.arange(Q_BLOCK)
        causal = k_pos[None, :] <= q_pos[:, None]
        s = jnp.where(causal[None, None], s, -jnp.inf)
        p = jax.nn.softmax(s, axis=-1).astype(v.dtype)
        return jnp.einsum('bhqk,bkhd->bqhd', p, v)

    out = lax.map(one_block, (q_blocks, cum_blocks, jnp.arange(nb)))
    return out.transpose(1, 0, 2, 3, 4).reshape(B, S, D_ATT)


def _multiscale_pool(p, w_pool_mix, b_pool_mix, pool_scale):
    B, S, _ = p.shape
    p32 = p.astype(jnp.float32)
    cs = jnp.cumsum(p32, axis=1)
    count = jnp.arange(1, S + 1, dtype=jnp.float32)
    outs = []
    for g, w in enumerate(POOL_WINDOWS):
        sl = slice(g * POOL_GROUP_DIM, (g + 1) * POOL_GROUP_DIM)
        cs_g = cs[..., sl]
        lagged = jnp.pad(cs_g, ((0, 0), (w, 0), (0, 0)))[:, :S]
        mean = (cs_g - lagged) / jnp.minimum(count, float(w))[None, :, None]
        outs.append(mean - p32[..., sl])
    pooled = jnp.stack(outs, axis=2).astype(p.dtype)
    mixed = jnp.einsum('bsgc,gce->bsge', pooled, w_pool_mix) + b_pool_mix
    return mixed.reshape(B, S, D_POOL) * pool_scale


def _hybrid_layer(x, c, w_ada, b_ada, w_in, b_in, w_pool_mix, b_pool_mix, pool_scale,
                  w_out, b_out, ln_g, ln_b):
    ada = jnp.einsum('bd,de->be', jax.nn.silu(c), w_ada) + b_ada
    shift, scale, gate = jnp.split(ada, 3, axis=-1)
    u = x * (1 + scale[:, None, :]) + shift[:, None, :]
    proj = jnp.einsum('bsd,de->bse', u, w_in) + b_in
    q, k, v, f_logit, p, g_att, g_pool = jnp.split(proj, SPLIT_POINTS, axis=-1)
    att = _forgetting_attention(q, k, v, f_logit)
    pool = _multiscale_pool(p, w_pool_mix, b_pool_mix, pool_scale)
    y = jnp.concatenate([att * jax.nn.silu(g_att), pool * jax.nn.silu(g_pool)], axis=-1)
    y = jnp.einsum('bse,ed->bsd', y, w_out) + b_out
    h = DEEPNORM_ALPHA * x + gate[:, None, :] * y
    return _layer_norm(h, ln_g, ln_b)


def setup_inputs(seed: int = 0) -> dict:
    key = jax.random.key(seed)
    ks = jax.random.split(key, 20)
    D = D_MODEL
    s_d = D ** -0.5
    x = jax.random.normal(ks[0], (BATCH, SEQ, D), jnp.float32)
    c = jax.random.normal(ks[1], (BATCH, D), jnp.float32)
    w_ada = jax.random.normal(ks[2], (DEPTH, D, 3 * D), jnp.float32) * s_d
    b_ada = 0.01 * jax.random.normal(ks[3], (DEPTH, 3 * D), jnp.float32)
    w_q = jax.random.normal(ks[4], (DEPTH, D, D_ATT), jnp.float32) * s_d
    w_k = jax.random.normal(ks[5], (DEPTH, D, D_ATT), jnp.float32) * s_d
    w_v = jax.random.normal(ks[6], (DEPTH, D, D_ATT), jnp.float32) * (s_d * DEEPNORM_BETA)
    w_f = jax.random.normal(ks[7], (DEPTH, D, N_ATT_HEADS), jnp.float32) * (0.1 * s_d)
    w_p = jax.random.normal(ks[8], (DEPTH, D, D_POOL), jnp.float32) * s_d
    w_g = jax.random.normal(ks[9], (DEPTH, D, D_ATT + D_POOL), jnp.float32) * s_d
    w_in = jnp.concatenate([w_q, w_k, w_v, w_f, w_p, w_g], axis=-1)
    b_in = 0.01 * jax.random.normal(ks[10], (DEPTH, D_IN), jnp.float32)
    b_in = b_in.at[:, 3 * D_ATT:3 * D_ATT + N_ATT_HEADS].add(FORGET_BIAS_INIT)
    w_pool_mix = jax.random.normal(ks[11], (DEPTH, N_POOL_GROUPS, POOL_GROUP_DIM, POOL_GROUP_DIM),
                                   jnp.float32) * POOL_GROUP_DIM ** -0.5
    b_pool_mix = 0.01 * jax.random.normal(ks[12], (DEPTH, N_POOL_GROUPS, POOL_GROUP_DIM), jnp.float32)
    pool_scale = 1.0 + 0.02 * jax.random.normal(ks[13], (DEPTH, D_POOL), jnp.float32)
    w_out = jax.random.normal(ks[14], (DEPTH, D_MIX, D), jnp.float32) * (D_MIX ** -0.5 * DEEPNORM_BETA)
    b_out = 0.01 * jax.random.normal(ks[15], (DEPTH, D), jnp.float32)
    ln_g = 1.0 + 0.02 * jax.random.normal(ks[16], (DEPTH, D), jnp.float32)
    ln_b = 0.01 * jax.random.normal(ks[17], (DEPTH, D), jnp.float32)
    return {"x": x, "c": c, "w_ada": w_ada, "b_ada": b_ada, "w_in": w_in, "b_in": b_in,
            "w_pool_mix": w_pool_mix, "b_pool_mix": b_pool_mix, "pool_scale": pool_scale,
            "w_out": w_out, "b_out": b_out, "ln_g": ln_g, "ln_b": ln_b}


def reference(x, c, w_ada, b_ada, w_in, b_in, w_pool_mix, b_pool_mix, pool_scale,
              w_out, b_out, ln_g, ln_b):
    for layer in range(DEPTH):
        x = _hybrid_layer(x, c, w_ada[layer], b_ada[layer], w_in[layer], b_in[layer],
                          w_pool_mix[layer], b_pool_mix[layer], pool_scale[layer],
                          w_out[layer], b_out[layer], ln_g[layer], ln_b[layer])
    return x
```

```python
import numpy as np
from contextlib import ExitStack
import concourse.bass as bass
import concourse.mybir as mybir
from concourse.bass_utils import run_bass_kernel_spmd

F32 = mybir.dt.float32
BF16 = mybir.dt.bfloat16
AF = mybir.ActivationFunctionType
ALU = mybir.AluOpType

NCORES = 8
SEQ = 4096
D = 1024
NBLK = 32
ALPHA = float(2.0 ** 0.25)
EPS = 1e-5
NEG = -30000.0
WINS = (2, 4, 8, 16)


class Tok:
    __slots__ = ("sem", "val", "eng", "key")

    def __init__(self, sem, val, eng, key):
        self.sem, self.val, self.eng, self.key = sem, val, eng, key


class Res:
    def __init__(self, name, track_reads=True):
        self.name = name
        self.w = None
        self.r = []
        self.track = track_reads


class DSem:
    def __init__(self, sem, key):
        self.sem, self.n, self.key = sem, 0, key


class Sched:
    ENGS = ("pe", "act", "dve", "pool", "sp")

    def __init__(self, nc, es):
        self.nc = nc
        self.es = es
        self.q = {e: [] for e in self.ENGS}
        self.sem = {e: es.enter_context(nc.semaphore("s_" + e)) for e in self.ENGS}
        self.cnt = {e: 0 for e in self.ENGS}
        self.seen = {e: {} for e in self.ENGS}
        self.nd = 0

    def dsem(self, name):
        self.nd += 1
        return DSem(self.es.enter_context(self.nc.semaphore("d_" + name)), "d%d" % self.nd)

    def _waits(self, eng, reads, writes, extra, is_dma=False):
        need = {}

        def add(t):
            if t is None:
                return
            if t.eng == eng and eng == "pe" and not is_dma:
                return
            if self.seen[eng].get(t.key, 0) >= t.val:
                return
            if need.get(t.key, (None, 0))[1] < t.val:
                need[t.key] = (t.sem, t.val)

        for r in reads:
            add(r.w)
        for w in writes:
            add(w.w)
            for t in w.r:
                add(t)
        for t in extra:
            add(t)
        for k, (s, v) in need.items():
            self.seen[eng][k] = v
        return list(need.values())

    def _commit(self, tok, reads, writes):
        for w in writes:
            w.w = tok
            w.r = []
        for r in reads:
            if r.track:
                r.r.append(tok)

    def op(self, eng, fn, reads=(), writes=(), extra=()):
        waits = self._waits(eng, reads, writes, extra)
        self.cnt[eng] += 1
        tok = Tok(self.sem[eng], self.cnt[eng], eng, eng)
        self.q[eng].append((waits, fn, (self.sem[eng], 1)))
        self._commit(tok, reads, writes)
        return tok

    def dma(self, eng, fn, ds, reads=(), writes=(), extra=()):
        waits = self._waits(eng, reads, writes, extra, is_dma=True)
        ds.n += 1
        tok = Tok(ds.sem, ds.n * 16, "dma", ds.key)
        self.q[eng].append((waits, fn, (ds.sem, 16)))
        self._commit(tok, reads, writes)
        return tok

    def wait_only(self, eng, toks):
        waits = self._waits(eng, (), (), toks)
        self.q[eng].append((waits, None, None))

    def replay(self, eng, e):
        for waits, fn, inc in self.q[eng]:
            for s, v in waits:
                e.wait_ge(s, v)
            if fn is not None:
                ins = fn(e)
                ins.then_inc(inc[0], inc[1])


def build_nc(debug=False):
    nc = bass.Bass("TRN2", target_bir_lowering=False)

    def din(name, shape):
        return nc.dram_tensor(name, list(shape), F32, kind="ExternalInput").ap()

    xp = din("xp", [SEQ, D])
    xpT = din("xpT", [D, SEQ])
    xhT = din("xhT", [D, 256])
    cT_d = din("cT", [128, 8])
    w_ada = din("w_ada", [D, 3 * D])
    b_adaT_d = din("b_adaT", [128, 16])
    b_gate_d = din("b_gate", [1, D])
    w_in = din("w_in", [D, 3080])
    b_qT_d = din("b_qT", [128, 4])
    b_kT_d = din("b_kT", [128, 4])
    b_gT_d = din("b_gT", [128, 8])
    b_vfp_d = din("b_vfp", [1, 1032])
    w_pm_d = din("w_pm", [4, 128, 128])
    b_pmT_d = din("b_pmT", [128, 4])
    pscT_d = din("pscT", [128, 4])
    w_out_d = din("w_out", [D, D])
    b_out_d = din("b_out", [1, D])
    ln_g_d = din("ln_g", [1, D])
    ln_b_d = din("ln_b", [1, D])
    ident_d = din("ident", [128, 128])
    ones_d = din("ones", [128, 128])
    U_d = din("U", [128, 128])
    pred_d = din("pred", [32, 32])
    masks_d = din("masks", [128, 2, 128])
    bm_d = din("bm", [128, 8, 128])
    bh_d = din("bh", [128, 36, 16])
    out_d = nc.dram_tensor("out", [2048, D], F32, kind="ExternalOutput").ap()

    wbf = nc.dram_tensor("wbf", [4, 128, 8, 512], BF16, kind="Internal").ap()
    xpT_v = xpT.rearrange("(kc p) t -> p kc t", p=128)
    xhT_v = xhT.rearrange("(kc p) t -> p kc t", p=128)
    w_ada_v = w_ada.rearrange("(kc p) e -> p kc e", p=128)
    w_in_v = w_in.rearrange("(kc p) e -> p kc e", p=128)
    w_out_v = w_out_d.rearrange("(kc p) e -> p kc e", p=128)

    with ExitStack() as es:
        S = Sched(nc, es)

        def sb(name, shape, dt=F32):
            return es.enter_context(nc.sbuf_tensor("sb_" + name, list(shape), dt))

        banks = [es.enter_context(nc.psum_tensor("ps%d" % i, [128, 512], F32)) for i in range(8)]
        bres = [Res("bank%d" % i) for i in range(8)]

        nlf = sb("nlf", [128, NBLK, 8])
        Cpos = sb("Cpos", [128, NBLK, 8])
        biasG = [sb("biasG%d" % i, [128, NBLK, 8]) for i in range(2)]
        ident = sb("ident", [128, 128])
        onesf = sb("onesf", [128, 128])
        Uf = sb("Uf", [128, 128])
        predf = sb("predf", [32, 32])
        tot = sb("tot", [32, 8])
        Z = sb("Z", [32, 256])
        identb = sb("identb", [128, 128], BF16)
        masks = sb("masks", [128, 2, 128], BF16)
        bm = sb("bm", [128, 8, 128], BF16)
        bh = sb("bh", [128, 36, 16], BF16)
        gate_bc = sb("gate_bc", [128, D])
        gb = sb("gb", [128, D])
        lng = sb("lng", [128, D])
        lnb = sb("lnb", [128, D])
        bvfp = sb("bvfp", [128, 1032])
        adaT = sb("adaT", [128, 16])
        scale1 = sb("scale1", [128, 8])
        b_adaT = sb("b_adaT", [128, 16])
        b_qT = sb("b_qT", [128, 4])
        b_kT = sb("b_kT", [128, 4])
        b_gT = sb("b_gT", [128, 8])
        b_pmT = sb("b_pmT", [128, 4])
        pscT = sb("pscT", [128, 4])
        cT = sb("cT", [128, 8])
        sc = sb("sc", [128, 8])
        phalo = sb("phalo", [128, 2, 512], BF16)
        wpm = sb("wpm", [128, 4, 128], BF16)
        NXB = 4
        xb = [sb("xb%d" % i, [128, D]) for i in range(NXB)]
        uT = [sb("uT%d" % i, [128, 8, 512], BF16) for i in range(2)]
        wA = sb("wA", [128, 8, 1032], BF16)
        wS = [sb("wS%d" % i, [128, 8, 512], BF16) for i in range(2)]
        QT = sb("QT", [128, 8, 512], BF16)
        gatt = sb("gatt", [128, 4, 512], BF16)
        scr = sb("scr", [128, 12, 512], BF16)
        yT = sb("yT", [128, 4, 512], BF16)
        gatt1 = sb("gatt1", [128, 4, 512], BF16)
        stats2 = [sb("stats2_%d" % i, [128, 2, 6]) for i in range(2)]
        mv2 = [sb("mv2_%d" % i, [128, 2]) for i in range(2)]
        ve2 = [sb("ve2_%d" % i, [128, 1]) for i in range(2)]
        rstd2 = [sb("rstd2_%d" % i, [128, 1]) for i in range(2)]
        mhalf = sb("mhalf", [128, 1])
        r_sm = [Res("sm0"), Res("sm1")]
        nbg = sb("nbg", [128, 8])
        tmpf = sb("tmpf", [128, 512])
        tmpf2 = sb("tmpf2", [128, 512])
        hb = [sb("hb%d" % i, [128, D]) for i in range(2)]
        rl = sb("rl", [128, 4])
        fl = sb("fl", [128, 4, 8])
        ex = sb("ex", [128, 4, 8])

        r_KT = Res("KT", False)
        r_V = Res("V", False)
        r_nlf = Res("nlf")
        r_Cpos = Res("Cpos", False)
        r_biasG = [Res("biasG0"), Res("biasG1")]
        r_const = Res("const", False)
        r_constb = Res("constb", False)
        r_ada = Res("ada", False)
        r_gate = Res("gate", False)
        r_gb = Res("gb", False)
        r_sc = Res("sc", False)
        r_xb = [Res("xb%d" % i) for i in range(NXB)]
        r_uT = [Res("uT0"), Res("uT1")]
        r_wA = Res("wA")
        r_wS = [Res("wS0"), Res("wS1")]
        r_QT = Res("QT")
        r_gatt = Res("gatt")
        r_scr = [Res("scr%d" % i) for i in range(12)]
        r_yT = [Res("yT%d" % i) for i in range(4)]
        r_small2 = Res("small2", False)
        r_tmpf = Res("tmpf")
        r_tmpf2 = Res("tmpf2")
        tmpfs = [tmpf, tmpf2]
        r_tmpfs = [r_tmpf, r_tmpf2]
        r_hb = [Res("hb0"), Res("hb1")]
        r_small = Res("small")
        r_rl = Res("rl")
        r_phalo = Res("phalo", False)
        r_fl = Res("fl")
        r_tot = Res("tot")
        r_Z = Res("Z")

        d_const = S.dsem("const")
        d_constb = S.dsem("constb")
        d_xb = [S.dsem("xb%d" % i) for i in range(NXB)]
        d_wa = [S.dsem("wa%d" % i) for i in range(3)]
        d_wA = S.dsem("wA")
        d_wS = [S.dsem("wS0"), S.dsem("wS1")]
        d_out = [S.dsem("out%d" % i) for i in range(4)]
        d_tmp = S.dsem("tmp")

        def cload(dst, src, eng="act"):
            r_const.w = S.dma(eng, lambda e, dst=dst, src=src: e.dma_start(out=dst, in_=src), d_const)

        r_c0 = Res("c0", False)
        d_c0 = S.dsem("c0")
        S.dma("act", lambda e: e.dma_start(out=cT[:, :], in_=cT_d[:, :]), d_c0)
        r_c0.w = S.dma("act", lambda e: e.dma_start(out=b_adaT[:, :], in_=b_adaT_d[:, :]), d_c0)
        S.op("act", lambda e: e.activation(sc[:, :], cT[:, :], AF.Silu), reads=[r_c0], writes=[r_sc])
        cload(ident[:, :], ident_d[:, :])
        cload(onesf[:, :], ones_d[:, :])
        cload(Uf[:, :], U_d[:, :])
        cload(predf[:, :], pred_d[:, :])
        cload(b_qT[:, :], b_qT_d[:, :])
        cload(b_kT[:, :], b_kT_d[:, :])
        cload(b_gT[:, :], b_gT_d[:, :])
        cload(b_pmT[:, :], b_pmT_d[:, :])
        cload(pscT[:, :], pscT_d[:, :])
        cload(bvfp[:, :], b_vfp_d.partition_broadcast(128))
        cload(lng[:, :], ln_g_d.partition_broadcast(128))
        cload(lnb[:, :], ln_b_d.partition_broadcast(128))
        cload(gb[:, :], b_out_d.partition_broadcast(128))
        cload(gate_bc[:, :], b_gate_d.partition_broadcast(128))

        def cloadb(dst, src):
            r_constb.w = S.dma("pool", lambda e, dst=dst, src=src: e.dma_start(out=dst, in_=src), d_constb)

        cloadb(identb[:, :], ident_d[:, :])
        cloadb(masks[:, :, :], masks_d[:, :, :])
        cloadb(bm[:, :, :], bm_d[:, :, :])
        cloadb(bh[:, :, :], bh_d[:, :, :])
        cloadb(wpm[:, :, :], w_pm_d.rearrange("g c e -> c g e"))
        for half in range(2):
            r_wA.w = S.dma("pool", lambda e, half=half: e.dma_start(out=wA[:, half * 4:(half + 1) * 4, :],
                                                                   in_=w_in_v[:, half * 4:(half + 1) * 4, 512:1544]),
                           d_wA)

        S.op("dve", lambda e: e.memset(mhalf[:, :], -0.5), writes=[r_small])
        r_wbf = [Res("wbf%d" % i, False) for i in range(4)]
        d_wbf = [S.dsem("wbf%d" % i) for i in range(4)]
        for hh in range(2):
            S.op("pool", lambda e, hh=hh: e.memset(QT[(1 - hh) * 64:(2 - hh) * 64, hh:8:2, :], 0.0), writes=[r_QT])

        with ExitStack() as es2:
            wa = [es2.enter_context(nc.sbuf_tensor("wa%d" % i, [128, 8, 512], F32)) for i in range(3)]
            scbc = es2.enter_context(nc.sbuf_tensor("scbc", [128, 8, 128], F32))
            r_wa = [Res("wa%d" % i) for i in range(3)]
            r_scbc = Res("scbc")

            S.op("dve", lambda e: e.tensor_copy(scbc[:, :, :], sc[:, :].unsqueeze(2).to_broadcast([128, 8, 128])),
                 reads=[r_sc], writes=[r_scbc])
            psA = banks[7]
            for j in range(6):
                buf = wa[j % 3]
                S.dma("sp", lambda e, buf=buf, j=j: e.dma_start(out=buf[:, :, :], in_=w_ada_v[:, :, j * 512:(j + 1) * 512]),
                      d_wa[j % 3], writes=[r_wa[j % 3]])
                if j < 4:
                    def mm(e, buf=buf, j=j):
                        ins = None
                        for i in range(4):
                            fc = j * 4 + i
                            for kc in range(8):
                                ins = e.matmul(psA[:, fc:fc + 1], buf[:, kc, i * 128:(i + 1) * 128], sc[:, kc:kc + 1],
                                               start=(kc == 0), stop=(kc == 7))
                        return ins
                    S.op("pe", mm, reads=[r_wa[j % 3], r_sc], writes=[bres[7]])
                else:
                    bk = 5 + (j - 4)

                    def mm(e, buf=buf, bk=bk):
                        ins = None
                        for kc in range(8):
                            ins = e.matmul(banks[bk][:, :], scbc[:, kc, :], buf[:, kc, :], start=(kc == 0), stop=(kc == 7))
                        return ins
                    S.op("pe", mm, reads=[r_wa[j % 3], r_scbc], writes=[bres[bk]])
                    half = j - 4
                    S.op("dve", lambda e, bk=bk, half=half: e.tensor_tensor(
                        gate_bc[:, half * 512:(half + 1) * 512], banks[bk][:, :],
                        gate_bc[:, half * 512:(half + 1) * 512], ALU.add),
                        reads=[bres[bk], r_const], writes=[r_gate])
                if j == 3:
                    S.op("dve", lambda e: e.tensor_tensor(adaT[:, :], psA[:, 0:16], b_adaT[:, :], ALU.add),
                         reads=[bres[7], r_c0], writes=[r_ada])
                    S.op("dve", lambda e: e.tensor_scalar_add(scale1[:, :], adaT[:, 8:16], 1.0), writes=[r_ada])
            S.op("pool", lambda e: e.tensor_tensor(gb[:, :], gb[:, :], gate_bc[:, :], ALU.mult),
                 reads=[r_gate, r_const], writes=[r_gb])
            t_end_ada = S.op("pe", lambda e: e.matmul(banks[7][:, 0:1], onesf[:, 0:128], onesf[:, 0:1], start=True, stop=True),
                             reads=[r_const], writes=[bres[7]])

        shiftT = adaT
        KT = sb("KT", [128, 4, SEQ], BF16)
        V = sb("V", [128, NBLK, 8, 66], BF16)
        S.op("pool", lambda e: e.memset(V[:, :, :, 64], 1.0), writes=[r_V], extra=[t_end_ada])

        xb_rot = [0]

        def load_x(src_rows, q="sp"):
            i = xb_rot[0] % NXB
            xb_rot[0] += 1
            S.dma(q, lambda e, i=i, src_rows=src_rows: e.dma_start(out=xb[i][:, :], in_=src_rows), d_xb[i],
                  writes=[r_xb[i]])
            return i

        def load_feat(src_v, tok0, ntok, q="sp"):
            per = 1024 // ntok
            bufs = []
            for kc0 in range(0, 8, per):
                i = xb_rot[0] % NXB
                xb_rot[0] += 1
                S.dma(q, lambda e, i=i, kc0=kc0: e.dma_start(
                    out=xb[i][:, :].rearrange("p (a b) -> p a b", b=ntok), in_=src_v[:, kc0:kc0 + per, tok0:tok0 + ntok]),
                    d_xb[i], writes=[r_xb[i]])
                bufs.append(i)
            return bufs

        def modulate_sb(kc, bufs, ntok, ut, eng):
            per = 1024 // ntok
            xi = bufs[kc // per]
            off = (kc % per) * ntok
            if eng == "act":
                S.op("act", lambda e: e.activation(uT[ut][:, kc, 0:ntok], xb[xi][:, off:off + ntok], AF.Identity,
                                                   bias=shiftT[:, kc:kc + 1], scale=scale1[:, kc:kc + 1]),
                     reads=[r_xb[xi], r_ada], writes=[r_uT[ut]])
            else:
                S.op(eng, lambda e: e.tensor_scalar(uT[ut][:, kc, 0:ntok], xb[xi][:, off:off + ntok],
                                                    scale1[:, kc:kc + 1], shiftT[:, kc:kc + 1], ALU.mult, ALU.add),
                     reads=[r_xb[xi], r_ada], writes=[r_uT[ut]])

        ENG_MIX = ["dve", "act", "pool", "act", "dve", "act", "pool", "act"]

        rotT = [0]
        rotKV = [0]
        KVB = [0, 1, 2, 3, 4, 5, 7]
        def emit_T(ch):
            bufs = load_feat(xpT_v, ch * 512, 512, "act" if ch == 0 else "sp")
            for kc in range(8):
                modulate_sb(kc, bufs, 512, ch % 2, "act" if ch == 0 else ENG_MIX[kc])

        def emit_K(ch):
            ut = ch % 2
            for pair in range(4):
                bank = KVB[rotKV[0] % len(KVB)]
                rotKV[0] += 1

                def mm(e, pair=pair, bank=bank, ut=ut):
                    ins = None
                    for kc in range(8):
                        ins = e.matmul(banks[bank][:, :], wA[:, kc, pair * 128:(pair + 1) * 128], uT[ut][:, kc, :],
                                       start=(kc == 0), stop=(kc == 7))
                    return ins
                S.op("pe", mm, reads=[r_wA, r_uT[ut]], writes=[bres[bank]])
                S.op("act", lambda e, pair=pair, bank=bank, ch=ch: e.activation(
                    KT[:, pair, ch * 512:(ch + 1) * 512], banks[bank][:, :], AF.Identity, bias=b_kT[:, pair:pair + 1], scale=1.0),
                    reads=[bres[bank], r_const], writes=[r_KT], extra=[t_end_ada])

        def emit_V(ch):
            ut = ch % 2
            for bi in range(4):
                pos = ch * 4 + bi
                bank = KVB[rotKV[0] % len(KVB)]
                rotKV[0] += 1

                def mm(e, bi=bi, bank=bank, ut=ut):
                    ins = None
                    for kc in range(8):
                        ins = e.matmul(banks[bank][:, :], uT[ut][:, kc, bi * 128:(bi + 1) * 128], wA[:, kc, 512:1024],
                                       start=(kc == 0), stop=(kc == 7))
                    return ins
                S.op("pe", mm, reads=[r_wA, r_uT[ut]], writes=[bres[bank]])
                S.op("dve", lambda e, pos=pos, bank=bank: e.tensor_tensor(
                    V[:, pos, :, 0:64], banks[bank][:, :].rearrange("p (h d) -> p h d", d=64),
                    bvfp[:, 0:512].rearrange("p (h d) -> p h d", d=64), ALU.add),
                    reads=[bres[bank], r_const], writes=[r_V], extra=[t_end_ada])

            def mmf(e, ut=ut):
                ins = None
                for bi in range(4):
                    for kc in range(8):
                        ins = e.matmul(banks[6][:, bi * 8:(bi + 1) * 8], uT[ut][:, kc, bi * 128:(bi + 1) * 128],
                                       wA[:, kc, 1024:1032], start=(kc == 0), stop=(kc == 7))
                return ins
            S.op("pe", mmf, reads=[r_wA, r_uT[ut]], writes=[bres[6]])
            S.op("dve", lambda e: e.tensor_tensor(
                fl[:, :, :], banks[6][:, 0:32].rearrange("p (b h) -> p b h", h=8),
                bvfp[:, 512:520].unsqueeze(1).to_broadcast([128, 4, 8]), ALU.add),
                reads=[bres[6], r_const], writes=[r_fl])
            S.op("act", lambda e: e.activation(ex[:, :, :], fl[:, :, :], AF.Exp, scale=-1.0), reads=[r_fl], writes=[r_small])
            S.op("act", lambda e, ch=ch: e.activation(nlf[:, ch * 4:(ch + 1) * 4, :], ex[:, :, :], AF.Ln, bias=1.0, scale=1.0),
                 reads=[r_small], writes=[r_nlf, r_fl])

        emit_T(0)
        for ch in range(8):
            emit_K(ch)
            if ch == 2:
                for c0, pc in ((0, 0), (2056, 1), (1544, 3), (2568, 2)):
                    S.dma("pool", lambda e, c0=c0, pc=pc: e.dma_start(out=wbf[pc, :, :, :], in_=w_in_v[:, :, c0:c0 + 512]),
                          d_wbf[pc], writes=[r_wbf[pc]], extra=[r_KT.w])
            if ch + 1 < 8:
                emit_T(ch + 1)
            emit_V(ch)

        def mmtot(e):
            ins = None
            for h in range(8):
                ins = e.matmul(banks[0][0:32, 256 + h:257 + h], nlf[:, :, h], onesf[:, 0:1], start=True, stop=True)
            return ins
        def cum1():
            S.op("pe", mmtot, reads=[r_nlf, r_const], writes=[bres[0]])
            S.op("dve", lambda e: e.tensor_copy(tot[:, :], banks[0][0:32, 256:264]), reads=[bres[0]], writes=[r_tot])
            S.op("dve", lambda e: e.tensor_tensor(
                Z[:, :].rearrange("p (b h) -> p b h", h=8), tot[:, :].unsqueeze(1).to_broadcast([32, 32, 8]),
                predf[:, :].unsqueeze(2).to_broadcast([32, 32, 8]), ALU.mult), reads=[r_tot, r_const], writes=[r_Z])

        def mmcum(e):
            e.matmul(banks[0][:, 0:256], Uf[:, :], nlf[:, :, :].rearrange("p b h -> p (b h)"), start=True, stop=False)
            return e.matmul(banks[0][:, 0:256], onesf[0:32, :], Z[:, :], start=False, stop=True)
        def cum2():
            S.op("pe", mmcum, reads=[r_nlf, r_Z, r_const], writes=[bres[0]])
            S.op("dve", lambda e: e.tensor_copy(Cpos[:, :, :].rearrange("p b h -> p (b h)"), banks[0][:, 0:256]),
                 reads=[bres[0]], writes=[r_Cpos])

        SB_ = [0, 1, 2, 3]
        OB_ = [4, 5]
        MB_ = [6, 7]
        rotM = [0]
        rotW = [0]

        def mbank():
            b = MB_[rotM[0] % len(MB_)]
            rotM[0] += 1
            return b

        PIECE = {0: 0, 2056: 1, 2568: 2, 1544: 3}

        def load_ws(c0, first=False):
            i = rotW[0] % 2
            rotW[0] += 1
            pc = PIECE[c0]
            S.dma("pool", lambda e, i=i, pc=pc: e.dma_start(out=wS[i][:, :, :], in_=wbf[pc, :, :, :]),
                  d_wS[i], reads=[r_wbf[pc]], writes=[r_wS[i]])
            return i

        PCOL = 1544
        QCOL = 0
        GACOL = 2056
        GPCOL = 2568

        On = hb[0][:, :].bitcast(BF16).rearrange("p (a b) -> p a b", b=512)
        PTv = hb[1][:, :].bitcast(BF16).rearrange("p (a b) -> p a b", b=512)
        r_On = Res("On")
        r_pt = [Res("pt%d" % i) for i in range(4)]
        QTb = [QT, uT[1]]
        r_QTb = [r_QT, r_uT[1]]
        gattb = [gatt, gatt1]
        r_gattb = [r_gatt, Res("gatt1")]
        nb_gT = nbg
        S.op("dve", lambda e: e.tensor_scalar(nb_gT[:, :], b_gT[:, :], 0.5, None, ALU.mult), reads=[r_const], writes=[r_small2])

        tm_rot = [0]

        def silu_evac(bank, c8, out_ap, out_res):
            ti = tm_rot[0] % 2
            tm_rot[0] += 1
            tm, r_tm = tmpfs[ti], r_tmpfs[ti]
            S.op("act", lambda e: e.activation(tm[:, :], banks[bank][:, :], AF.Tanh, bias=nb_gT[:, c8:c8 + 1], scale=0.5),
                 reads=[bres[bank], r_small2], writes=[r_tm])
            S.op("pool", lambda e: e.tensor_scalar(tm[:, :], tm[:, :], 0.5, 0.5, ALU.mult, ALU.add), reads=[r_tm], writes=[r_tm])
            S.op("dve", lambda e: e.scalar_tensor_tensor(out_ap, banks[bank][:, :], b_gT[:, c8:c8 + 1], tm[:, :], ALU.add, ALU.mult),
                 reads=[bres[bank], r_tm, r_const], writes=out_res)

        def group2(lhs_fn, rhs_fn, rd, bank):
            for part in range(2):
                def mm(e, part=part):
                    ins = None
                    for kc in range(part * 4, part * 4 + 4):
                        ins = e.matmul(banks[bank][:, :], lhs_fn(kc), rhs_fn(kc), start=(kc == 0), stop=(kc == 7))
                    return ins
                S.op("pe", mm, reads=rd, writes=[bres[bank]])
                yield

        mods_done = [False]
        GORD = [3, 2, 1, 0]

        def emit_biasG(G):
            bG = G % 2
            bank = mbank()
            S.op("pe", lambda e, bank=bank, G=G: e.matmul(banks[bank][:, 0:8], onesf[:, :], Cpos[:, 4 * G + 2, :], start=True, stop=True),
                 reads=[r_Cpos, r_const], writes=[bres[bank]])
            S.op("dve", lambda e, bank=bank, bG=bG: e.scalar_tensor_tensor(
                biasG[bG][:, :, :], banks[bank][:, 0:8].unsqueeze(1).to_broadcast([128, NBLK, 8]), -1.0 / 128.0,
                Cpos[:, :, :], ALU.mult, ALU.add), reads=[bres[bank], r_Cpos], writes=[r_biasG[bG]])

        def prelude(G, overlapped, n_sp=14):
            qb = (G + 1) % 2
            X, Y = 4 * ((G + 2) % 3), 4 * (G % 3)
            wq = load_ws(QCOL, G == 0)
            wga = load_ws(GACOL, G == 0)
            bufs = load_feat(xpT_v, G * 512, 512)
            for _ in range(n_sp):
                yield
            for kc in range(8):
                modulate_sb(kc, bufs, 512, 0, "pool" if overlapped else ENG_MIX[kc])
                if kc % 2 == 1:
                    yield
            mods_done[0] = True
            if G == GORD[0]:
                hbufs = load_feat(xhT_v, 0, 256)
                for kc in range(8):
                    modulate_sb(kc, hbufs, 256, 1, ENG_MIX[kc])
            if G == GORD[1]:
                for hh in range(2):
                    S.op("pool", lambda e, hh=hh: e.memset(uT[1][(1 - hh) * 64:(2 - hh) * 64, hh:8:2, :], 0.0), writes=[r_uT[1]])
            for c in range(4):
                bank = mbank()
                for _ in group2(lambda kc, c=c: wS[wq][:, kc, c * 128:(c + 1) * 128], lambda kc: uT[0][:, kc, :],
                                  [r_wS[wq], r_uT[0]], bank):
                    pass
                for hh in range(2):
                    S.op("dve", lambda e, hh=hh, c=c, bank=bank: e.tensor_scalar(
                        QTb[qb][hh * 64:(hh + 1) * 64, 2 * c + hh, :], banks[bank][hh * 64:(hh + 1) * 64, :],
                        b_qT[hh * 64:(hh + 1) * 64, c:c + 1], None, ALU.add),
                        reads=[bres[bank], r_const], writes=[r_QTb[qb]])
                yield
            wp = load_ws(PCOL, G == 0)
            for c in range(4):
                bank = mbank()
                for _ in group2(lambda kc, c=c: wS[wga][:, kc, c * 128:(c + 1) * 128], lambda kc: uT[0][:, kc, :],
                                  [r_wS[wga], r_uT[0]], bank):
                    pass
                silu_evac(bank, c, gattb[qb][:, c, :], [r_gattb[qb]])
                yield
            wgp = load_ws(GPCOL, G == 0)
            for mi in range(4):
                bank = mbank()
                for _ in group2(lambda kc, mi=mi: uT[0][:, kc, mi * 128:(mi + 1) * 128], lambda kc: wS[wp][:, kc, :],
                                  [r_wS[wp], r_uT[0]], bank):
                    pass
                S.op("dve", lambda e, mi=mi, bank=bank: e.tensor_tensor(scr[:, X + mi, :], banks[bank][:, :], bvfp[:, 520:1032], ALU.add),
                     reads=[bres[bank], r_const], writes=[r_scr[X + mi]])
                yield
            if G == GORD[0]:
                for hbk in range(2):
                    bank = mbank()
                    for _ in group2(lambda kc, hbk=hbk: uT[1][:, kc, hbk * 128:(hbk + 1) * 128], lambda kc: wS[wp][:, kc, :],
                                    [r_wS[wp], r_uT[1]], bank):
                        pass
                    S.op("dve", lambda e, hbk=hbk, bank=bank: e.tensor_tensor(phalo[:, hbk, :], banks[bank][:, :], bvfp[:, 520:1032], ALU.add),
                         reads=[bres[bank], r_const], writes=[r_phalo])
            for mi in range(4):
                m = 4 * G + mi
                bank = mbank()

                def mm(e, mi=mi, m=m, bank=bank):
                    ins = None
                    for g in range(4):
                        bmi = g if m > 0 else 4 + g
                        bhi = ((m % 8) * 4 + g) if m > 0 else 32 + g
                        e.matmul(banks[bank][:, g * 128:(g + 1) * 128], scr[:, X + mi, g * 128:(g + 1) * 128], bm[:, bmi, :],
                                 start=True, stop=False)
                        ins = e.matmul(banks[bank][:, g * 128:g * 128 + 16], phalo[:, m // 8, g * 128:(g + 1) * 128],
                                       bh[:, bhi, :], start=False, stop=True)
                    return ins
                S.op("pe", mm, reads=[r_scr[X + mi], r_phalo, r_constb], writes=[bres[bank]])
                S.op("dve", lambda e, mi=mi, bank=bank: e.tensor_copy(
                    scr[:, Y:Y + 4, mi * 128:(mi + 1) * 128], banks[bank][:, :].rearrange("p (g t) -> p g t", t=128)),
                    reads=[bres[bank]], writes=[r_scr[Y + g] for g in range(4)])
                yield
            for c in range(4):
                bank = mbank()
                for _ in group2(lambda kc, c=c: wS[wgp][:, kc, c * 128:(c + 1) * 128], lambda kc: uT[0][:, kc, :],
                                  [r_wS[wgp], r_uT[0]], bank):
                    pass
                silu_evac(bank, 4 + c, scr[:, X + c, :], [r_scr[X + c]])
                yield
            for g in range(4):
                bank = mbank()
                S.op("pe", lambda e, g=g, bank=bank: e.matmul(banks[bank][:, :], wpm[:, g, :], scr[:, Y + g, :], start=True, stop=True),
                     reads=[r_scr[Y + g], r_constb], writes=[bres[bank]])
                S.op("dve", lambda e, g=g, bank=bank: e.tensor_scalar(
                    tmpf[:, :], banks[bank][:, :], b_pmT[:, g:g + 1], pscT[:, g:g + 1], ALU.add, ALU.mult),
                    reads=[bres[bank], r_const], writes=[r_tmpf])
                S.op("pool", lambda e, g=g: e.tensor_tensor(scr[:, Y + g, :], tmpf[:, :], scr[:, X + g, :], ALU.mult),
                     reads=[r_tmpf, r_scr[X + g]], writes=[r_scr[Y + g]])
                yield
            if overlapped:
                emit_biasG(G)

        pg = prelude(GORD[0], False, 0)
        for _ in range(6):
            next(pg, None)
        cum1()
        for _ in range(8):
            next(pg, None)
        cum2()
        for _ in pg:
            pass


        epg = [None]
        for gi, G in enumerate(GORD):
            Gn = GORD[gi + 1] if gi + 1 < len(GORD) else None
            qb = (G + 1) % 2
            Y = 4 * (G % 3)
            nit_g = 8 * (8 * G + 8)
            sp_items = 40 if gi == 0 else 45
            stride = max(1, nit_g // (50 + sp_items))
            gen = prelude(Gn, True, -(-sp_items // stride)) if Gn is not None else None
            if gi == 0:
                if gen is not None:
                    next(gen)
                for half in range(2):
                    S.dma("pool", lambda e, half=half: e.dma_start(out=wA[:, half * 4:(half + 1) * 4, 0:1024],
                                                                  in_=w_out_v[:, half * 4:(half + 1) * 4, :]),
                          d_wA, writes=[r_wA])
            bG = G % 2
            if gi == 0:
                emit_biasG(G)

            nk = 4 * G + 4
            kblocks = [(i, i) for i in range(nk)] + [(16 + i, i) for i in range(nk)]
            items = []
            for h in range(8):
                for j, (pos, i) in enumerate(kblocks):
                    items.append((h, pos, i, j == 0, j == len(kblocks) - 1))
            LA = 3
            prev_ep = epg[0]
            ep_xi = None
            ep_stt = False
            ep_at = 0
            mods_done[0] = False
            nit = len(items)
            for idx in range(nit + LA):
                if idx < nit:
                    h, pos, i, first, last = items[idx]
                    pair, hh = h // 2, h % 2
                    own = pos < 16
                    col0 = max(0, i - 4 * G) * 128
                    masked = i >= 4 * G
                    sbk = SB_[idx % len(SB_)]
                    pti = idx % 4

                    def mms(e, pos=pos, col0=col0, masked=masked, sbk=sbk, own=own, pair=pair, h=h, qb=qb):
                        ins = e.matmul(banks[sbk][:, col0:512], KT[:, pair, pos * 128:(pos + 1) * 128],
                                       QTb[qb][:, h, col0:512], start=True, stop=(not masked))
                        if masked:
                            ins = e.matmul(banks[sbk][:, col0:col0 + 128], identb[:, :], masks[:, 0 if own else 1, :],
                                           start=False, stop=True)
                        return ins
                    S.op("pe", mms, reads=[r_KT, r_QTb[qb], r_constb], writes=[bres[sbk]])
                    S.op("act", lambda e, pos=pos, col0=col0, sbk=sbk, pti=pti, h=h, bG=bG: e.activation(
                        PTv[:, pti, col0:512], banks[sbk][:, col0:512], AF.Exp, bias=biasG[bG][:, pos, h:h + 1], scale=0.125),
                        reads=[bres[sbk], r_biasG[bG]], writes=[r_pt[pti]])
                if idx >= LA:
                    jdx = idx - LA
                    h, pos, i, first, last = items[jdx]
                    own = pos < 16
                    col0 = max(0, i - 4 * G) * 128
                    pti = jdx % 4
                    ob = OB_[h % 2]

                    def mmpv(e, pos=pos, col0=col0, pti=pti, h=h, ob=ob, first=first, i=i, own=own, G=G):
                        ins = None
                        for mi in range(col0 // 128, 4):
                            st = first and mi == 0
                            sp_ = (not own) and (i == 4 * G + mi)
                            ins = e.matmul(banks[ob][:, mi * 65:(mi + 1) * 65], PTv[:, pti, mi * 128:(mi + 1) * 128],
                                           V[:, pos, h, 0:65], start=st, stop=sp_, skip_group_check=True)
                        return ins
                    S.op("pe", mmpv, reads=[r_pt[pti], r_V], writes=[bres[ob]])
                    if last:
                        S.op("dve", lambda e, ob=ob: e.reciprocal(
                            rl[:, :], banks[ob][:, 0:260].rearrange("p (a b) -> p a b", b=65)[:, :, 64]),
                            reads=[bres[ob]], writes=[r_rl])
                        S.op("dve", lambda e, ob=ob, h=h: e.tensor_tensor(
                            On[:, :, h * 64:(h + 1) * 64], banks[ob][:, 0:260].rearrange("p (a b) -> p a b", b=65)[:, :, 0:64],
                            rl[:, :].unsqueeze(2).to_broadcast([128, 4, 64]), ALU.mult),
                            reads=[bres[ob], r_rl], writes=[r_On])
                if prev_ep is not None and idx % 2 == 1:
                    try:
                        next(prev_ep)
                    except StopIteration:
                        prev_ep = None
                if gen is not None and prev_ep is None and idx % stride == stride - 1:
                    next(gen, None)
                if ep_xi is None and idx >= nit - 40 and prev_ep is None and (gen is None or mods_done[0]):
                    ep_xi = [load_x(xp[(4 * G + mi) * 128:(4 * G + mi + 1) * 128, :]) for mi in range(4)]
                    ep_at = idx
                if ep_xi is not None and not ep_stt and idx >= max(nit - 20, ep_at + 10):
                    ep_stt = True
                    for xi in ep_xi:
                        S.op("dve", lambda e, xi=xi: e.scalar_tensor_tensor(xb[xi][:, :], xb[xi][:, :], ALPHA, gb[:, :], ALU.mult, ALU.add),
                             reads=[r_gb], writes=[r_xb[xi]])
            if gen is not None:
                for _ in gen:
                    pass
            if prev_ep is not None:
                for _ in prev_ep:
                    pass
            if ep_xi is None:
                ep_xi = [load_x(xp[(4 * G + mi) * 128:(4 * G + mi + 1) * 128, :]) for mi in range(4)]
            if not ep_stt:
                for xi in ep_xi:
                    S.op("dve", lambda e, xi=xi: e.scalar_tensor_tensor(xb[xi][:, :], xb[xi][:, :], ALPHA, gb[:, :], ALU.mult, ALU.add),
                         reads=[r_gb], writes=[r_xb[xi]])
            for cpair in range(2):
                bank = mbank()
                bview = banks[bank][:, :].bitcast(BF16)

                def tr(e, cpair=cpair, bview=bview):
                    ins = None
                    for cc in range(2):
                        c = cpair * 2 + cc
                        for mi in range(4):
                            ins = e.transpose(bview[:, cc * 512 + mi * 128: cc * 512 + (mi + 1) * 128],
                                              On[:, mi, c * 128:(c + 1) * 128], identb[:, :])
                    return ins
                S.op("pe", tr, reads=[r_On, r_constb], writes=[bres[bank]])
                for cc in range(2):
                    c = cpair * 2 + cc
                    S.op("dve", lambda e, c=c, cc=cc, bview=bview, qb=qb: e.tensor_tensor(
                        yT[:, c, :], bview[:, cc * 512:(cc + 1) * 512], gattb[qb][:, c, :], ALU.mult),
                        reads=[bres[bank], r_gattb[qb]], writes=[r_yT[c]])
            def epilogue(G=G, Y=Y, ep_xi=ep_xi):
                for mi in range(4):
                    m = 4 * G + mi
                    xi = ep_xi[mi]
                    sp2 = m % 2
                    bks = [mbank(), mbank()]
                    for half in range(2):
                        def mm(e, half=half, mi=mi, bank=bks[half], Y=Y):
                            ins = None
                            for kc in range(8):
                                lhs = yT[:, kc, mi * 128:(mi + 1) * 128] if kc < 4 else scr[:, Y + kc - 4, mi * 128:(mi + 1) * 128]
                                ins = e.matmul(banks[bank][:, :], lhs, wA[:, kc, half * 512:(half + 1) * 512],
                                               start=(kc == 0), stop=(kc == 7))
                            return ins
                        S.op("pe", mm, reads=r_yT + [r_scr[Y + g] for g in range(4)] + [r_wA], writes=[bres[bks[half]]])
                        tk = tm_rot[0] % 2
                        tm_rot[0] += 1
                        S.op("dve", lambda e, half=half, bank=bks[half], tk=tk: e.tensor_tensor(
                            tmpfs[tk][:, :], banks[bank][:, :], gate_bc[:, half * 512:(half + 1) * 512], ALU.mult),
                            reads=[bres[bks[half]], r_gate], writes=[r_tmpfs[tk]])
                        S.op("dve", lambda e, half=half, xi=xi, tk=tk: e.tensor_tensor(
                            xb[xi][:, half * 512:(half + 1) * 512], xb[xi][:, half * 512:(half + 1) * 512], tmpfs[tk][:, :], ALU.add),
                            reads=[r_tmpfs[tk]], writes=[r_xb[xi]])
                        S.op("dve", lambda e, half=half, xi=xi, sp2=sp2: e.bn_stats(stats2[sp2][:, half, :], xb[xi][:, half * 512:(half + 1) * 512]),
                             reads=[r_xb[xi]], writes=[r_sm[sp2]])
                        yield
                    S.op("dve", lambda e, sp2=sp2: e.bn_aggr(mv2[sp2][:, :], stats2[sp2][:, :, :].rearrange("p a b -> p (a b)")),
                         writes=[r_sm[sp2]])
                    S.op("pool", lambda e, sp2=sp2: e.tensor_scalar(ve2[sp2][:, :], mv2[sp2][:, 1:2], EPS, 0.0, ALU.add, ALU.add),
                         reads=[r_sm[sp2]], writes=[r_sm[sp2]])
                    S.op("pool", lambda e, sp2=sp2: e.tensor_tensor(rstd2[sp2][:, :], ve2[sp2][:, :], mhalf[:, :], ALU.pow),
                         reads=[r_small], writes=[r_sm[sp2]])
                    S.op("dve", lambda e, xi=xi, sp2=sp2: e.tensor_scalar(xb[xi][:, :], xb[xi][:, :], mv2[sp2][:, 0:1], rstd2[sp2][:, 0:1],
                                                                         ALU.subtract, ALU.mult),
                         reads=[r_sm[sp2]], writes=[r_xb[xi], r_sm[sp2]])
                    S.op("pool", lambda e, xi=xi: e.tensor_tensor(xb[xi][:, :], xb[xi][:, :], lng[:, :], ALU.mult),
                         reads=[r_const], writes=[r_xb[xi]])
                    S.op("pool", lambda e, xi=xi: e.tensor_tensor(xb[xi][:, :], xb[xi][:, :], lnb[:, :], ALU.add),
                         reads=[r_const], writes=[r_xb[xi]])
                    S.dma("pool", lambda e, xi=xi, m=m: e.dma_start(out=out_d[m * 128:(m + 1) * 128, :], in_=xb[xi][:, :]),
                          d_out[xi], reads=[r_xb[xi]])
                    yield

            epg[0] = epilogue()
            if gi == len(GORD) - 1:
                for _ in epg[0]:
                    pass
        S.wait_only("pool", [Tok(d.sem, d.n * 16, "dma", d.key) for d in d_out])
        if debug:
            S.wait_only("sp", [Tok(S.sem[en], S.cnt[en], en, en) for en in ("pe", "act", "dve", "pool")]
                        + [Tok(d.sem, d.n * 16, "dma", d.key) for d in d_out])
            dumps = {"adaT": (adaT, F32), "scale1": (scale1, F32), "gate_bc": (gate_bc, F32), "gb": (gb, F32),
                     "KT": (KT, BF16), "V": (V, BF16), "Cpos": (Cpos, F32), "nlf": (nlf, F32), "biasG1": (biasG[1], F32),
                     "QT": (QT, BF16), "gatt": (gatt, BF16), "yT": (yT, BF16), "scr": (scr, BF16), "gatt1": (gatt1, BF16), "phalo": (phalo, BF16),
                     "wA": (wA, BF16), "hb1": (hb[1], F32), "uT1": (uT[1], BF16), "uT0": (uT[0], BF16), "biasG0": (biasG[0], F32), "sc": (sc, F32), "tot": (tot, F32),
                     "Z": (Z, F32), "bvfp": (bvfp, F32), "masks": (masks, BF16), "bm": (bm, BF16)}
            for nm, (tl, dt) in dumps.items():
                shp = list(tl.shape)
                dd = nc.dram_tensor("dbg_" + nm, shp, dt, kind="ExternalOutput").ap()
                full = tuple(slice(None) for _ in shp)
                S.dma("sp", lambda e, dd=dd, tl=tl, full=full: e.dma_start(out=dd[full], in_=tl[full]), d_tmp)
            S.wait_only("sp", [Tok(d_tmp.sem, d_tmp.n * 16, "dma", d_tmp.key)])

        with nc.Block() as block:
            @block.tensor
            def _(e):
                S.replay("pe", e)

            @block.scalar
            def _(e):
                S.replay("act", e)

            @block.vector
            def _(e):
                S.replay("dve", e)

            @block.gpsimd
            def _(e):
                S.replay("pool", e)

            @block.sync
            def _(e):
                S.replay("sp", e)
    return nc


def _consts(par):
    ident = np.eye(128, dtype=np.float32)
    ones = np.ones((128, 128), np.float32)
    s = np.arange(128)[:, None]
    t = np.arange(128)[None, :]
    U = (s <= t).astype(np.float32)
    glob = np.array([2 * p + par if p < 16 else 2 * (p - 16) + 1 - par for p in range(32)])
    pred = (glob[:, None] < glob[None, :]).astype(np.float32)
    masks = np.zeros((128, 2, 128), np.float32)
    masks[:, 0, :] = np.where(s <= t, 0.0, NEG)
    masks[:, 1, :] = 0.0 if par == 1 else NEG
    bm = np.zeros((128, 8, 128), np.float32)
    bh = np.zeros((128, 36, 16), np.float32)
    eye = np.eye(128, dtype=np.float32)
    for g, w in enumerate(WINS):
        inwin = ((t - s) >= 0) & ((t - s) < w)
        bm[:, g, :] = np.where(inwin, 1.0 / w, 0.0) - eye
        if par == 0:
            cnt = np.minimum(t + 1, w).astype(np.float32)
            bm[:, 4 + g, :] = np.where(inwin, 1.0 / cnt, 0.0) - eye
        else:
            bm[:, 4 + g, :] = bm[:, g, :]
        for j in range(8):
            for i in range(16):
                for tt in range(16):
                    if tt + 16 - i < w:
                        bh[j * 16 + i, j * 4 + g, tt] = 1.0 / w
        if par == 1:
            bh[:, 32 + g, :] = bh[:, 0 * 4 + g, :]
    return ident, ones, U, pred, masks, bm, bh


def _colT(v, n):
    return np.ascontiguousarray(np.asarray(v, np.float32).reshape(n, 128).T)


_NC_CACHE = {}
_DEBUG = [False]
_NG = [4]


def kernel(x, c, w_ada, b_ada, w_in, b_in, w_pool_mix, b_pool_mix, pool_scale, w_out, b_out, ln_g, ln_b):
    x = np.asarray(x, np.float32)
    c = np.asarray(c, np.float32)
    w_ada = np.ascontiguousarray(np.asarray(w_ada, np.float32)[0])
    b_ada = np.asarray(b_ada, np.float32)[0]
    w_in = np.ascontiguousarray(np.asarray(w_in, np.float32)[0])
    b_in = np.asarray(b_in, np.float32)[0]
    w_pm = np.ascontiguousarray(np.asarray(w_pool_mix, np.float32)[0])
    b_pm = np.asarray(b_pool_mix, np.float32)[0]
    psc = np.asarray(pool_scale, np.float32)[0]
    w_out = np.ascontiguousarray(np.asarray(w_out, np.float32)[0])
    b_out = np.asarray(b_out, np.float32)[0]
    ln_g = np.asarray(ln_g, np.float32)[0]
    ln_b = np.asarray(ln_b, np.float32)[0]

    common = {
        "w_ada": w_ada,
        "b_adaT": _colT(b_ada[0:2048], 16),
        "b_gate": np.ascontiguousarray(b_ada[2048:3072].reshape(1, D)),
        "w_in": w_in,
        "b_qT": _colT(b_in[0:512], 4),
        "b_kT": _colT(b_in[512:1024], 4),
        "b_gT": _colT(b_in[2056:3080], 8),
        "b_vfp": np.ascontiguousarray(b_in[1024:2056].reshape(1, 1032)),
        "w_pm": w_pm,
        "b_pmT": np.ascontiguousarray(b_pm.T),
        "pscT": _colT(psc, 4),
        "w_out": w_out,
        "b_out": np.ascontiguousarray(b_out.reshape(1, D)),
        "ln_g": np.ascontiguousarray(ln_g.reshape(1, D)),
        "ln_b": np.ascontiguousarray(ln_b.reshape(1, D)),
    }
    in_maps = []
    for core in range(NCORES):
        b, par = core // 2, core % 2
        xb_ = x[b].reshape(NBLK, 128, D)
        own = [2 * m + par for m in range(16)]
        oth = [2 * m + 1 - par for m in range(16)]
        xp = np.ascontiguousarray(xb_[own + oth].reshape(SEQ, D))
        xh = np.zeros((256, D), np.float32)
        for m in range(16):
            g = own[m]
            if g > 0:
                xh[m * 16:(m + 1) * 16] = x[b, g * 128 - 16:g * 128]
        ident, ones, U, pred, masks, bm, bh = _consts(par)
        mp = dict(common)
        mp.update({"xp": xp, "xpT": np.ascontiguousarray(xp.T), "xhT": np.ascontiguousarray(xh.T), "cT": _colT(c[b], 8), "ident": ident, "ones": ones, "U": U,
                   "pred": pred, "masks": masks, "bm": bm, "bh": bh})
        in_maps.append(mp)

    if "nc" not in _NC_CACHE:
        _NC_CACHE["nc"] = build_nc(_DEBUG[0])
    nc = _NC_CACHE["nc"]
    res = run_bass_kernel_spmd(nc, in_maps, core_ids=list(range(NCORES)))
    if _DEBUG[0]:
        _DEBUG.append(res.results)
    out = np.empty((4, SEQ, D), np.float32)
    for core in range(NCORES):
        b, par = core // 2, core % 2
        o = np.asarray(res.results[core]["out"], np.float32).reshape(16, 128, D)
        for m in range(16):
            g = 2 * m + par
            out[b, g * 128:(g + 1) * 128] = o[m]
    return out
```

```python
import numpy as np
from contextlib import ExitStack
import concourse.bass as bass
import concourse.mybir as mybir
from concourse.bass_utils import run_bass_kernel_spmd

F32 = mybir.dt.float32
BF16 = mybir.dt.bfloat16
AF = mybir.ActivationFunctionType
ALU = mybir.AluOpType

NCORES = 8
SEQ = 4096
D = 1024
NBLK = 32
ALPHA = float(2.0 ** 0.25)
EPS = 1e-5
NEG = -30000.0
WINS = (2, 4, 8, 16)


class Tok:
    __slots__ = ("sem", "val", "eng", "key")

    def __init__(self, sem, val, eng, key):
        self.sem, self.val, self.eng, self.key = sem, val, eng, key


class Res:
    def __init__(self, name, track_reads=True):
        self.name = name
        self.w = None
        self.r = []
        self.track = track_reads


class DSem:
    def __init__(self, sem, key):
        self.sem, self.n, self.key = sem, 0, key


class Sched:
    ENGS = ("pe", "act", "dve", "pool", "sp")

    def __init__(self, nc, es):
        self.nc = nc
        self.es = es
        self.q = {e: [] for e in self.ENGS}
        self.sem = {e: es.enter_context(nc.semaphore("s_" + e)) for e in self.ENGS}
        self.cnt = {e: 0 for e in self.ENGS}
        self.seen = {e: {} for e in self.ENGS}
        self.nd = 0

    def dsem(self, name):
        self.nd += 1
        return DSem(self.es.enter_context(self.nc.semaphore("d_" + name)), "d%d" % self.nd)

    def _waits(self, eng, reads, writes, extra, is_dma=False):
        need = {}

        def add(t):
            if t is None:
                return
            if t.eng == eng and eng == "pe" and not is_dma:
                return
            if self.seen[eng].get(t.key, 0) >= t.val:
                return
            if need.get(t.key, (None, 0))[1] < t.val:
                need[t.key] = (t.sem, t.val)

        for r in reads:
            add(r.w)
        for w in writes:
            add(w.w)
            for t in w.r:
                add(t)
        for t in extra:
            add(t)
        for k, (s, v) in need.items():
            self.seen[eng][k] = v
        return list(need.values())

    def _commit(self, tok, reads, writes):
        for w in writes:
            w.w = tok
            w.r = []
        for r in reads:
            if r.track:
                r.r.append(tok)

    def op(self, eng, fn, reads=(), writes=(), extra=()):
        waits = self._waits(eng, reads, writes, extra)
        self.cnt[eng] += 1
        tok = Tok(self.sem[eng], self.cnt[eng], eng, eng)
        self.q[eng].append((waits, fn, (self.sem[eng], 1)))
        self._commit(tok, reads, writes)
        return tok

    def dma(self, eng, fn, ds, reads=(), writes=(), extra=()):
        waits = self._waits(eng, reads, writes, extra, is_dma=True)
        ds.n += 1
        tok = Tok(ds.sem, ds.n * 16, "dma", ds.key)
        self.q[eng].append((waits, fn, (ds.sem, 16)))
        self._commit(tok, reads, writes)
        return tok

    def wait_only(self, eng, toks):
        waits = self._waits(eng, (), (), toks)
        self.q[eng].append((waits, None, None))

    def replay(self, eng, e):
        for waits, fn, inc in self.q[eng]:
            for s, v in waits:
                e.wait_ge(s, v)
            if fn is not None:
                ins = fn(e)
                ins.then_inc(inc[0], inc[1])


def build_nc(debug=False):
    nc = bass.Bass("TRN2", target_bir_lowering=False)

    def din(name, shape):
        return nc.dram_tensor(name, list(shape), F32, kind="ExternalInput").ap()

    xp = din("xp", [SEQ, D])
    xpT = din("xpT", [D, SEQ])
    xhT = din("xhT", [D, 256])
    cT_d = din("cT", [128, 8])
    w_ada = din("w_ada", [D, 3 * D])
    b_adaT_d = din("b_adaT", [128, 16])
    b_gate_d = din("b_gate", [1, D])
    w_in = din("w_in", [D, 3080])
    b_qT_d = din("b_qT", [128, 4])
    b_kT_d = din("b_kT", [128, 4])
    b_gT_d = din("b_gT", [128, 8])
    b_vfp_d = din("b_vfp", [1, 1032])
    w_pm_d = din("w_pm", [4, 128, 128])
    b_pmT_d = din("b_pmT", [128, 4])
    pscT_d = din("pscT", [128, 4])
    w_out_d = din("w_out", [D, D])
    b_out_d = din("b_out", [1, D])
    ln_g_d = din("ln_g", [1, D])
    ln_b_d = din("ln_b", [1, D])
    ident_d = din("ident", [128, 128])
    ones_d = din("ones", [128, 128])
    U_d = din("U", [128, 128])
    pred_d = din("pred", [32, 32])
    masks_d = din("masks", [128, 2, 128])
    bm_d = din("bm", [128, 8, 128])
    bh_d = din("bh", [128, 36, 16])
    out_d = nc.dram_tensor("out", [2048, D], F32, kind="ExternalOutput").ap()

    wbf = nc.dram_tensor("wbf", [4, 128, 8, 512], BF16, kind="Internal").ap()
    xpT_v = xpT.rearrange("(kc p) t -> p kc t", p=128)
    xhT_v = xhT.rearrange("(kc p) t -> p kc t", p=128)
    w_ada_v = w_ada.rearrange("(kc p) e -> p kc e", p=128)
    w_in_v = w_in.rearrange("(kc p) e -> p kc e", p=128)
    w_out_v = w_out_d.rearrange("(kc p) e -> p kc e", p=128)

    with ExitStack() as es:
        S = Sched(nc, es)

        def sb(name, shape, dt=F32):
            return es.enter_context(nc.sbuf_tensor("sb_" + name, list(shape), dt))

        banks = [es.enter_context(nc.psum_tensor("ps%d" % i, [128, 512], F32)) for i in range(8)]
        bres = [Res("bank%d" % i) for i in range(8)]

        nlf = sb("nlf", [128, NBLK, 8])
        Cpos = sb("Cpos", [128, NBLK, 8])
        biasG = [sb("biasG%d" % i, [128, NBLK, 8]) for i in range(2)]
        ident = sb("ident", [128, 128])
        onesf = sb("onesf", [128, 128])
        Uf = sb("Uf", [128, 128])
        predf = sb("predf", [32, 32])
        tot = sb("tot", [32, 8])
        Z = sb("Z", [32, 256])
        identb = sb("identb", [128, 128], BF16)
        masks = sb("masks", [128, 2, 128], BF16)
        bm = sb("bm", [128, 8, 128], BF16)
        bh = sb("bh", [128, 36, 16], BF16)
        gate_bc = sb("gate_bc", [128, D])
        gb = sb("gb", [128, D])
        lng = sb("lng", [128, D])
        lnb = sb("lnb", [128, D])
        bvfp = sb("bvfp", [128, 1032])
        adaT = sb("adaT", [128, 16])
        scale1 = sb("scale1", [128, 8])
        b_adaT = sb("b_adaT", [128, 16])
        b_qT = sb("b_qT", [128, 4])
        b_kT = sb("b_kT", [128, 4])
        b_gT = sb("b_gT", [128, 8])
        b_pmT = sb("b_pmT", [128, 4])
        pscT = sb("pscT", [128, 4])
        cT = sb("cT", [128, 8])
        sc = sb("sc", [128, 8])
        phalo = sb("phalo", [128, 2, 512], BF16)
        wpm = sb("wpm", [128, 4, 128], BF16)
        NXB = 4
        xb = [sb("xb%d" % i, [128, D]) for i in range(NXB)]
        uT = [sb("uT%d" % i, [128, 8, 512], BF16) for i in range(2)]
        wA = sb("wA", [128, 8, 1032], BF16)
        wS = [sb("wS%d" % i, [128, 8, 512], BF16) for i in range(2)]
        QT = sb("QT", [128, 8, 512], BF16)
        gatt = sb("gatt", [128, 4, 512], BF16)
        scr = sb("scr", [128, 12, 512], BF16)
        yT = sb("yT", [128, 4, 512], BF16)
        gatt1 = sb("gatt1", [128, 4, 512], BF16)
        stats2 = [sb("stats2_%d" % i, [128, 2, 6]) for i in range(2)]
        mv2 = [sb("mv2_%d" % i, [128, 2]) for i in range(2)]
        ve2 = [sb("ve2_%d" % i, [128, 1]) for i in range(2)]
        rstd2 = [sb("rstd2_%d" % i, [128, 1]) for i in range(2)]
        mhalf = sb("mhalf", [128, 1])
        r_sm = [Res("sm0"), Res("sm1")]
        nbg = sb("nbg", [128, 8])
        tmpf = sb("tmpf", [128, 512])
        tmpf2 = sb("tmpf2", [128, 512])
        hb = [sb("hb%d" % i, [128, D]) for i in range(2)]
        rl = sb("rl", [128, 4])
        fl = sb("fl", [128, 4, 8])
        ex = sb("ex", [128, 4, 8])

        r_KT = Res("KT", False)
        r_V = Res("V", False)
        r_nlf = Res("nlf")
        r_Cpos = Res("Cpos", False)
        r_biasG = [Res("biasG0"), Res("biasG1")]
        r_const = Res("const", False)
        r_constb = Res("constb", False)
        r_ada = Res("ada", False)
        r_gate = Res("gate", False)
        r_gb = Res("gb", False)
        r_sc = Res("sc", False)
        r_xb = [Res("xb%d" % i) for i in range(NXB)]
        r_uT = [Res("uT0"), Res("uT1")]
        r_wA = Res("wA")
        r_wS = [Res("wS0"), Res("wS1")]
        r_QT = Res("QT")
        r_gatt = Res("gatt")
        r_scr = [Res("scr%d" % i) for i in range(12)]
        r_yT = [Res("yT%d" % i) for i in range(4)]
        r_small2 = Res("small2", False)
        r_tmpf = Res("tmpf")
        r_tmpf2 = Res("tmpf2")
        tmpfs = [tmpf, tmpf2]
        r_tmpfs = [r_tmpf, r_tmpf2]
        r_hb = [Res("hb0"), Res("hb1")]
        r_small = Res("small")
        r_rl = Res("rl")
        r_phalo = Res("phalo", False)
        r_fl = Res("fl")
        r_tot = Res("tot")
        r_Z = Res("Z")

        d_const = S.dsem("const")
        d_constb = S.dsem("constb")
        d_xb = [S.dsem("xb%d" % i) for i in range(NXB)]
        d_wa = [S.dsem("wa%d" % i) for i in range(3)]
        d_wA = S.dsem("wA")
        d_wS = [S.dsem("wS0"), S.dsem("wS1")]
        d_out = [S.dsem("out%d" % i) for i in range(4)]
        d_tmp = S.dsem("tmp")

        def cload(dst, src, eng="act"):
            r_const.w = S.dma(eng, lambda e, dst=dst, src=src: e.dma_start(out=dst, in_=src), d_const)

        r_c0 = Res("c0", False)
        d_c0 = S.dsem("c0")
        S.dma("act", lambda e: e.dma_start(out=cT[:, :], in_=cT_d[:, :]), d_c0)
        r_c0.w = S.dma("act", lambda e: e.dma_start(out=b_adaT[:, :], in_=b_adaT_d[:, :]), d_c0)
        S.op("act", lambda e: e.activation(sc[:, :], cT[:, :], AF.Silu), reads=[r_c0], writes=[r_sc])
        cload(ident[:, :], ident_d[:, :])
        cload(onesf[:, :], ones_d[:, :])
        cload(Uf[:, :], U_d[:, :])
        cload(predf[:, :], pred_d[:, :])
        cload(b_qT[:, :], b_qT_d[:, :])
        cload(b_kT[:, :], b_kT_d[:, :])
        cload(b_gT[:, :], b_gT_d[:, :])
        cload(b_pmT[:, :], b_pmT_d[:, :])
        cload(pscT[:, :], pscT_d[:, :])
        cload(bvfp[:, :], b_vfp_d.partition_broadcast(128))
        cload(lng[:, :], ln_g_d.partition_broadcast(128))
        cload(lnb[:, :], ln_b_d.partition_broadcast(128))
        cload(gb[:, :], b_out_d.partition_broadcast(128))
        cload(gate_bc[:, :], b_gate_d.partition_broadcast(128))

        def cloadb(dst, src):
            r_constb.w = S.dma("pool", lambda e, dst=dst, src=src: e.dma_start(out=dst, in_=src), d_constb)

        cloadb(identb[:, :], ident_d[:, :])
        cloadb(masks[:, :, :], masks_d[:, :, :])
        cloadb(bm[:, :, :], bm_d[:, :, :])
        cloadb(bh[:, :, :], bh_d[:, :, :])
        cloadb(wpm[:, :, :], w_pm_d.rearrange("g c e -> c g e"))
        for half in range(2):
            r_wA.w = S.dma("pool", lambda e, half=half: e.dma_start(out=wA[:, half * 4:(half + 1) * 4, :],
                                                                   in_=w_in_v[:, half * 4:(half + 1) * 4, 512:1544]),
                           d_wA)

        S.op("dve", lambda e: e.memset(mhalf[:, :], -0.5), writes=[r_small])
        r_wbf = [Res("wbf%d" % i, False) for i in range(4)]
        d_wbf = [S.dsem("wbf%d" % i) for i in range(4)]
        for hh in range(2):
            S.op("pool", lambda e, hh=hh: e.memset(QT[(1 - hh) * 64:(2 - hh) * 64, hh:8:2, :], 0.0), writes=[r_QT])

        with ExitStack() as es2:
            wa = [es2.enter_context(nc.sbuf_tensor("wa%d" % i, [128, 8, 512], F32)) for i in range(3)]
            scbc = es2.enter_context(nc.sbuf_tensor("scbc", [128, 8, 128], F32))
            r_wa = [Res("wa%d" % i) for i in range(3)]
            r_scbc = Res("scbc")

            S.op("dve", lambda e: e.tensor_copy(scbc[:, :, :], sc[:, :].unsqueeze(2).to_broadcast([128, 8, 128])),
                 reads=[r_sc], writes=[r_scbc])
            psA = banks[7]
            for j in range(6):
                buf = wa[j % 3]
                S.dma("sp", lambda e, buf=buf, j=j: e.dma_start(out=buf[:, :, :], in_=w_ada_v[:, :, j * 512:(j + 1) * 512]),
                      d_wa[j % 3], writes=[r_wa[j % 3]])
                if j < 4:
                    def mm(e, buf=buf, j=j):
                        ins = None
                        for i in range(4):
                            fc = j * 4 + i
                            for kc in range(8):
                                ins = e.matmul(psA[:, fc:fc + 1], buf[:, kc, i * 128:(i + 1) * 128], sc[:, kc:kc + 1],
                                               start=(kc == 0), stop=(kc == 7))
                        return ins
                    S.op("pe", mm, reads=[r_wa[j % 3], r_sc], writes=[bres[7]])
                else:
                    bk = 5 + (j - 4)

                    def mm(e, buf=buf, bk=bk):
                        ins = None
                        for kc in range(8):
                            ins = e.matmul(banks[bk][:, :], scbc[:, kc, :], buf[:, kc, :], start=(kc == 0), stop=(kc == 7))
                        return ins
                    S.op("pe", mm, reads=[r_wa[j % 3], r_scbc], writes=[bres[bk]])
                    half = j - 4
                    S.op("dve", lambda e, bk=bk, half=half: e.tensor_tensor(
                        gate_bc[:, half * 512:(half + 1) * 512], banks[bk][:, :],
                        gate_bc[:, half * 512:(half + 1) * 512], ALU.add),
                        reads=[bres[bk], r_const], writes=[r_gate])
                if j == 3:
                    S.op("dve", lambda e: e.tensor_tensor(adaT[:, :], psA[:, 0:16], b_adaT[:, :], ALU.add),
                         reads=[bres[7], r_c0], writes=[r_ada])
                    S.op("dve", lambda e: e.tensor_scalar_add(scale1[:, :], adaT[:, 8:16], 1.0), writes=[r_ada])
            S.op("pool", lambda e: e.tensor_tensor(gb[:, :], gb[:, :], gate_bc[:, :], ALU.mult),
                 reads=[r_gate, r_const], writes=[r_gb])
            t_end_ada = S.op("pe", lambda e: e.matmul(banks[7][:, 0:1], onesf[:, 0:128], onesf[:, 0:1], start=True, stop=True),
                             reads=[r_const], writes=[bres[7]])

        shiftT = adaT
        KT = sb("KT", [128, 4, SEQ], BF16)
        V = sb("V", [128, NBLK, 8, 66], BF16)
        S.op("pool", lambda e: e.memset(V[:, :, :, 64], 1.0), writes=[r_V], extra=[t_end_ada])

        xb_rot = [0]

        def load_x(src_rows, q="sp"):
            i = xb_rot[0] % NXB
            xb_rot[0] += 1
            S.dma(q, lambda e, i=i, src_rows=src_rows: e.dma_start(out=xb[i][:, :], in_=src_rows), d_xb[i],
                  writes=[r_xb[i]])
            return i

        def load_feat(src_v, tok0, ntok, q="sp"):
            per = 1024 // ntok
            bufs = []
            for kc0 in range(0, 8, per):
                i = xb_rot[0] % NXB
                xb_rot[0] += 1
                S.dma(q, lambda e, i=i, kc0=kc0: e.dma_start(
                    out=xb[i][:, :].rearrange("p (a b) -> p a b", b=ntok), in_=src_v[:, kc0:kc0 + per, tok0:tok0 + ntok]),
                    d_xb[i], writes=[r_xb[i]])
                bufs.append(i)
            return bufs

        def modulate_sb(kc, bufs, ntok, ut, eng):
            per = 1024 // ntok
            xi = bufs[kc // per]
            off = (kc % per) * ntok
            if eng == "act":
                S.op("act", lambda e: e.activation(uT[ut][:, kc, 0:ntok], xb[xi][:, off:off + ntok], AF.Identity,
                                                   bias=shiftT[:, kc:kc + 1], scale=scale1[:, kc:kc + 1]),
                     reads=[r_xb[xi], r_ada], writes=[r_uT[ut]])
            else:
                S.op(eng, lambda e: e.tensor_scalar(uT[ut][:, kc, 0:ntok], xb[xi][:, off:off + ntok],
                                                    scale1[:, kc:kc + 1], shiftT[:, kc:kc + 1], ALU.mult, ALU.add),
                     reads=[r_xb[xi], r_ada], writes=[r_uT[ut]])

        ENG_MIX = ["dve", "act", "pool", "act", "dve", "act", "pool", "act"]

        rotT = [0]
        rotKV = [0]
        KVB = [0, 1, 2, 3, 4, 5, 7]
        def emit_T(ch):
            bufs = load_feat(xpT_v, ch * 512, 512, "act" if ch == 0 else "sp")
            for kc in range(8):
                modulate_sb(kc, bufs, 512, ch % 2, "act" if ch == 0 else ENG_MIX[kc])

        def emit_K(ch):
            ut = ch % 2
            for pair in range(4):
                bank = KVB[rotKV[0] % len(KVB)]
                rotKV[0] += 1

                def mm(e, pair=pair, bank=bank, ut=ut):
                    ins = None
                    for kc in range(8):
                        ins = e.matmul(banks[bank][:, :], wA[:, kc, pair * 128:(pair + 1) * 128], uT[ut][:, kc, :],
                                       start=(kc == 0), stop=(kc == 7))
                    return ins
                S.op("pe", mm, reads=[r_wA, r_uT[ut]], writes=[bres[bank]])
                S.op("act", lambda e, pair=pair, bank=bank, ch=ch: e.activation(
                    KT[:, pair, ch * 512:(ch + 1) * 512], banks[bank][:, :], AF.Identity, bias=b_kT[:, pair:pair + 1], scale=1.0),
                    reads=[bres[bank], r_const], writes=[r_KT], extra=[t_end_ada])

        def emit_V(ch):
            ut = ch % 2
            for bi in range(4):
                pos = ch * 4 + bi
                bank = KVB[rotKV[0] % len(KVB)]
                rotKV[0] += 1

                def mm(e, bi=bi, bank=bank, ut=ut):
                    ins = None
                    for kc in range(8):
                        ins = e.matmul(banks[bank][:, :], uT[ut][:, kc, bi * 128:(bi + 1) * 128], wA[:, kc, 512:1024],
                                       start=(kc == 0), stop=(kc == 7))
                    return ins
                S.op("pe", mm, reads=[r_wA, r_uT[ut]], writes=[bres[bank]])
                S.op("dve", lambda e, pos=pos, bank=bank: e.tensor_tensor(
                    V[:, pos, :, 0:64], banks[bank][:, :].rearrange("p (h d) -> p h d", d=64),
                    bvfp[:, 0:512].rearrange("p (h d) -> p h d", d=64), ALU.add),
                    reads=[bres[bank], r_const], writes=[r_V], extra=[t_end_ada])

            def mmf(e, ut=ut):
                ins = None
                for bi in range(4):
                    for kc in range(8):
                        ins = e.matmul(banks[6][:, bi * 8:(bi + 1) * 8], uT[ut][:, kc, bi * 128:(bi + 1) * 128],
                                       wA[:, kc, 1024:1032], start=(kc == 0), stop=(kc == 7))
                return ins
            S.op("pe", mmf, reads=[r_wA, r_uT[ut]], writes=[bres[6]])
            S.op("dve", lambda e: e.tensor_tensor(
                fl[:, :, :], banks[6][:, 0:32].rearrange("p (b h) -> p b h", h=8),
                bvfp[:, 512:520].unsqueeze(1).to_broadcast([128, 4, 8]), ALU.add),
                reads=[bres[6], r_const], writes=[r_fl])
            S.op("act", lambda e: e.activation(ex[:, :, :], fl[:, :, :], AF.Exp, scale=-1.0), reads=[r_fl], writes=[r_small])
            S.op("act", lambda e, ch=ch: e.activation(nlf[:, ch * 4:(ch + 1) * 4, :], ex[:, :, :], AF.Ln, bias=1.0, scale=1.0),
                 reads=[r_small], writes=[r_nlf, r_fl])

        emit_T(0)
        for ch in range(8):
            emit_K(ch)
            if ch == 2:
                for c0, pc in ((0, 0), (2056, 1), (1544, 3), (2568, 2)):
                    S.dma("pool", lambda e, c0=c0, pc=pc: e.dma_start(out=wbf[pc, :, :, :], in_=w_in_v[:, :, c0:c0 + 512]),
                          d_wbf[pc], writes=[r_wbf[pc]], extra=[r_KT.w])
            if ch + 1 < 8:
                emit_T(ch + 1)
            emit_V(ch)

        def mmtot(e):
            ins = None
            for h in range(8):
                ins = e.matmul(banks[0][0:32, 256 + h:257 + h], nlf[:, :, h], onesf[:, 0:1], start=True, stop=True)
            return ins
        def cum1():
            S.op("pe", mmtot, reads=[r_nlf, r_const], writes=[bres[0]])
            S.op("dve", lambda e: e.tensor_copy(tot[:, :], banks[0][0:32, 256:264]), reads=[bres[0]], writes=[r_tot])
            S.op("dve", lambda e: e.tensor_tensor(
                Z[:, :].rearrange("p (b h) -> p b h", h=8), tot[:, :].unsqueeze(1).to_broadcast([32, 32, 8]),
                predf[:, :].unsqueeze(2).to_broadcast([32, 32, 8]), ALU.mult), reads=[r_tot, r_const], writes=[r_Z])

        def mmcum(e):
            e.matmul(banks[0][:, 0:256], Uf[:, :], nlf[:, :, :].rearrange("p b h -> p (b h)"), start=True, stop=False)
            return e.matmul(banks[0][:, 0:256], onesf[0:32, :], Z[:, :], start=False, stop=True)
        def cum2():
            S.op("pe", mmcum, reads=[r_nlf, r_Z, r_const], writes=[bres[0]])
            S.op("dve", lambda e: e.tensor_copy(Cpos[:, :, :].rearrange("p b h -> p (b h)"), banks[0][:, 0:256]),
                 reads=[bres[0]], writes=[r_Cpos])

        SB_ = [0, 1, 2, 3]
        OB_ = [4, 5]
        MB_ = [6, 7]
        rotM = [0]
        rotW = [0]

        def mbank():
            b = MB_[rotM[0] % len(MB_)]
            rotM[0] += 1
            return b

        PIECE = {0: 0, 2056: 1, 2568: 2, 1544: 3}

        def load_ws(c0, first=False):
            i = rotW[0] % 2
            rotW[0] += 1
            pc = PIECE[c0]
            S.dma("pool", lambda e, i=i, pc=pc: e.dma_start(out=wS[i][:, :, :], in_=wbf[pc, :, :, :]),
                  d_wS[i], reads=[r_wbf[pc]], writes=[r_wS[i]])
            return i

        PCOL = 1544
        QCOL = 0
        GACOL = 2056
        GPCOL = 2568

        On = hb[0][:, :].bitcast(BF16).rearrange("p (a b) -> p a b", b=512)
        PTv = hb[1][:, :].bitcast(BF16).rearrange("p (a b) -> p a b", b=512)
        r_On = Res("On")
        r_pt = [Res("pt%d" % i) for i in range(4)]
        QTb = [QT, uT[1]]
        r_QTb = [r_QT, r_uT[1]]
        gattb = [gatt, gatt1]
        r_gattb = [r_gatt, Res("gatt1")]
        nb_gT = nbg
        S.op("dve", lambda e: e.tensor_scalar(nb_gT[:, :], b_gT[:, :], 0.5, None, ALU.mult), reads=[r_const], writes=[r_small2])

        tm_rot = [0]

        def silu_evac(bank, c8, out_ap, out_res):
            ti = tm_rot[0] % 2
            tm_rot[0] += 1
            tm, r_tm = tmpfs[ti], r_tmpfs[ti]
            S.op("act", lambda e: e.activation(tm[:, :], banks[bank][:, :], AF.Tanh, bias=nb_gT[:, c8:c8 + 1], scale=0.5),
                 reads=[bres[bank], r_small2], writes=[r_tm])
            S.op("pool", lambda e: e.tensor_scalar(tm[:, :], tm[:, :], 0.5, 0.5, ALU.mult, ALU.add), reads=[r_tm], writes=[r_tm])
            S.op("dve", lambda e: e.scalar_tensor_tensor(out_ap, banks[bank][:, :], b_gT[:, c8:c8 + 1], tm[:, :], ALU.add, ALU.mult),
                 reads=[bres[bank], r_tm, r_const], writes=out_res)

        def group2(lhs_fn, rhs_fn, rd, bank):
            for part in range(2):
                def mm(e, part=part):
                    ins = None
                    for kc in range(part * 4, part * 4 + 4):
                        ins = e.matmul(banks[bank][:, :], lhs_fn(kc), rhs_fn(kc), start=(kc == 0), stop=(kc == 7))
                    return ins
                S.op("pe", mm, reads=rd, writes=[bres[bank]])
                yield

        mods_done = [False]
        GORD = [3, 2, 1, 0]

        def emit_biasG(G):
            bG = G % 2
            bank = mbank()
            S.op("pe", lambda e, bank=bank, G=G: e.matmul(banks[bank][:, 0:8], onesf[:, :], Cpos[:, 4 * G + 2, :], start=True, stop=True),
                 reads=[r_Cpos, r_const], writes=[bres[bank]])
            S.op("dve", lambda e, bank=bank, bG=bG: e.scalar_tensor_tensor(
                biasG[bG][:, :, :], banks[bank][:, 0:8].unsqueeze(1).to_broadcast([128, NBLK, 8]), -1.0 / 128.0,
                Cpos[:, :, :], ALU.mult, ALU.add), reads=[bres[bank], r_Cpos], writes=[r_biasG[bG]])

        def prelude(G, overlapped, n_sp=14):
            qb = (G + 1) % 2
            X, Y = 4 * ((G + 2) % 3), 4 * (G % 3)
            wq = load_ws(QCOL, G == 0)
            wga = load_ws(GACOL, G == 0)
            bufs = load_feat(xpT_v, G * 512, 512)
            for _ in range(n_sp):
                yield
            for kc in range(8):
                modulate_sb(kc, bufs, 512, 0, "pool" if overlapped else ENG_MIX[kc])
                if kc % 2 == 1:
                    yield
            mods_done[0] = True
            if G == GORD[0]:
                hbufs = load_feat(xhT_v, 0, 256)
                for kc in range(8):
                    modulate_sb(kc, hbufs, 256, 1, ENG_MIX[kc])
            if G == GORD[1]:
                for hh in range(2):
                    S.op("pool", lambda e, hh=hh: e.memset(uT[1][(1 - hh) * 64:(2 - hh) * 64, hh:8:2, :], 0.0), writes=[r_uT[1]])
            for c in range(4):
                bank = mbank()
                for _ in group2(lambda kc, c=c: wS[wq][:, kc, c * 128:(c + 1) * 128], lambda kc: uT[0][:, kc, :],
                                  [r_wS[wq], r_uT[0]], bank):
                    pass
                for hh in range(2):
                    S.op("dve", lambda e, hh=hh, c=c, bank=bank: e.tensor_scalar(
                        QTb[qb][hh * 64:(hh + 1) * 64, 2 * c + hh, :], banks[bank][hh * 64:(hh + 1) * 64, :],
                        b_qT[hh * 64:(hh + 1) * 64, c:c + 1], None, ALU.add),
                        reads=[bres[bank], r_const], writes=[r_QTb[qb]])
                yield
            wp = load_ws(PCOL, G == 0)
            for c in range(4):
                bank = mbank()
                for _ in group2(lambda kc, c=c: wS[wga][:, kc, c * 128:(c + 1) * 128], lambda kc: uT[0][:, kc, :],
                                  [r_wS[wga], r_uT[0]], bank):
                    pass
                silu_evac(bank, c, gattb[qb][:, c, :], [r_gattb[qb]])
                yield
            wgp = load_ws(GPCOL, G == 0)
            for mi in range(4):
                bank = mbank()
                for _ in group2(lambda kc, mi=mi: uT[0][:, kc, mi * 128:(mi + 1) * 128], lambda kc: wS[wp][:, kc, :],
                                  [r_wS[wp], r_uT[0]], bank):
                    pass
                S.op("dve", lambda e, mi=mi, bank=bank: e.tensor_tensor(scr[:, X + mi, :], banks[bank][:, :], bvfp[:, 520:1032], ALU.add),
                     reads=[bres[bank], r_const], writes=[r_scr[X + mi]])
                yield
            if G == GORD[0]:
                for hbk in range(2):
                    bank = mbank()
                    for _ in group2(lambda kc, hbk=hbk: uT[1][:, kc, hbk * 128:(hbk + 1) * 128], lambda kc: wS[wp][:, kc, :],
                                    [r_wS[wp], r_uT[1]], bank):
                        pass
                    S.op("dve", lambda e, hbk=hbk, bank=bank: e.tensor_tensor(phalo[:, hbk, :], banks[bank][:, :], bvfp[:, 520:1032], ALU.add),
                         reads=[bres[bank], r_const], writes=[r_phalo])
            for mi in range(4):
                m = 4 * G + mi
                bank = mbank()

                def mm(e, mi=mi, m=m, bank=bank):
                    ins = None
                    for g in range(4):
                        bmi = g if m > 0 else 4 + g
                        bhi = ((m % 8) * 4 + g) if m > 0 else 32 + g
                        e.matmul(banks[bank][:, g * 128:(g + 1) * 128], scr[:, X + mi, g * 128:(g + 1) * 128], bm[:, bmi, :],
                                 start=True, stop=False)
                        ins = e.matmul(banks[bank][:, g * 128:g * 128 + 16], phalo[:, m // 8, g * 128:(g + 1) * 128],
                                       bh[:, bhi, :], start=False, stop=True)
                    return ins
                S.op("pe", mm, reads=[r_scr[X + mi], r_phalo, r_constb], writes=[bres[bank]])
                S.op("dve", lambda e, mi=mi, bank=bank: e.tensor_copy(
                    scr[:, Y:Y + 4, mi * 128:(mi + 1) * 128], banks[bank][:, :].rearrange("p (g t) -> p g t", t=128)),
                    reads=[bres[bank]], writes=[r_scr[Y + g] for g in range(4)])
                yield
            for c in range(4):
                bank = mbank()
                for _ in group2(lambda kc, c=c: wS[wgp][:, kc, c * 128:(c + 1) * 128], lambda kc: uT[0][:, kc, :],
                                  [r_wS[wgp], r_uT[0]], bank):
                    pass
                silu_evac(bank, 4 + c, scr[:, X + c, :], [r_scr[X + c]])
                yield
            for g in range(4):
                bank = mbank()
                S.op("pe", lambda e, g=g, bank=bank: e.matmul(banks[bank][:, :], wpm[:, g, :], scr[:, Y + g, :], start=True, stop=True),
                     reads=[r_scr[Y + g], r_constb], writes=[bres[bank]])
                S.op("dve", lambda e, g=g, bank=bank: e.tensor_scalar(
                    tmpf[:, :], banks[bank][:, :], b_pmT[:, g:g + 1], pscT[:, g:g + 1], ALU.add, ALU.mult),
                    reads=[bres[bank], r_const], writes=[r_tmpf])
                S.op("pool", lambda e, g=g: e.tensor_tensor(scr[:, Y + g, :], tmpf[:, :], scr[:, X + g, :], ALU.mult),
                     reads=[r_tmpf, r_scr[X + g]], writes=[r_scr[Y + g]])
                yield
            if overlapped:
                emit_biasG(G)

        pg = prelude(GORD[0], False, 0)
        for _ in range(4):
            next(pg, None)
        cum1()
        for _ in range(4):
            next(pg, None)
        cum2()


        epg = [None]
        for gi, G in enumerate(GORD):
            Gn = GORD[gi + 1] if gi + 1 < len(GORD) else None
            qb = (G + 1) % 2
            Y = 4 * (G % 3)
            nit_g = 8 * (8 * G + 8)
            sp_items = 40 if gi == 0 else 45
            stride = max(1, nit_g // (50 + sp_items))
            gen = prelude(Gn, True, -(-sp_items // stride)) if Gn is not None else None
            if gi == 0:
                for half in range(2):
                    S.dma("pool", lambda e, half=half: e.dma_start(out=wA[:, half * 4:(half + 1) * 4, 0:1024],
                                                                  in_=w_out_v[:, half * 4:(half + 1) * 4, :]),
                          d_wA, writes=[r_wA])
            bG = G % 2
            if gi == 0:
                emit_biasG(G)

            nk = 4 * G + 4
            kblocks = [(i, i) for i in range(nk)] + [(16 + i, i) for i in range(nk)]
            items = []
            for h in range(8):
                for j, (pos, i) in enumerate(kblocks):
                    items.append((h, pos, i, j == 0, j == len(kblocks) - 1))
            LA = 3
            prev_ep = epg[0]
            first_rest = pg if gi == 0 else None
            ep_xi = None
            ep_stt = False
            ep_at = 0
            mods_done[0] = False
            nit = len(items)
            for idx in range(nit + LA):
                if idx < nit:
                    h, pos, i, first, last = items[idx]
                    pair, hh = h // 2, h % 2
                    own = pos < 16
                    col0 = max(0, i - 4 * G) * 128
                    masked = i >= 4 * G
                    sbk = SB_[idx % len(SB_)]
                    pti = idx % 4

                    def mms(e, pos=pos, col0=col0, masked=masked, sbk=sbk, own=own, pair=pair, h=h, qb=qb):
                        ins = e.matmul(banks[sbk][:, col0:512], KT[:, pair, pos * 128:(pos + 1) * 128],
                                       QTb[qb][:, h, col0:512], start=True, stop=(not masked))
                        if masked:
                            ins = e.matmul(banks[sbk][:, col0:col0 + 128], identb[:, :], masks[:, 0 if own else 1, :],
                                           start=False, stop=True)
                        return ins
                    S.op("pe", mms, reads=[r_KT, r_QTb[qb], r_constb], writes=[bres[sbk]])
                    S.op("act", lambda e, pos=pos, col0=col0, sbk=sbk, pti=pti, h=h, bG=bG: e.activation(
                        PTv[:, pti, col0:512], banks[sbk][:, col0:512], AF.Exp, bias=biasG[bG][:, pos, h:h + 1], scale=0.125),
                        reads=[bres[sbk], r_biasG[bG]], writes=[r_pt[pti]])
                if idx >= LA:
                    jdx = idx - LA
                    h, pos, i, first, last = items[jdx]
                    own = pos < 16
                    col0 = max(0, i - 4 * G) * 128
                    pti = jdx % 4
                    ob = OB_[h % 2]

                    def mmpv(e, pos=pos, col0=col0, pti=pti, h=h, ob=ob, first=first, i=i, own=own, G=G):
                        ins = None
                        for mi in range(col0 // 128, 4):
                            st = first and mi == 0
                            sp_ = (not own) and (i == 4 * G + mi)
                            ins = e.matmul(banks[ob][:, mi * 65:(mi + 1) * 65], PTv[:, pti, mi * 128:(mi + 1) * 128],
                                           V[:, pos, h, 0:65], start=st, stop=sp_, skip_group_check=True)
                        return ins
                    S.op("pe", mmpv, reads=[r_pt[pti], r_V], writes=[bres[ob]])
                    if last:
                        S.op("dve", lambda e, ob=ob: e.reciprocal(
                            rl[:, :], banks[ob][:, 0:260].rearrange("p (a b) -> p a b", b=65)[:, :, 64]),
                            reads=[bres[ob]], writes=[r_rl])
                        S.op("dve", lambda e, ob=ob, h=h: e.tensor_tensor(
                            On[:, :, h * 64:(h + 1) * 64], banks[ob][:, 0:260].rearrange("p (a b) -> p a b", b=65)[:, :, 0:64],
                            rl[:, :].unsqueeze(2).to_broadcast([128, 4, 64]), ALU.mult),
                            reads=[bres[ob], r_rl], writes=[r_On])
                if prev_ep is not None and idx % 2 == 1:
                    try:
                        next(prev_ep)
                    except StopIteration:
                        prev_ep = None
                if prev_ep is None and idx % stride == stride - 1:
                    if first_rest is not None:
                        try:
                            next(first_rest)
                        except StopIteration:
                            first_rest = None
                    elif gen is not None:
                        next(gen, None)
                if ep_xi is None and idx >= nit - 40 and prev_ep is None and (gen is None or mods_done[0]):
                    ep_xi = [load_x(xp[(4 * G + mi) * 128:(4 * G + mi + 1) * 128, :]) for mi in range(4)]
                    ep_at = idx
                if ep_xi is not None and not ep_stt and idx >= max(nit - 20, ep_at + 10):
                    ep_stt = True
                    for xi in ep_xi:
                        S.op("dve", lambda e, xi=xi: e.scalar_tensor_tensor(xb[xi][:, :], xb[xi][:, :], ALPHA, gb[:, :], ALU.mult, ALU.add),
                             reads=[r_gb], writes=[r_xb[xi]])
            if first_rest is not None:
                for _ in first_rest:
                    pass
            if gen is not None:
                for _ in gen:
                    pass
            if prev_ep is not None:
                for _ in prev_ep:
                    pass
            if ep_xi is None:
                ep_xi = [load_x(xp[(4 * G + mi) * 128:(4 * G + mi + 1) * 128, :]) for mi in range(4)]
            if not ep_stt:
                for xi in ep_xi:
                    S.op("dve", lambda e, xi=xi: e.scalar_tensor_tensor(xb[xi][:, :], xb[xi][:, :], ALPHA, gb[:, :], ALU.mult, ALU.add),
                         reads=[r_gb], writes=[r_xb[xi]])
            for cpair in range(2):
                bank = mbank()
                bview = banks[bank][:, :].bitcast(BF16)

                def tr(e, cpair=cpair, bview=bview):
                    ins = None
                    for cc in range(2):
                        c = cpair * 2 + cc
                        for mi in range(4):
                            ins = e.transpose(bview[:, cc * 512 + mi * 128: cc * 512 + (mi + 1) * 128],
                                              On[:, mi, c * 128:(c + 1) * 128], identb[:, :])
                    return ins
                S.op("pe", tr, reads=[r_On, r_constb], writes=[bres[bank]])
                for cc in range(2):
                    c = cpair * 2 + cc
                    S.op("dve", lambda e, c=c, cc=cc, bview=bview, qb=qb: e.tensor_tensor(
                        yT[:, c, :], bview[:, cc * 512:(cc + 1) * 512], gattb[qb][:, c, :], ALU.mult),
                        reads=[bres[bank], r_gattb[qb]], writes=[r_yT[c]])
            def epilogue(G=G, Y=Y, ep_xi=ep_xi):
                for mi in range(4):
                    m = 4 * G + mi
                    xi = ep_xi[mi]
                    sp2 = m % 2
                    bks = [mbank(), mbank()]
                    for half in range(2):
                        def mm(e, half=half, mi=mi, bank=bks[half], Y=Y):
                            ins = None
                            for kc in range(8):
                                lhs = yT[:, kc, mi * 128:(mi + 1) * 128] if kc < 4 else scr[:, Y + kc - 4, mi * 128:(mi + 1) * 128]
                                ins = e.matmul(banks[bank][:, :], lhs, wA[:, kc, half * 512:(half + 1) * 512],
                                               start=(kc == 0), stop=(kc == 7))
                            return ins
                        S.op("pe", mm, reads=r_yT + [r_scr[Y + g] for g in range(4)] + [r_wA], writes=[bres[bks[half]]])
                        tk = tm_rot[0] % 2
                        tm_rot[0] += 1
                        S.op("dve", lambda e, half=half, bank=bks[half], tk=tk: e.tensor_tensor(
                            tmpfs[tk][:, :], banks[bank][:, :], gate_bc[:, half * 512:(half + 1) * 512], ALU.mult),
                            reads=[bres[bks[half]], r_gate], writes=[r_tmpfs[tk]])
                        S.op("dve", lambda e, half=half, xi=xi, tk=tk: e.tensor_tensor(
                            xb[xi][:, half * 512:(half + 1) * 512], xb[xi][:, half * 512:(half + 1) * 512], tmpfs[tk][:, :], ALU.add),
                            reads=[r_tmpfs[tk]], writes=[r_xb[xi]])
                        S.op("dve", lambda e, half=half, xi=xi, sp2=sp2: e.bn_stats(stats2[sp2][:, half, :], xb[xi][:, half * 512:(half + 1) * 512]),
                             reads=[r_xb[xi]], writes=[r_sm[sp2]])
                        yield
                    S.op("dve", lambda e, sp2=sp2: e.bn_aggr(mv2[sp2][:, :], stats2[sp2][:, :, :].rearrange("p a b -> p (a b)")),
                         writes=[r_sm[sp2]])
                    S.op("pool", lambda e, sp2=sp2: e.tensor_scalar(ve2[sp2][:, :], mv2[sp2][:, 1:2], EPS, 0.0, ALU.add, ALU.add),
                         reads=[r_sm[sp2]], writes=[r_sm[sp2]])
                    S.op("pool", lambda e, sp2=sp2: e.tensor_tensor(rstd2[sp2][:, :], ve2[sp2][:, :], mhalf[:, :], ALU.pow),
                         reads=[r_small], writes=[r_sm[sp2]])
                    S.op("dve", lambda e, xi=xi, sp2=sp2: e.tensor_scalar(xb[xi][:, :], xb[xi][:, :], mv2[sp2][:, 0:1], rstd2[sp2][:, 0:1],
                                                                         ALU.subtract, ALU.mult),
                         reads=[r_sm[sp2]], writes=[r_xb[xi], r_sm[sp2]])
                    S.op("pool", lambda e, xi=xi: e.tensor_tensor(xb[xi][:, :], xb[xi][:, :], lng[:, :], ALU.mult),
                         reads=[r_const], writes=[r_xb[xi]])
                    S.op("pool", lambda e, xi=xi: e.tensor_tensor(xb[xi][:, :], xb[xi][:, :], lnb[:, :], ALU.add),
                         reads=[r_const], writes=[r_xb[xi]])
                    S.dma("pool", lambda e, xi=xi, m=m: e.dma_start(out=out_d[m * 128:(m + 1) * 128, :], in_=xb[xi][:, :]),
                          d_out[xi], reads=[r_xb[xi]])
                    yield

            epg[0] = epilogue()
            if gi == len(GORD) - 1:
                for _ in epg[0]:
                    pass
        S.wait_only("pool", [Tok(d.sem, d.n * 16, "dma", d.key) for d in d_out])
        if debug:
            S.wait_only("sp", [Tok(S.sem[en], S.cnt[en], en, en) for en in ("pe", "act", "dve", "pool")]
                        + [Tok(d.sem, d.n * 16, "dma", d.key) for d in d_out])
            dumps = {"adaT": (adaT, F32), "scale1": (scale1, F32), "gate_bc": (gate_bc, F32), "gb": (gb, F32),
                     "KT": (KT, BF16), "V": (V, BF16), "Cpos": (Cpos, F32), "nlf": (nlf, F32), "biasG1": (biasG[1], F32),
                     "QT": (QT, BF16), "gatt": (gatt, BF16), "yT": (yT, BF16), "scr": (scr, BF16), "gatt1": (gatt1, BF16), "phalo": (phalo, BF16),
                     "wA": (wA, BF16), "hb1": (hb[1], F32), "uT1": (uT[1], BF16), "uT0": (uT[0], BF16), "biasG0": (biasG[0], F32), "sc": (sc, F32), "tot": (tot, F32),
                     "Z": (Z, F32), "bvfp": (bvfp, F32), "masks": (masks, BF16), "bm": (bm, BF16)}
            for nm, (tl, dt) in dumps.items():
                shp = list(tl.shape)
                dd = nc.dram_tensor("dbg_" + nm, shp, dt, kind="ExternalOutput").ap()
                full = tuple(slice(None) for _ in shp)
                S.dma("sp", lambda e, dd=dd, tl=tl, full=full: e.dma_start(out=dd[full], in_=tl[full]), d_tmp)
            S.wait_only("sp", [Tok(d_tmp.sem, d_tmp.n * 16, "dma", d_tmp.key)])

        with nc.Block() as block:
            @block.tensor
            def _(e):
                S.replay("pe", e)

            @block.scalar
            def _(e):
                S.replay("act", e)

            @block.vector
            def _(e):
                S.replay("dve", e)

            @block.gpsimd
            def _(e):
                S.replay("pool", e)

            @block.sync
            def _(e):
                S.replay("sp", e)
    return nc


def _consts(par):
    ident = np.eye(128, dtype=np.float32)
    ones = np.ones((128, 128), np.float32)
    s = np.arange(128)[:, None]
    t = np.arange(128)[None, :]
    U = (s <= t).astype(np.float32)
    glob = np.array([2 * p + par if p < 16 else 2 * (p - 16) + 1 - par for p in range(32)])
    pred = (glob[:, None] < glob[None, :]).astype(np.float32)
    masks = np.zeros((128, 2, 128), np.float32)
    masks[:, 0, :] = np.where(s <= t, 0.0, NEG)
    masks[:, 1, :] = 0.0 if par == 1 else NEG
    bm = np.zeros((128, 8, 128), np.float32)
    bh = np.zeros((128, 36, 16), np.float32)
    eye = np.eye(128, dtype=np.float32)
    for g, w in enumerate(WINS):
        inwin = ((t - s) >= 0) & ((t - s) < w)
        bm[:, g, :] = np.where(inwin, 1.0 / w, 0.0) - eye
        if par == 0:
            cnt = np.minimum(t + 1, w).astype(np.float32)
            bm[:, 4 + g, :] = np.where(inwin, 1.0 / cnt, 0.0) - eye
        else:
            bm[:, 4 + g, :] = bm[:, g, :]
        for j in range(8):
            for i in range(16):
                for tt in range(16):
                    if tt + 16 - i < w:
                        bh[j * 16 + i, j * 4 + g, tt] = 1.0 / w
        if par == 1:
            bh[:, 32 + g, :] = bh[:, 0 * 4 + g, :]
    return ident, ones, U, pred, masks, bm, bh


def _colT(v, n):
    return np.ascontiguousarray(np.asarray(v, np.float32).reshape(n, 128).T)


_NC_CACHE = {}
_DEBUG = [False]
_NG = [4]


def kernel(x, c, w_ada, b_ada, w_in, b_in, w_pool_mix, b_pool_mix, pool_scale, w_out, b_out, ln_g, ln_b):
    x = np.asarray(x, np.float32)
    c = np.asarray(c, np.float32)
    w_ada = np.ascontiguousarray(np.asarray(w_ada, np.float32)[0])
    b_ada = np.asarray(b_ada, np.float32)[0]
    w_in = np.ascontiguousarray(np.asarray(w_in, np.float32)[0])
    b_in = np.asarray(b_in, np.float32)[0]
    w_pm = np.ascontiguousarray(np.asarray(w_pool_mix, np.float32)[0])
    b_pm = np.asarray(b_pool_mix, np.float32)[0]
    psc = np.asarray(pool_scale, np.float32)[0]
    w_out = np.ascontiguousarray(np.asarray(w_out, np.float32)[0])
    b_out = np.asarray(b_out, np.float32)[0]
    ln_g = np.asarray(ln_g, np.float32)[0]
    ln_b = np.asarray(ln_b, np.float32)[0]

    common = {
        "w_ada": w_ada,
        "b_adaT": _colT(b_ada[0:2048], 16),
        "b_gate": np.ascontiguousarray(b_ada[2048:3072].reshape(1, D)),
        "w_in": w_in,
        "b_qT": _colT(b_in[0:512], 4),
        "b_kT": _colT(b_in[512:1024], 4),
        "b_gT": _colT(b_in[2056:3080], 8),
        "b_vfp": np.ascontiguousarray(b_in[1024:2056].reshape(1, 1032)),
        "w_pm": w_pm,
        "b_pmT": np.ascontiguousarray(b_pm.T),
        "pscT": _colT(psc, 4),
        "w_out": w_out,
        "b_out": np.ascontiguousarray(b_out.reshape(1, D)),
        "ln_g": np.ascontiguousarray(ln_g.reshape(1, D)),
        "ln_b": np.ascontiguousarray(ln_b.reshape(1, D)),
    }
    in_maps = []
    for core in range(NCORES):
        b, par = core // 2, core % 2
        xb_ = x[b].reshape(NBLK, 128, D)
        own = [2 * m + par for m in range(16)]
        oth = [2 * m + 1 - par for m in range(16)]
        xp = np.ascontiguousarray(xb_[own + oth].reshape(SEQ, D))
        xh = np.zeros((256, D), np.float32)
        for m in range(16):
            g = own[m]
            if g > 0:
                xh[m * 16:(m + 1) * 16] = x[b, g * 128 - 16:g * 128]
        ident, ones, U, pred, masks, bm, bh = _consts(par)
        mp = dict(common)
        mp.update({"xp": xp, "xpT": np.ascontiguousarray(xp.T), "xhT": np.ascontiguousarray(xh.T), "cT": _colT(c[b], 8), "ident": ident, "ones": ones, "U": U,
                   "pred": pred, "masks": masks, "bm": bm, "bh": bh})
        in_maps.append(mp)

    if "nc" not in _NC_CACHE:
        _NC_CACHE["nc"] = build_nc(_DEBUG[0])
    nc = _NC_CACHE["nc"]
    res = run_bass_kernel_spmd(nc, in_maps, core_ids=list(range(NCORES)))
    if _DEBUG[0]:
        _DEBUG.append(res.results)
    out = np.empty((4, SEQ, D), np.float32)
    for core in range(NCORES):
        b, par = core // 2, core % 2
        o = np.asarray(res.results[core]["out"], np.float32).reshape(16, 128, D)
        for m in range(16):
            g = 2 * m + par
            out[b, g * 128:(g + 1) * 128] = o[m]
    return out
```

```python
import numpy as np
from contextlib import ExitStack
import concourse.bass as bass
import concourse.mybir as mybir
from concourse.bass_utils import run_bass_kernel_spmd

F32 = mybir.dt.float32
BF16 = mybir.dt.bfloat16
AF = mybir.ActivationFunctionType
ALU = mybir.AluOpType

NCORES = 8
SEQ = 4096
D = 1024
NBLK = 32
ALPHA = float(2.0 ** 0.25)
EPS = 1e-5
NEG = -30000.0
WINS = (2, 4, 8, 16)


class Tok:
    __slots__ = ("sem", "val", "eng", "key")

    def __init__(self, sem, val, eng, key):
        self.sem, self.val, self.eng, self.key = sem, val, eng, key


class Res:
    def __init__(self, name, track_reads=True):
        self.name = name
        self.w = None
        self.r = []
        self.track = track_reads


class DSem:
    def __init__(self, sem, key):
        self.sem, self.n, self.key = sem, 0, key


class Sched:
    ENGS = ("pe", "act", "dve", "pool", "sp")

    def __init__(self, nc, es):
        self.nc = nc
        self.es = es
        self.q = {e: [] for e in self.ENGS}
        self.sem = {e: es.enter_context(nc.semaphore("s_" + e)) for e in self.ENGS}
        self.cnt = {e: 0 for e in self.ENGS}
        self.seen = {e: {} for e in self.ENGS}
        self.nd = 0

    def dsem(self, name):
        self.nd += 1
        return DSem(self.es.enter_context(self.nc.semaphore("d_" + name)), "d%d" % self.nd)

    def _waits(self, eng, reads, writes, extra, is_dma=False):
        need = {}

        def add(t):
            if t is None:
                return
            if t.eng == eng and eng == "pe" and not is_dma:
                return
            if self.seen[eng].get(t.key, 0) >= t.val:
                return
            if need.get(t.key, (None, 0))[1] < t.val:
                need[t.key] = (t.sem, t.val)

        for r in reads:
            add(r.w)
        for w in writes:
            add(w.w)
            for t in w.r:
                add(t)
        for t in extra:
            add(t)
        for k, (s, v) in need.items():
            self.seen[eng][k] = v
        return list(need.values())

    def _commit(self, tok, reads, writes):
        for w in writes:
            w.w = tok
            w.r = []
        for r in reads:
            if r.track:
                r.r.append(tok)

    def op(self, eng, fn, reads=(), writes=(), extra=()):
        waits = self._waits(eng, reads, writes, extra)
        self.cnt[eng] += 1
        tok = Tok(self.sem[eng], self.cnt[eng], eng, eng)
        self.q[eng].append((waits, fn, (self.sem[eng], 1)))
        self._commit(tok, reads, writes)
        return tok

    def dma(self, eng, fn, ds, reads=(), writes=(), extra=()):
        waits = self._waits(eng, reads, writes, extra, is_dma=True)
        ds.n += 1
        tok = Tok(ds.sem, ds.n * 16, "dma", ds.key)
        self.q[eng].append((waits, fn, (ds.sem, 16)))
        self._commit(tok, reads, writes)
        return tok

    def wait_only(self, eng, toks):
        waits = self._waits(eng, (), (), toks)
        self.q[eng].append((waits, None, None))

    def replay(self, eng, e):
        for waits, fn, inc in self.q[eng]:
            for s, v in waits:
                e.wait_ge(s, v)
            if fn is not None:
                ins = fn(e)
                ins.then_inc(inc[0], inc[1])


def build_nc(debug=False):
    nc = bass.Bass("TRN2", target_bir_lowering=False)

    def din(name, shape):
        return nc.dram_tensor(name, list(shape), F32, kind="ExternalInput").ap()

    xp = din("xp", [SEQ, D])
    xpT = din("xpT", [D, SEQ])
    xhT = din("xhT", [D, 256])
    cT_d = din("cT", [128, 8])
    w_ada = din("w_ada", [D, 3 * D])
    b_adaT_d = din("b_adaT", [128, 16])
    b_gate_d = din("b_gate", [1, D])
    w_in = din("w_in", [D, 3080])
    b_qT_d = din("b_qT", [128, 4])
    b_kT_d = din("b_kT", [128, 4])
    b_gT_d = din("b_gT", [128, 8])
    b_vfp_d = din("b_vfp", [1, 1032])
    w_pm_d = din("w_pm", [4, 128, 128])
    b_pmT_d = din("b_pmT", [128, 4])
    pscT_d = din("pscT", [128, 4])
    w_out_d = din("w_out", [D, D])
    b_out_d = din("b_out", [1, D])
    ln_g_d = din("ln_g", [1, D])
    ln_b_d = din("ln_b", [1, D])
    ident_d = din("ident", [128, 128])
    ones_d = din("ones", [128, 128])
    U_d = din("U", [128, 128])
    pred_d = din("pred", [32, 32])
    masks_d = din("masks", [128, 2, 128])
    bm_d = din("bm", [128, 8, 128])
    bh_d = din("bh", [128, 36, 16])
    out_d = nc.dram_tensor("out", [2048, D], F32, kind="ExternalOutput").ap()

    wbf = nc.dram_tensor("wbf", [4, 128, 8, 512], BF16, kind="Internal").ap()
    xpT_v = xpT.rearrange("(kc p) t -> p kc t", p=128)
    xhT_v = xhT.rearrange("(kc p) t -> p kc t", p=128)
    w_ada_v = w_ada.rearrange("(kc p) e -> p kc e", p=128)
    w_in_v = w_in.rearrange("(kc p) e -> p kc e", p=128)
    w_out_v = w_out_d.rearrange("(kc p) e -> p kc e", p=128)

    with ExitStack() as es:
        S = Sched(nc, es)

        def sb(name, shape, dt=F32):
            return es.enter_context(nc.sbuf_tensor("sb_" + name, list(shape), dt))

        banks = [es.enter_context(nc.psum_tensor("ps%d" % i, [128, 512], F32)) for i in range(8)]
        bres = [Res("bank%d" % i) for i in range(8)]

        nlf = sb("nlf", [128, NBLK, 8])
        Cpos = sb("Cpos", [128, NBLK, 8])
        biasG = [sb("biasG%d" % i, [128, NBLK, 8]) for i in range(2)]
        ident = sb("ident", [128, 128])
        onesf = sb("onesf", [128, 128])
        Uf = sb("Uf", [128, 128])
        predf = sb("predf", [32, 32])
        tot = sb("tot", [32, 8])
        Z = sb("Z", [32, 256])
        identb = sb("identb", [128, 128], BF16)
        masks = sb("masks", [128, 2, 128], BF16)
        bm = sb("bm", [128, 8, 128], BF16)
        bh = sb("bh", [128, 36, 16], BF16)
        gate_bc = sb("gate_bc", [128, D])
        gb = sb("gb", [128, D])
        lng = sb("lng", [128, D])
        lnb = sb("lnb", [128, D])
        bvfp = sb("bvfp", [128, 1032])
        adaT = sb("adaT", [128, 16])
        scale1 = sb("scale1", [128, 8])
        b_adaT = sb("b_adaT", [128, 16])
        b_qT = sb("b_qT", [128, 4])
        b_kT = sb("b_kT", [128, 4])
        b_gT = sb("b_gT", [128, 8])
        b_pmT = sb("b_pmT", [128, 4])
        pscT = sb("pscT", [128, 4])
        cT = sb("cT", [128, 8])
        sc = sb("sc", [128, 8])
        phalo = sb("phalo", [128, 2, 512], BF16)
        wpm = sb("wpm", [128, 4, 128], BF16)
        NXB = 4
        xb = [sb("xb%d" % i, [128, D]) for i in range(NXB)]
        uT = [sb("uT%d" % i, [128, 8, 512], BF16) for i in range(2)]
        wA = sb("wA", [128, 8, 1032], BF16)
        wS = [sb("wS%d" % i, [128, 8, 512], BF16) for i in range(2)]
        QT = sb("QT", [128, 8, 512], BF16)
        gatt = sb("gatt", [128, 4, 512], BF16)
        scr = sb("scr", [128, 12, 512], BF16)
        yT = sb("yT", [128, 4, 512], BF16)
        gatt1 = sb("gatt1", [128, 4, 512], BF16)
        stats2 = [sb("stats2_%d" % i, [128, 2, 6]) for i in range(2)]
        mv2 = [sb("mv2_%d" % i, [128, 2]) for i in range(2)]
        ve2 = [sb("ve2_%d" % i, [128, 1]) for i in range(2)]
        rstd2 = [sb("rstd2_%d" % i, [128, 1]) for i in range(2)]
        mhalf = sb("mhalf", [128, 1])
        r_sm = [Res("sm0"), Res("sm1")]
        nbg = sb("nbg", [128, 8])
        tmpf = sb("tmpf", [128, 512])
        tmpf2 = sb("tmpf2", [128, 512])
        hb = [sb("hb%d" % i, [128, D]) for i in range(2)]
        rl = sb("rl", [128, 4])
        fl = sb("fl", [128, 4, 8])
        ex = sb("ex", [128, 4, 8])

        r_KT = Res("KT", False)
        r_V = Res("V", False)
        r_nlf = Res("nlf")
        r_Cpos = Res("Cpos", False)
        r_biasG = [Res("biasG0"), Res("biasG1")]
        r_const = Res("const", False)
        r_constb = Res("constb", False)
        r_ada = Res("ada", False)
        r_gate = Res("gate", False)
        r_gb = Res("gb", False)
        r_sc = Res("sc", False)
        r_xb = [Res("xb%d" % i) for i in range(NXB)]
        r_uT = [Res("uT0"), Res("uT1")]
        r_wA = Res("wA")
        r_wS = [Res("wS0"), Res("wS1")]
        r_QT = Res("QT")
        r_gatt = Res("gatt")
        r_scr = [Res("scr%d" % i) for i in range(12)]
        r_yT = [Res("yT%d" % i) for i in range(4)]
        r_small2 = Res("small2", False)
        r_tmpf = Res("tmpf")
        r_tmpf2 = Res("tmpf2")
        tmpfs = [tmpf, tmpf2]
        r_tmpfs = [r_tmpf, r_tmpf2]
        r_hb = [Res("hb0"), Res("hb1")]
        r_small = Res("small")
        r_rl = Res("rl")
        r_phalo = Res("phalo", False)
        r_fl = Res("fl")
        r_tot = Res("tot")
        r_Z = Res("Z")

        d_const = S.dsem("const")
        d_constb = S.dsem("constb")
        d_xb = [S.dsem("xb%d" % i) for i in range(NXB)]
        d_wa = [S.dsem("wa%d" % i) for i in range(3)]
        d_wA = S.dsem("wA")
        d_wS = [S.dsem("wS0"), S.dsem("wS1")]
        d_out = [S.dsem("out%d" % i) for i in range(4)]
        d_tmp = S.dsem("tmp")

        def cload(dst, src, eng="act"):
            r_const.w = S.dma(eng, lambda e, dst=dst, src=src: e.dma_start(out=dst, in_=src), d_const)

        r_c0 = Res("c0", False)
        d_c0 = S.dsem("c0")
        S.dma("act", lambda e: e.dma_start(out=cT[:, :], in_=cT_d[:, :]), d_c0)
        r_c0.w = S.dma("act", lambda e: e.dma_start(out=b_adaT[:, :], in_=b_adaT_d[:, :]), d_c0)
        S.op("act", lambda e: e.activation(sc[:, :], cT[:, :], AF.Silu), reads=[r_c0], writes=[r_sc])
        cload(ident[:, :], ident_d[:, :])
        cload(onesf[:, :], ones_d[:, :])
        cload(Uf[:, :], U_d[:, :])
        cload(predf[:, :], pred_d[:, :])
        cload(b_qT[:, :], b_qT_d[:, :])
        cload(b_kT[:, :], b_kT_d[:, :])
        cload(b_gT[:, :], b_gT_d[:, :])
        cload(b_pmT[:, :], b_pmT_d[:, :])
        cload(pscT[:, :], pscT_d[:, :])
        cload(bvfp[:, :], b_vfp_d.partition_broadcast(128))
        cload(lng[:, :], ln_g_d.partition_broadcast(128))
        cload(lnb[:, :], ln_b_d.partition_broadcast(128))
        cload(gb[:, :], b_out_d.partition_broadcast(128))
        cload(gate_bc[:, :], b_gate_d.partition_broadcast(128))

        def cloadb(dst, src):
            r_constb.w = S.dma("pool", lambda e, dst=dst, src=src: e.dma_start(out=dst, in_=src), d_constb)

        cloadb(identb[:, :], ident_d[:, :])
        cloadb(masks[:, :, :], masks_d[:, :, :])
        cloadb(bm[:, :, :], bm_d[:, :, :])
        cloadb(bh[:, :, :], bh_d[:, :, :])
        cloadb(wpm[:, :, :], w_pm_d.rearrange("g c e -> c g e"))
        for half in range(2):
            r_wA.w = S.dma("pool", lambda e, half=half: e.dma_start(out=wA[:, half * 4:(half + 1) * 4, :],
                                                                   in_=w_in_v[:, half * 4:(half + 1) * 4, 512:1544]),
                           d_wA)

        S.op("dve", lambda e: e.memset(mhalf[:, :], -0.5), writes=[r_small])
        r_wbf = [Res("wbf%d" % i, False) for i in range(4)]
        d_wbf = [S.dsem("wbf%d" % i) for i in range(4)]
        for hh in range(2):
            S.op("pool", lambda e, hh=hh: e.memset(QT[(1 - hh) * 64:(2 - hh) * 64, hh:8:2, :], 0.0), writes=[r_QT])

        with ExitStack() as es2:
            wa = [es2.enter_context(nc.sbuf_tensor("wa%d" % i, [128, 8, 512], F32)) for i in range(3)]
            scbc = es2.enter_context(nc.sbuf_tensor("scbc", [128, 8, 128], F32))
            r_wa = [Res("wa%d" % i) for i in range(3)]
            r_scbc = Res("scbc")

            S.op("dve", lambda e: e.tensor_copy(scbc[:, :, :], sc[:, :].unsqueeze(2).to_broadcast([128, 8, 128])),
                 reads=[r_sc], writes=[r_scbc])
            psA = banks[7]
            for j in range(6):
                buf = wa[j % 3]
                S.dma("sp", lambda e, buf=buf, j=j: e.dma_start(out=buf[:, :, :], in_=w_ada_v[:, :, j * 512:(j + 1) * 512]),
                      d_wa[j % 3], writes=[r_wa[j % 3]])
                if j < 4:
                    def mm(e, buf=buf, j=j):
                        ins = None
                        for i in range(4):
                            fc = j * 4 + i
                            for kc in range(8):
                                ins = e.matmul(psA[:, fc:fc + 1], buf[:, kc, i * 128:(i + 1) * 128], sc[:, kc:kc + 1],
                                               start=(kc == 0), stop=(kc == 7))
                        return ins
                    S.op("pe", mm, reads=[r_wa[j % 3], r_sc], writes=[bres[7]])
                else:
                    bk = 5 + (j - 4)

                    def mm(e, buf=buf, bk=bk):
                        ins = None
                        for kc in range(8):
                            ins = e.matmul(banks[bk][:, :], scbc[:, kc, :], buf[:, kc, :], start=(kc == 0), stop=(kc == 7))
                        return ins
                    S.op("pe", mm, reads=[r_wa[j % 3], r_scbc], writes=[bres[bk]])
                    half = j - 4
                    S.op("dve", lambda e, bk=bk, half=half: e.tensor_tensor(
                        gate_bc[:, half * 512:(half + 1) * 512], banks[bk][:, :],
                        gate_bc[:, half * 512:(half + 1) * 512], ALU.add),
                        reads=[bres[bk], r_const], writes=[r_gate])
                if j == 3:
                    S.op("dve", lambda e: e.tensor_tensor(adaT[:, :], psA[:, 0:16], b_adaT[:, :], ALU.add),
                         reads=[bres[7], r_c0], writes=[r_ada])
                    S.op("dve", lambda e: e.tensor_scalar_add(scale1[:, :], adaT[:, 8:16], 1.0), writes=[r_ada])
            S.op("pool", lambda e: e.tensor_tensor(gb[:, :], gb[:, :], gate_bc[:, :], ALU.mult),
                 reads=[r_gate, r_const], writes=[r_gb])
            t_end_ada = S.op("pe", lambda e: e.matmul(banks[7][:, 0:1], onesf[:, 0:128], onesf[:, 0:1], start=True, stop=True),
                             reads=[r_const], writes=[bres[7]])

        shiftT = adaT
        KT = sb("KT", [128, 4, SEQ], BF16)
        V = sb("V", [128, NBLK, 8, 66], BF16)
        S.op("pool", lambda e: e.memset(V[:, :, :, 64], 1.0), writes=[r_V], extra=[t_end_ada])

        xb_rot = [0]

        def load_x(src_rows, q="sp"):
            i = xb_rot[0] % NXB
            xb_rot[0] += 1
            S.dma(q, lambda e, i=i, src_rows=src_rows: e.dma_start(out=xb[i][:, :], in_=src_rows), d_xb[i],
                  writes=[r_xb[i]])
            return i

        def load_feat(src_v, tok0, ntok, q="sp"):
            per = 1024 // ntok
            bufs = []
            for kc0 in range(0, 8, per):
                i = xb_rot[0] % NXB
                xb_rot[0] += 1
                S.dma(q, lambda e, i=i, kc0=kc0: e.dma_start(
                    out=xb[i][:, :].rearrange("p (a b) -> p a b", b=ntok), in_=src_v[:, kc0:kc0 + per, tok0:tok0 + ntok]),
                    d_xb[i], writes=[r_xb[i]])
                bufs.append(i)
            return bufs

        def modulate_sb(kc, bufs, ntok, ut, eng):
            per = 1024 // ntok
            xi = bufs[kc // per]
            off = (kc % per) * ntok
            if eng == "act":
                S.op("act", lambda e: e.activation(uT[ut][:, kc, 0:ntok], xb[xi][:, off:off + ntok], AF.Identity,
                                                   bias=shiftT[:, kc:kc + 1], scale=scale1[:, kc:kc + 1]),
                     reads=[r_xb[xi], r_ada], writes=[r_uT[ut]])
            else:
                S.op(eng, lambda e: e.tensor_scalar(uT[ut][:, kc, 0:ntok], xb[xi][:, off:off + ntok],
                                                    scale1[:, kc:kc + 1], shiftT[:, kc:kc + 1], ALU.mult, ALU.add),
                     reads=[r_xb[xi], r_ada], writes=[r_uT[ut]])

        ENG_MIX = ["dve", "act", "pool", "act", "dve", "act", "pool", "act"]

        rotT = [0]
        rotKV = [0]
        KVB = [0, 1, 2, 3, 4, 5, 7]
        def emit_T(ch):
            bufs = load_feat(xpT_v, ch * 512, 512, "act" if ch == 0 else "sp")
            for kc in range(8):
                modulate_sb(kc, bufs, 512, ch % 2, "act" if ch == 0 else ENG_MIX[kc])

        def emit_K(ch):
            ut = ch % 2
            for pair in range(4):
                bank = KVB[rotKV[0] % len(KVB)]
                rotKV[0] += 1

                def mm(e, pair=pair, bank=bank, ut=ut):
                    ins = None
                    for kc in range(8):
                        ins = e.matmul(banks[bank][:, :], wA[:, kc, pair * 128:(pair + 1) * 128], uT[ut][:, kc, :],
                                       start=(kc == 0), stop=(kc == 7))
                    return ins
                S.op("pe", mm, reads=[r_wA, r_uT[ut]], writes=[bres[bank]])
                S.op("act", lambda e, pair=pair, bank=bank, ch=ch: e.activation(
                    KT[:, pair, ch * 512:(ch + 1) * 512], banks[bank][:, :], AF.Identity, bias=b_kT[:, pair:pair + 1], scale=1.0),
                    reads=[bres[bank], r_const], writes=[r_KT], extra=[t_end_ada])

        def emit_V(ch):
            ut = ch % 2
            for bi in range(4):
                pos = ch * 4 + bi
                bank = KVB[rotKV[0] % len(KVB)]
                rotKV[0] += 1

                def mm(e, bi=bi, bank=bank, ut=ut):
                    ins = None
                    for kc in range(8):
                        ins = e.matmul(banks[bank][:, :], uT[ut][:, kc, bi * 128:(bi + 1) * 128], wA[:, kc, 512:1024],
                                       start=(kc == 0), stop=(kc == 7))
                    return ins
                S.op("pe", mm, reads=[r_wA, r_uT[ut]], writes=[bres[bank]])
                S.op("dve", lambda e, pos=pos, bank=bank: e.tensor_tensor(
                    V[:, pos, :, 0:64], banks[bank][:, :].rearrange("p (h d) -> p h d", d=64),
                    bvfp[:, 0:512].rearrange("p (h d) -> p h d", d=64), ALU.add),
                    reads=[bres[bank], r_const], writes=[r_V], extra=[t_end_ada])

            def mmf(e, ut=ut):
                ins = None
                for bi in range(4):
                    for kc in range(8):
                        ins = e.matmul(banks[6][:, bi * 8:(bi + 1) * 8], uT[ut][:, kc, bi * 128:(bi + 1) * 128],
                                       wA[:, kc, 1024:1032], start=(kc == 0), stop=(kc == 7))
                return ins
            S.op("pe", mmf, reads=[r_wA, r_uT[ut]], writes=[bres[6]])
            S.op("dve", lambda e: e.tensor_tensor(
                fl[:, :, :], banks[6][:, 0:32].rearrange("p (b h) -> p b h", h=8),
                bvfp[:, 512:520].unsqueeze(1).to_broadcast([128, 4, 8]), ALU.add),
                reads=[bres[6], r_const], writes=[r_fl])
            S.op("act", lambda e: e.activation(ex[:, :, :], fl[:, :, :], AF.Exp, scale=-1.0), reads=[r_fl], writes=[r_small])
            S.op("act", lambda e, ch=ch: e.activation(nlf[:, ch * 4:(ch + 1) * 4, :], ex[:, :, :], AF.Ln, bias=1.0, scale=1.0),
                 reads=[r_small], writes=[r_nlf, r_fl])

        emit_T(0)
        for ch in range(8):
            emit_K(ch)
            if ch == 2:
                for c0, pc in ((0, 0), (2056, 1), (1544, 3), (2568, 2)):
                    S.dma("pool", lambda e, c0=c0, pc=pc: e.dma_start(out=wbf[pc, :, :, :], in_=w_in_v[:, :, c0:c0 + 512]),
                          d_wbf[pc], writes=[r_wbf[pc]], extra=[r_KT.w])
            if ch + 1 < 8:
                emit_T(ch + 1)
            emit_V(ch)

        def mmtot(e):
            ins = None
            for h in range(8):
                ins = e.matmul(banks[0][0:32, 256 + h:257 + h], nlf[:, :, h], onesf[:, 0:1], start=True, stop=True)
            return ins
        def cum1():
            S.op("pe", mmtot, reads=[r_nlf, r_const], writes=[bres[0]])
            S.op("dve", lambda e: e.tensor_copy(tot[:, :], banks[0][0:32, 256:264]), reads=[bres[0]], writes=[r_tot])
            S.op("dve", lambda e: e.tensor_tensor(
                Z[:, :].rearrange("p (b h) -> p b h", h=8), tot[:, :].unsqueeze(1).to_broadcast([32, 32, 8]),
                predf[:, :].unsqueeze(2).to_broadcast([32, 32, 8]), ALU.mult), reads=[r_tot, r_const], writes=[r_Z])

        def mmcum(e):
            e.matmul(banks[0][:, 0:256], Uf[:, :], nlf[:, :, :].rearrange("p b h -> p (b h)"), start=True, stop=False)
            return e.matmul(banks[0][:, 0:256], onesf[0:32, :], Z[:, :], start=False, stop=True)
        def cum2():
            S.op("pe", mmcum, reads=[r_nlf, r_Z, r_const], writes=[bres[0]])
            S.op("dve", lambda e: e.tensor_copy(Cpos[:, :, :].rearrange("p b h -> p (b h)"), banks[0][:, 0:256]),
                 reads=[bres[0]], writes=[r_Cpos])

        SB_ = [0, 1, 2, 3]
        OB_ = [4, 5]
        MB_ = [6, 7]
        rotM = [0]
        rotW = [0]

        def mbank():
            b = MB_[rotM[0] % len(MB_)]
            rotM[0] += 1
            return b

        PIECE = {0: 0, 2056: 1, 2568: 2, 1544: 3}

        def load_ws(c0, first=False):
            i = rotW[0] % 2
            rotW[0] += 1
            pc = PIECE[c0]
            S.dma("pool", lambda e, i=i, pc=pc: e.dma_start(out=wS[i][:, :, :], in_=wbf[pc, :, :, :]),
                  d_wS[i], reads=[r_wbf[pc]], writes=[r_wS[i]])
            return i

        PCOL = 1544
        QCOL = 0
        GACOL = 2056
        GPCOL = 2568

        On = hb[0][:, :].bitcast(BF16).rearrange("p (a b) -> p a b", b=512)
        PTv = hb[1][:, :].bitcast(BF16).rearrange("p (a b) -> p a b", b=512)
        r_On = Res("On")
        r_pt = [Res("pt%d" % i) for i in range(4)]
        QTb = [QT, uT[1]]
        r_QTb = [r_QT, r_uT[1]]
        gattb = [gatt, gatt1]
        r_gattb = [r_gatt, Res("gatt1")]
        nb_gT = nbg
        S.op("dve", lambda e: e.tensor_scalar(nb_gT[:, :], b_gT[:, :], 0.5, None, ALU.mult), reads=[r_const], writes=[r_small2])

        tm_rot = [0]

        def silu_evac(bank, c8, out_ap, out_res):
            ti = tm_rot[0] % 2
            tm_rot[0] += 1
            tm, r_tm = tmpfs[ti], r_tmpfs[ti]
            S.op("act", lambda e: e.activation(tm[:, :], banks[bank][:, :], AF.Tanh, bias=nb_gT[:, c8:c8 + 1], scale=0.5),
                 reads=[bres[bank], r_small2], writes=[r_tm])
            S.op("pool", lambda e: e.tensor_scalar(tm[:, :], tm[:, :], 0.5, 0.5, ALU.mult, ALU.add), reads=[r_tm], writes=[r_tm])
            S.op("dve", lambda e: e.scalar_tensor_tensor(out_ap, banks[bank][:, :], b_gT[:, c8:c8 + 1], tm[:, :], ALU.add, ALU.mult),
                 reads=[bres[bank], r_tm, r_const], writes=out_res)

        def group2(lhs_fn, rhs_fn, rd, bank):
            for part in range(2):
                def mm(e, part=part):
                    ins = None
                    for kc in range(part * 4, part * 4 + 4):
                        ins = e.matmul(banks[bank][:, :], lhs_fn(kc), rhs_fn(kc), start=(kc == 0), stop=(kc == 7))
                    return ins
                S.op("pe", mm, reads=rd, writes=[bres[bank]])
                yield

        mods_done = [False]
        GORD = [3, 2, 1, 0]

        def emit_biasG(G):
            bG = G % 2
            bank = mbank()
            S.op("pe", lambda e, bank=bank, G=G: e.matmul(banks[bank][:, 0:8], onesf[:, :], Cpos[:, 4 * G + 2, :], start=True, stop=True),
                 reads=[r_Cpos, r_const], writes=[bres[bank]])
            S.op("dve", lambda e, bank=bank, bG=bG: e.scalar_tensor_tensor(
                biasG[bG][:, :, :], banks[bank][:, 0:8].unsqueeze(1).to_broadcast([128, NBLK, 8]), -1.0 / 128.0,
                Cpos[:, :, :], ALU.mult, ALU.add), reads=[bres[bank], r_Cpos], writes=[r_biasG[bG]])

        def prelude(G, overlapped, n_sp=14):
            qb = (G + 1) % 2
            X, Y = 4 * ((G + 2) % 3), 4 * (G % 3)
            wq = load_ws(QCOL, G == 0)
            wga = load_ws(GACOL, G == 0)
            bufs = load_feat(xpT_v, G * 512, 512)
            for _ in range(n_sp):
                yield
            for kc in range(8):
                modulate_sb(kc, bufs, 512, 0, "pool" if overlapped else ENG_MIX[kc])
                if kc % 2 == 1:
                    yield
            mods_done[0] = True
            if G == GORD[0]:
                hbufs = load_feat(xhT_v, 0, 256)
                for kc in range(8):
                    modulate_sb(kc, hbufs, 256, 1, ENG_MIX[kc])
            if G == GORD[1]:
                for hh in range(2):
                    S.op("pool", lambda e, hh=hh: e.memset(uT[1][(1 - hh) * 64:(2 - hh) * 64, hh:8:2, :], 0.0), writes=[r_uT[1]])
            for c in range(4):
                bank = mbank()
                for _ in group2(lambda kc, c=c: wS[wq][:, kc, c * 128:(c + 1) * 128], lambda kc: uT[0][:, kc, :],
                                  [r_wS[wq], r_uT[0]], bank):
                    pass
                for hh in range(2):
                    S.op("dve", lambda e, hh=hh, c=c, bank=bank: e.tensor_scalar(
                        QTb[qb][hh * 64:(hh + 1) * 64, 2 * c + hh, :], banks[bank][hh * 64:(hh + 1) * 64, :],
                        b_qT[hh * 64:(hh + 1) * 64, c:c + 1], None, ALU.add),
                        reads=[bres[bank], r_const], writes=[r_QTb[qb]])
                yield
            wp = load_ws(PCOL, G == 0)
            for c in range(4):
                bank = mbank()
                for _ in group2(lambda kc, c=c: wS[wga][:, kc, c * 128:(c + 1) * 128], lambda kc: uT[0][:, kc, :],
                                  [r_wS[wga], r_uT[0]], bank):
                    pass
                silu_evac(bank, c, gattb[qb][:, c, :], [r_gattb[qb]])
                yield
            wgp = load_ws(GPCOL, G == 0)
            for mi in range(4):
                bank = mbank()
                for _ in group2(lambda kc, mi=mi: uT[0][:, kc, mi * 128:(mi + 1) * 128], lambda kc: wS[wp][:, kc, :],
                                  [r_wS[wp], r_uT[0]], bank):
                    pass
                S.op("dve", lambda e, mi=mi, bank=bank: e.tensor_tensor(scr[:, X + mi, :], banks[bank][:, :], bvfp[:, 520:1032], ALU.add),
                     reads=[bres[bank], r_const], writes=[r_scr[X + mi]])
                yield
            if G == GORD[0]:
                for hbk in range(2):
                    bank = mbank()
                    for _ in group2(lambda kc, hbk=hbk: uT[1][:, kc, hbk * 128:(hbk + 1) * 128], lambda kc: wS[wp][:, kc, :],
                                    [r_wS[wp], r_uT[1]], bank):
                        pass
                    S.op("dve", lambda e, hbk=hbk, bank=bank: e.tensor_tensor(phalo[:, hbk, :], banks[bank][:, :], bvfp[:, 520:1032], ALU.add),
                         reads=[bres[bank], r_const], writes=[r_phalo])
            for mi in range(4):
                m = 4 * G + mi
                bank = mbank()

                def mm(e, mi=mi, m=m, bank=bank):
                    ins = None
                    for g in range(4):
                        bmi = g if m > 0 else 4 + g
                        bhi = ((m % 8) * 4 + g) if m > 0 else 32 + g
                        e.matmul(banks[bank][:, g * 128:(g + 1) * 128], scr[:, X + mi, g * 128:(g + 1) * 128], bm[:, bmi, :],
                                 start=True, stop=False)
                        ins = e.matmul(banks[bank][:, g * 128:g * 128 + 16], phalo[:, m // 8, g * 128:(g + 1) * 128],
                                       bh[:, bhi, :], start=False, stop=True)
                    return ins
                S.op("pe", mm, reads=[r_scr[X + mi], r_phalo, r_constb], writes=[bres[bank]])
                S.op("dve", lambda e, mi=mi, bank=bank: e.tensor_copy(
                    scr[:, Y:Y + 4, mi * 128:(mi + 1) * 128], banks[bank][:, :].rearrange("p (g t) -> p g t", t=128)),
                    reads=[bres[bank]], writes=[r_scr[Y + g] for g in range(4)])
                yield
            for c in range(4):
                bank = mbank()
                for _ in group2(lambda kc, c=c: wS[wgp][:, kc, c * 128:(c + 1) * 128], lambda kc: uT[0][:, kc, :],
                                  [r_wS[wgp], r_uT[0]], bank):
                    pass
                silu_evac(bank, 4 + c, scr[:, X + c, :], [r_scr[X + c]])
                yield
            for g in range(4):
                bank = mbank()
                S.op("pe", lambda e, g=g, bank=bank: e.matmul(banks[bank][:, :], wpm[:, g, :], scr[:, Y + g, :], start=True, stop=True),
                     reads=[r_scr[Y + g], r_constb], writes=[bres[bank]])
                S.op("dve", lambda e, g=g, bank=bank: e.tensor_scalar(
                    tmpf[:, :], banks[bank][:, :], b_pmT[:, g:g + 1], pscT[:, g:g + 1], ALU.add, ALU.mult),
                    reads=[bres[bank], r_const], writes=[r_tmpf])
                S.op("pool", lambda e, g=g: e.tensor_tensor(scr[:, Y + g, :], tmpf[:, :], scr[:, X + g, :], ALU.mult),
                     reads=[r_tmpf, r_scr[X + g]], writes=[r_scr[Y + g]])
                yield
            if overlapped:
                emit_biasG(G)

        pg = prelude(GORD[0], False, 0)
        for _ in range(4):
            next(pg, None)
        cum1()
        for _ in range(4):
            next(pg, None)
        cum2()


        epg = [None]
        for gi, G in enumerate(GORD):
            Gn = GORD[gi + 1] if gi + 1 < len(GORD) else None
            qb = (G + 1) % 2
            Y = 4 * (G % 3)
            nit_g = 8 * (8 * G + 8)
            sp_items = 40 if gi == 0 else 45
            stride = max(1, nit_g // (50 + sp_items))
            if gi == 0:
                sp_items, stride = 24, 4
            gen = prelude(Gn, True, -(-sp_items // stride)) if Gn is not None else None
            if gi == 0:
                for half in range(2):
                    S.dma("pool", lambda e, half=half: e.dma_start(out=wA[:, half * 4:(half + 1) * 4, 0:1024],
                                                                  in_=w_out_v[:, half * 4:(half + 1) * 4, :]),
                          d_wA, writes=[r_wA])
            bG = G % 2
            if gi == 0:
                emit_biasG(G)

            nk = 4 * G + 4
            kblocks = [(i, i) for i in range(nk)] + [(16 + i, i) for i in range(nk)]
            items = []
            for h in range(8):
                for j, (pos, i) in enumerate(kblocks):
                    items.append((h, pos, i, j == 0, j == len(kblocks) - 1))
            LA = 3
            prev_ep = epg[0]
            first_rest = pg if gi == 0 else None
            ep_xi = None
            ep_stt = False
            ep_at = 0
            mods_done[0] = False
            nit = len(items)
            for idx in range(nit + LA):
                if idx < nit:
                    h, pos, i, first, last = items[idx]
                    pair, hh = h // 2, h % 2
                    own = pos < 16
                    col0 = max(0, i - 4 * G) * 128
                    masked = i >= 4 * G
                    sbk = SB_[idx % len(SB_)]
                    pti = idx % 4

                    def mms(e, pos=pos, col0=col0, masked=masked, sbk=sbk, own=own, pair=pair, h=h, qb=qb):
                        ins = e.matmul(banks[sbk][:, col0:512], KT[:, pair, pos * 128:(pos + 1) * 128],
                                       QTb[qb][:, h, col0:512], start=True, stop=(not masked))
                        if masked:
                            ins = e.matmul(banks[sbk][:, col0:col0 + 128], identb[:, :], masks[:, 0 if own else 1, :],
                                           start=False, stop=True)
                        return ins
                    S.op("pe", mms, reads=[r_KT, r_QTb[qb], r_constb], writes=[bres[sbk]])
                    S.op("act", lambda e, pos=pos, col0=col0, sbk=sbk, pti=pti, h=h, bG=bG: e.activation(
                        PTv[:, pti, col0:512], banks[sbk][:, col0:512], AF.Exp, bias=biasG[bG][:, pos, h:h + 1], scale=0.125),
                        reads=[bres[sbk], r_biasG[bG]], writes=[r_pt[pti]])
                if idx >= LA:
                    jdx = idx - LA
                    h, pos, i, first, last = items[jdx]
                    own = pos < 16
                    col0 = max(0, i - 4 * G) * 128
                    pti = jdx % 4
                    ob = OB_[h % 2]

                    def mmpv(e, pos=pos, col0=col0, pti=pti, h=h, ob=ob, first=first, i=i, own=own, G=G):
                        ins = None
                        for mi in range(col0 // 128, 4):
                            st = first and mi == 0
                            sp_ = (not own) and (i == 4 * G + mi)
                            ins = e.matmul(banks[ob][:, mi * 65:(mi + 1) * 65], PTv[:, pti, mi * 128:(mi + 1) * 128],
                                           V[:, pos, h, 0:65], start=st, stop=sp_, skip_group_check=True)
                        return ins
                    S.op("pe", mmpv, reads=[r_pt[pti], r_V], writes=[bres[ob]])
                    if last:
                        S.op("dve", lambda e, ob=ob: e.reciprocal(
                            rl[:, :], banks[ob][:, 0:260].rearrange("p (a b) -> p a b", b=65)[:, :, 64]),
                            reads=[bres[ob]], writes=[r_rl])
                        S.op("dve", lambda e, ob=ob, h=h: e.tensor_tensor(
                            On[:, :, h * 64:(h + 1) * 64], banks[ob][:, 0:260].rearrange("p (a b) -> p a b", b=65)[:, :, 0:64],
                            rl[:, :].unsqueeze(2).to_broadcast([128, 4, 64]), ALU.mult),
                            reads=[bres[ob], r_rl], writes=[r_On])
                if prev_ep is not None and idx % 2 == 1:
                    try:
                        next(prev_ep)
                    except StopIteration:
                        prev_ep = None
                if prev_ep is None and idx % stride == stride - 1:
                    if first_rest is not None:
                        try:
                            next(first_rest)
                        except StopIteration:
                            first_rest = None
                    elif gen is not None:
                        next(gen, None)
                if ep_xi is None and idx >= nit - 40 and prev_ep is None and (gen is None or mods_done[0]):
                    ep_xi = [load_x(xp[(4 * G + mi) * 128:(4 * G + mi + 1) * 128, :]) for mi in range(4)]
                    ep_at = idx
                if ep_xi is not None and not ep_stt and idx >= max(nit - 20, ep_at + 10):
                    ep_stt = True
                    for xi in ep_xi:
                        S.op("dve", lambda e, xi=xi: e.scalar_tensor_tensor(xb[xi][:, :], xb[xi][:, :], ALPHA, gb[:, :], ALU.mult, ALU.add),
                             reads=[r_gb], writes=[r_xb[xi]])
            if first_rest is not None:
                for _ in first_rest:
                    pass
            if gen is not None:
                for _ in gen:
                    pass
            if prev_ep is not None:
                for _ in prev_ep:
                    pass
            if ep_xi is None:
                ep_xi = [load_x(xp[(4 * G + mi) * 128:(4 * G + mi + 1) * 128, :]) for mi in range(4)]
            if not ep_stt:
                for xi in ep_xi:
                    S.op("dve", lambda e, xi=xi: e.scalar_tensor_tensor(xb[xi][:, :], xb[xi][:, :], ALPHA, gb[:, :], ALU.mult, ALU.add),
                         reads=[r_gb], writes=[r_xb[xi]])
            for cpair in range(2):
                bank = mbank()
                bview = banks[bank][:, :].bitcast(BF16)

                def tr(e, cpair=cpair, bview=bview):
                    ins = None
                    for cc in range(2):
                        c = cpair * 2 + cc
                        for mi in range(4):
                            ins = e.transpose(bview[:, cc * 512 + mi * 128: cc * 512 + (mi + 1) * 128],
                                              On[:, mi, c * 128:(c + 1) * 128], identb[:, :])
                    return ins
                S.op("pe", tr, reads=[r_On, r_constb], writes=[bres[bank]])
                for cc in range(2):
                    c = cpair * 2 + cc
                    S.op("dve", lambda e, c=c, cc=cc, bview=bview, qb=qb: e.tensor_tensor(
                        yT[:, c, :], bview[:, cc * 512:(cc + 1) * 512], gattb[qb][:, c, :], ALU.mult),
                        reads=[bres[bank], r_gattb[qb]], writes=[r_yT[c]])
            def epilogue(G=G, Y=Y, ep_xi=ep_xi):
                for mi in range(4):
                    m = 4 * G + mi
                    xi = ep_xi[mi]
                    sp2 = m % 2
                    bks = [mbank(), mbank()]
                    for half in range(2):
                        def mm(e, half=half, mi=mi, bank=bks[half], Y=Y):
                            ins = None
                            for kc in range(8):
                                lhs = yT[:, kc, mi * 128:(mi + 1) * 128] if kc < 4 else scr[:, Y + kc - 4, mi * 128:(mi + 1) * 128]
                                ins = e.matmul(banks[bank][:, :], lhs, wA[:, kc, half * 512:(half + 1) * 512],
                                               start=(kc == 0), stop=(kc == 7))
                            return ins
                        S.op("pe", mm, reads=r_yT + [r_scr[Y + g] for g in range(4)] + [r_wA], writes=[bres[bks[half]]])
                        tk = tm_rot[0] % 2
                        tm_rot[0] += 1
                        S.op("dve", lambda e, half=half, bank=bks[half], tk=tk: e.tensor_tensor(
                            tmpfs[tk][:, :], banks[bank][:, :], gate_bc[:, half * 512:(half + 1) * 512], ALU.mult),
                            reads=[bres[bks[half]], r_gate], writes=[r_tmpfs[tk]])
                        S.op("dve", lambda e, half=half, xi=xi, tk=tk: e.tensor_tensor(
                            xb[xi][:, half * 512:(half + 1) * 512], xb[xi][:, half * 512:(half + 1) * 512], tmpfs[tk][:, :], ALU.add),
                            reads=[r_tmpfs[tk]], writes=[r_xb[xi]])
                        S.op("dve", lambda e, half=half, xi=xi, sp2=sp2: e.bn_stats(stats2[sp2][:, half, :], xb[xi][:, half * 512:(half + 1) * 512]),
                             reads=[r_xb[xi]], writes=[r_sm[sp2]])
                        yield
                    S.op("dve", lambda e, sp2=sp2: e.bn_aggr(mv2[sp2][:, :], stats2[sp2][:, :, :].rearrange("p a b -> p (a b)")),
                         writes=[r_sm[sp2]])
                    S.op("pool", lambda e, sp2=sp2: e.tensor_scalar(ve2[sp2][:, :], mv2[sp2][:, 1:2], EPS, 0.0, ALU.add, ALU.add),
                         reads=[r_sm[sp2]], writes=[r_sm[sp2]])
                    S.op("pool", lambda e, sp2=sp2: e.tensor_tensor(rstd2[sp2][:, :], ve2[sp2][:, :], mhalf[:, :], ALU.pow),
                         reads=[r_small], writes=[r_sm[sp2]])
                    S.op("dve", lambda e, xi=xi, sp2=sp2: e.tensor_scalar(xb[xi][:, :], xb[xi][:, :], mv2[sp2][:, 0:1], rstd2[sp2][:, 0:1],
                                                                         ALU.subtract, ALU.mult),
                         reads=[r_sm[sp2]], writes=[r_xb[xi], r_sm[sp2]])
                    S.op("pool", lambda e, xi=xi: e.tensor_tensor(xb[xi][:, :], xb[xi][:, :], lng[:, :], ALU.mult),
                         reads=[r_const], writes=[r_xb[xi]])
                    S.op("pool", lambda e, xi=xi: e.tensor_tensor(xb[xi][:, :], xb[xi][:, :], lnb[:, :], ALU.add),
                         reads=[r_const], writes=[r_xb[xi]])
                    S.dma("pool", lambda e, xi=xi, m=m: e.dma_start(out=out_d[m * 128:(m + 1) * 128, :], in_=xb[xi][:, :]),
                          d_out[xi], reads=[r_xb[xi]])
                    yield

            epg[0] = epilogue()
            if gi == len(GORD) - 1:
                for _ in epg[0]:
                    pass
        S.wait_only("pool", [Tok(d.sem, d.n * 16, "dma", d.key) for d in d_out])
        if debug:
            S.wait_only("sp", [Tok(S.sem[en], S.cnt[en], en, en) for en in ("pe", "act", "dve", "pool")]
                        + [Tok(d.sem, d.n * 16, "dma", d.key) for d in d_out])
            dumps = {"adaT": (adaT, F32), "scale1": (scale1, F32), "gate_bc": (gate_bc, F32), "gb": (gb, F32),
                     "KT": (KT, BF16), "V": (V, BF16), "Cpos": (Cpos, F32), "nlf": (nlf, F32), "biasG1": (biasG[1], F32),
                     "QT": (QT, BF16), "gatt": (gatt, BF16), "yT": (yT, BF16), "scr": (scr, BF16), "gatt1": (gatt1, BF16), "phalo": (phalo, BF16),
                     "wA": (wA, BF16), "hb1": (hb[1], F32), "uT1": (uT[1], BF16), "uT0": (uT[0], BF16), "biasG0": (biasG[0], F32), "sc": (sc, F32), "tot": (tot, F32),
                     "Z": (Z, F32), "bvfp": (bvfp, F32), "masks": (masks, BF16), "bm": (bm, BF16)}
            for nm, (tl, dt) in dumps.items():
                shp = list(tl.shape)
                dd = nc.dram_tensor("dbg_" + nm, shp, dt, kind="ExternalOutput").ap()
                full = tuple(slice(None) for _ in shp)
                S.dma("sp", lambda e, dd=dd, tl=tl, full=full: e.dma_start(out=dd[full], in_=tl[full]), d_tmp)
            S.wait_only("sp", [Tok(d_tmp.sem, d_tmp.n * 16, "dma", d_tmp.key)])

        with nc.Block() as block:
            @block.tensor
            def _(e):
                S.replay("pe", e)

            @block.scalar
            def _(e):
                S.replay("act", e)

            @block.vector
            def _(e):
                S.replay("dve", e)

            @block.gpsimd
            def _(e):
                S.replay("pool", e)

            @block.sync
            def _(e):
                S.replay("sp", e)
    return nc


def _consts(par):
    ident = np.eye(128, dtype=np.float32)
    ones = np.ones((128, 128), np.float32)
    s = np.arange(128)[:, None]
    t = np.arange(128)[None, :]
    U = (s <= t).astype(np.float32)
    glob = np.array([2 * p + par if p < 16 else 2 * (p - 16) + 1 - par for p in range(32)])
    pred = (glob[:, None] < glob[None, :]).astype(np.float32)
    masks = np.zeros((128, 2, 128), np.float32)
    masks[:, 0, :] = np.where(s <= t, 0.0, NEG)
    masks[:, 1, :] = 0.0 if par == 1 else NEG
    bm = np.zeros((128, 8, 128), np.float32)
    bh = np.zeros((128, 36, 16), np.float32)
    eye = np.eye(128, dtype=np.float32)
    for g, w in enumerate(WINS):
        inwin = ((t - s) >= 0) & ((t - s) < w)
        bm[:, g, :] = np.where(inwin, 1.0 / w, 0.0) - eye
        if par == 0:
            cnt = np.minimum(t + 1, w).astype(np.float32)
            bm[:, 4 + g, :] = np.where(inwin, 1.0 / cnt, 0.0) - eye
        else:
            bm[:, 4 + g, :] = bm[:, g, :]
        for j in range(8):
            for i in range(16):
                for tt in range(16):
                    if tt + 16 - i < w:
                        bh[j * 16 + i, j * 4 + g, tt] = 1.0 / w
        if par == 1:
            bh[:, 32 + g, :] = bh[:, 0 * 4 + g, :]
    return ident, ones, U, pred, masks, bm, bh


def _colT(v, n):
    return np.ascontiguousarray(np.asarray(v, np.float32).reshape(n, 128).T)


_NC_CACHE = {}
_DEBUG = [False]
_NG = [4]


def kernel(x, c, w_ada, b_ada, w_in, b_in, w_pool_mix, b_pool_mix, pool_scale, w_out, b_out, ln_g, ln_b):
    x = np.asarray(x, np.float32)
    c = np.asarray(c, np.float32)
    w_ada = np.ascontiguousarray(np.asarray(w_ada, np.float32)[0])
    b_ada = np.asarray(b_ada, np.float32)[0]
    w_in = np.ascontiguousarray(np.asarray(w_in, np.float32)[0])
    b_in = np.asarray(b_in, np.float32)[0]
    w_pm = np.ascontiguousarray(np.asarray(w_pool_mix, np.float32)[0])
    b_pm = np.asarray(b_pool_mix, np.float32)[0]
    psc = np.asarray(pool_scale, np.float32)[0]
    w_out = np.ascontiguousarray(np.asarray(w_out, np.float32)[0])
    b_out = np.asarray(b_out, np.float32)[0]
    ln_g = np.asarray(ln_g, np.float32)[0]
    ln_b = np.asarray(ln_b, np.float32)[0]

    common = {
        "w_ada": w_ada,
        "b_adaT": _colT(b_ada[0:2048], 16),
        "b_gate": np.ascontiguousarray(b_ada[2048:3072].reshape(1, D)),
        "w_in": w_in,
        "b_qT": _colT(b_in[0:512], 4),
        "b_kT": _colT(b_in[512:1024], 4),
        "b_gT": _colT(b_in[2056:3080], 8),
        "b_vfp": np.ascontiguousarray(b_in[1024:2056].reshape(1, 1032)),
        "w_pm": w_pm,
        "b_pmT": np.ascontiguousarray(b_pm.T),
        "pscT": _colT(psc, 4),
        "w_out": w_out,
        "b_out": np.ascontiguousarray(b_out.reshape(1, D)),
        "ln_g": np.ascontiguousarray(ln_g.reshape(1, D)),
        "ln_b": np.ascontiguousarray(ln_b.reshape(1, D)),
    }
    in_maps = []
    for core in range(NCORES):
        b, par = core // 2, core % 2
        xb_ = x[b].reshape(NBLK, 128, D)
        own = [2 * m + par for m in range(16)]
        oth = [2 * m + 1 - par for m in range(16)]
        xp = np.ascontiguousarray(xb_[own + oth].reshape(SEQ, D))
        xh = np.zeros((256, D), np.float32)
        for m in range(16):
            g = own[m]
            if g > 0:
                xh[m * 16:(m + 1) * 16] = x[b, g * 128 - 16:g * 128]
        ident, ones, U, pred, masks, bm, bh = _consts(par)
        mp = dict(common)
        mp.update({"xp": xp, "xpT": np.ascontiguousarray(xp.T), "xhT": np.ascontiguousarray(xh.T), "cT": _colT(c[b], 8), "ident": ident, "ones": ones, "U": U,
                   "pred": pred, "masks": masks, "bm": bm, "bh": bh})
        in_maps.append(mp)

    if "nc" not in _NC_CACHE:
        _NC_CACHE["nc"] = build_nc(_DEBUG[0])
    nc = _NC_CACHE["nc"]
    res = run_bass_kernel_spmd(nc, in_maps, core_ids=list(range(NCORES)))
    if _DEBUG[0]:
        _DEBUG.append(res.results)
    out = np.empty((4, SEQ, D), np.float32)
    for core in range(NCORES):
        b, par = core // 2, core % 2
        o = np.asarray(res.results[core]["out"], np.float32).reshape(16, 128, D)
        for m in range(16):
            g = 2 * m + par
            out[b, g * 128:(g + 1) * 128] = o[m]
    return out
```

```python
import numpy as np
from contextlib import ExitStack
import concourse.bass as bass
import concourse.mybir as mybir
from concourse.bass_utils import run_bass_kernel_spmd

F32 = mybir.dt.float32
BF16 = mybir.dt.bfloat16
AF = mybir.ActivationFunctionType
ALU = mybir.AluOpType

NCORES = 8
SEQ = 4096
D = 1024
NBLK = 32
ALPHA = float(2.0 ** 0.25)
EPS = 1e-5
NEG = -30000.0
WINS = (2, 4, 8, 16)


class Tok:
    __slots__ = ("sem", "val", "eng", "key")

    def __init__(self, sem, val, eng, key):
        self.sem, self.val, self.eng, self.key = sem, val, eng, key


class Res:
    def __init__(self, name, track_reads=True):
        self.name = name
        self.w = None
        self.r = []
        self.track = track_reads


class DSem:
    def __init__(self, sem, key):
        self.sem, self.n, self.key = sem, 0, key


class Sched:
    ENGS = ("pe", "act", "dve", "pool", "sp")

    def __init__(self, nc, es):
        self.nc = nc
        self.es = es
        self.q = {e: [] for e in self.ENGS}
        self.sem = {e: es.enter_context(nc.semaphore("s_" + e)) for e in self.ENGS}
        self.cnt = {e: 0 for e in self.ENGS}
        self.seen = {e: {} for e in self.ENGS}
        self.nd = 0

    def dsem(self, name):
        self.nd += 1
        return DSem(self.es.enter_context(self.nc.semaphore("d_" + name)), "d%d" % self.nd)

    def _waits(self, eng, reads, writes, extra, is_dma=False):
        need = {}

        def add(t):
            if t is None:
                return
            if t.eng == eng and eng == "pe" and not is_dma:
                return
            if self.seen[eng].get(t.key, 0) >= t.val:
                return
            if need.get(t.key, (None, 0))[1] < t.val:
                need[t.key] = (t.sem, t.val)

        for r in reads:
            add(r.w)
        for w in writes:
            add(w.w)
            for t in w.r:
                add(t)
        for t in extra:
            add(t)
        for k, (s, v) in need.items():
            self.seen[eng][k] = v
        return list(need.values())

    def _commit(self, tok, reads, writes):
        for w in writes:
            w.w = tok
            w.r = []
        for r in reads:
            if r.track:
                r.r.append(tok)

    def op(self, eng, fn, reads=(), writes=(), extra=()):
        waits = self._waits(eng, reads, writes, extra)
        self.cnt[eng] += 1
        tok = Tok(self.sem[eng], self.cnt[eng], eng, eng)
        self.q[eng].append((waits, fn, (self.sem[eng], 1)))
        self._commit(tok, reads, writes)
        return tok

    def dma(self, eng, fn, ds, reads=(), writes=(), extra=()):
        waits = self._waits(eng, reads, writes, extra, is_dma=True)
        ds.n += 1
        tok = Tok(ds.sem, ds.n * 16, "dma", ds.key)
        self.q[eng].append((waits, fn, (ds.sem, 16)))
        self._commit(tok, reads, writes)
        return tok

    def wait_only(self, eng, toks):
        waits = self._waits(eng, (), (), toks)
        self.q[eng].append((waits, None, None))

    def replay(self, eng, e):
        for waits, fn, inc in self.q[eng]:
            for s, v in waits:
                e.wait_ge(s, v)
            if fn is not None:
                ins = fn(e)
                ins.then_inc(inc[0], inc[1])


def build_nc(debug=False):
    nc = bass.Bass("TRN2", target_bir_lowering=False)

    def din(name, shape):
        return nc.dram_tensor(name, list(shape), F32, kind="ExternalInput").ap()

    xp = din("xp", [2048, D])
    xpT = din("xpT", [D, SEQ])
    xhT = din("xhT", [D, 256])
    cT_d = din("cT", [128, 8])
    w_ada = din("w_ada", [D, 3 * D])
    b_adaT_d = din("b_adaT", [128, 16])
    b_gate_d = din("b_gate", [1, D])
    w_in = din("w_in", [D, 3080])
    b_qT_d = din("b_qT", [128, 4])
    b_kT_d = din("b_kT", [128, 4])
    b_gT_d = din("b_gT", [128, 8])
    b_vfp_d = din("b_vfp", [1, 1032])
    w_pm_d = din("w_pm", [4, 128, 128])
    b_pmT_d = din("b_pmT", [128, 4])
    pscT_d = din("pscT", [128, 4])
    w_out_d = din("w_out", [D, D])
    b_out_d = din("b_out", [1, D])
    ln_g_d = din("ln_g", [1, D])
    ln_b_d = din("ln_b", [1, D])
    ident_d = din("ident", [128, 128])
    ones_d = din("ones", [128, 128])
    U_d = din("U", [128, 128])
    pred_d = din("pred", [32, 32])
    masks_d = din("masks", [128, 2, 128])
    bm_d = din("bm", [128, 8, 128])
    bh_d = din("bh", [128, 36, 16])
    out_d = nc.dram_tensor("out", [2048, D], F32, kind="ExternalOutput").ap()

    wbf = nc.dram_tensor("wbf", [4, 128, 8, 512], BF16, kind="Internal").ap()
    xpT_v = xpT.rearrange("(kc p) t -> p kc t", p=128)
    xhT_v = xhT.rearrange("(kc p) t -> p kc t", p=128)
    w_ada_v = w_ada.rearrange("(kc p) e -> p kc e", p=128)
    w_in_v = w_in.rearrange("(kc p) e -> p kc e", p=128)
    w_out_v = w_out_d.rearrange("(kc p) e -> p kc e", p=128)

    with ExitStack() as es:
        S = Sched(nc, es)

        def sb(name, shape, dt=F32):
            return es.enter_context(nc.sbuf_tensor("sb_" + name, list(shape), dt))

        banks = [es.enter_context(nc.psum_tensor("ps%d" % i, [128, 512], F32)) for i in range(8)]
        bres = [Res("bank%d" % i) for i in range(8)]

        nlf = sb("nlf", [128, NBLK, 8])
        Cpos = sb("Cpos", [128, NBLK, 8])
        biasG = [sb("biasG%d" % i, [128, NBLK, 8]) for i in range(2)]
        ident = sb("ident", [128, 128])
        onesf = sb("onesf", [128, 128])
        Uf = sb("Uf", [128, 128])
        predf = sb("predf", [32, 32])
        tot = sb("tot", [32, 8])
        Z = sb("Z", [32, 256])
        identb = sb("identb", [128, 128], BF16)
        masks = sb("masks", [128, 2, 128], BF16)
        bm = sb("bm", [128, 8, 128], BF16)
        bh = sb("bh", [128, 36, 16], BF16)
        gate_bc = sb("gate_bc", [128, D])
        gb = sb("gb", [128, D])
        lng = sb("lng", [128, D])
        lnb = sb("lnb", [128, D])
        bvfp = sb("bvfp", [128, 1032])
        adaT = sb("adaT", [128, 16])
        scale1 = sb("scale1", [128, 8])
        b_adaT = sb("b_adaT", [128, 16])
        b_qT = sb("b_qT", [128, 4])
        b_kT = sb("b_kT", [128, 4])
        b_gT = sb("b_gT", [128, 8])
        b_pmT = sb("b_pmT", [128, 4])
        pscT = sb("pscT", [128, 4])
        cT = sb("cT", [128, 8])
        sc = sb("sc", [128, 8])
        phalo = sb("phalo", [128, 2, 512], BF16)
        wpm = sb("wpm", [128, 4, 128], BF16)
        NXB = 4
        xb = [sb("xb%d" % i, [128, D]) for i in range(NXB)]
        uT = [sb("uT%d" % i, [128, 8, 512], BF16) for i in range(2)]
        wA = sb("wA", [128, 8, 1032], BF16)
        wS = [sb("wS%d" % i, [128, 8, 512], BF16) for i in range(2)]
        QT = sb("QT", [128, 8, 512], BF16)
        gatt = sb("gatt", [128, 4, 512], BF16)
        scr = sb("scr", [128, 12, 512], BF16)
        yT = sb("yT", [128, 4, 512], BF16)
        gatt1 = sb("gatt1", [128, 4, 512], BF16)
        stats2 = [sb("stats2_%d" % i, [128, 2, 6]) for i in range(2)]
        mv2 = [sb("mv2_%d" % i, [128, 2]) for i in range(2)]
        ve2 = [sb("ve2_%d" % i, [128, 1]) for i in range(2)]
        rstd2 = [sb("rstd2_%d" % i, [128, 1]) for i in range(2)]
        mhalf = sb("mhalf", [128, 1])
        r_sm = [Res("sm0"), Res("sm1")]
        nbg = sb("nbg", [128, 8])
        tmpf = sb("tmpf", [128, 512])
        tmpf2 = sb("tmpf2", [128, 512])
        hb = [sb("hb%d" % i, [128, D]) for i in range(2)]
        rl = sb("rl", [128, 4])
        fl = sb("fl", [128, 4, 8])
        ex = sb("ex", [128, 4, 8])

        r_KT = Res("KT", False)
        r_V = Res("V", False)
        r_nlf = Res("nlf")
        r_Cpos = Res("Cpos", False)
        r_biasG = [Res("biasG0"), Res("biasG1")]
        r_const = Res("const", False)
        r_constb = Res("constb", False)
        r_ada = Res("ada", False)
        r_gate = Res("gate", False)
        r_gb = Res("gb", False)
        r_sc = Res("sc", False)
        r_xb = [Res("xb%d" % i) for i in range(NXB)]
        r_uT = [Res("uT0"), Res("uT1")]
        r_wA = Res("wA")
        r_wS = [Res("wS0"), Res("wS1")]
        r_QT = Res("QT")
        r_gatt = Res("gatt")
        r_scr = [Res("scr%d" % i) for i in range(12)]
        r_yT = [Res("yT%d" % i) for i in range(4)]
        r_small2 = Res("small2", False)
        r_tmpf = Res("tmpf")
        r_tmpf2 = Res("tmpf2")
        tmpfs = [tmpf, tmpf2]
        r_tmpfs = [r_tmpf, r_tmpf2]
        r_hb = [Res("hb0"), Res("hb1")]
        r_small = Res("small")
        r_rl = Res("rl")
        r_phalo = Res("phalo", False)
        r_fl = Res("fl")
        r_tot = Res("tot")
        r_Z = Res("Z")

        d_const = S.dsem("const")
        d_constb = S.dsem("constb")
        d_xb = [S.dsem("xb%d" % i) for i in range(NXB)]
        d_wa = [S.dsem("wa%d" % i) for i in range(3)]
        d_wA = S.dsem("wA")
        d_wS = [S.dsem("wS0"), S.dsem("wS1")]
        d_out = [S.dsem("out%d" % i) for i in range(4)]
        d_tmp = S.dsem("tmp")

        def cload(dst, src, eng="act"):
            r_const.w = S.dma(eng, lambda e, dst=dst, src=src: e.dma_start(out=dst, in_=src), d_const)

        r_c0 = Res("c0", False)
        d_c0 = S.dsem("c0")
        S.dma("act", lambda e: e.dma_start(out=cT[:, :], in_=cT_d[:, :]), d_c0)
        r_c0.w = S.dma("act", lambda e: e.dma_start(out=b_adaT[:, :], in_=b_adaT_d[:, :]), d_c0)
        S.op("act", lambda e: e.activation(sc[:, :], cT[:, :], AF.Silu), reads=[r_c0], writes=[r_sc])
        cload(ident[:, :], ident_d[:, :])
        cload(onesf[:, :], ones_d[:, :])
        cload(Uf[:, :], U_d[:, :])
        cload(predf[:, :], pred_d[:, :])
        cload(b_qT[:, :], b_qT_d[:, :])
        cload(b_kT[:, :], b_kT_d[:, :])
        cload(b_gT[:, :], b_gT_d[:, :])
        cload(b_pmT[:, :], b_pmT_d[:, :])
        cload(pscT[:, :], pscT_d[:, :])
        cload(bvfp[:, :], b_vfp_d.partition_broadcast(128))
        cload(lng[:, :], ln_g_d.partition_broadcast(128))
        cload(lnb[:, :], ln_b_d.partition_broadcast(128))
        cload(gb[:, :], b_out_d.partition_broadcast(128))
        cload(gate_bc[:, :], b_gate_d.partition_broadcast(128))

        def cloadb(dst, src):
            r_constb.w = S.dma("pool", lambda e, dst=dst, src=src: e.dma_start(out=dst, in_=src), d_constb)

        cloadb(identb[:, :], ident_d[:, :])
        cloadb(masks[:, :, :], masks_d[:, :, :])
        cloadb(bm[:, :, :], bm_d[:, :, :])
        cloadb(bh[:, :, :], bh_d[:, :, :])
        cloadb(wpm[:, :, :], w_pm_d.rearrange("g c e -> c g e"))
        for half in range(2):
            r_wA.w = S.dma("pool", lambda e, half=half: e.dma_start(out=wA[:, half * 4:(half + 1) * 4, :],
                                                                   in_=w_in_v[:, half * 4:(half + 1) * 4, 512:1544]),
                           d_wA)

        S.op("dve", lambda e: e.memset(mhalf[:, :], -0.5), writes=[r_small])
        r_wbf = [Res("wbf%d" % i, False) for i in range(4)]
        d_wbf = [S.dsem("wbf%d" % i) for i in range(4)]
        for hh in range(2):
            S.op("pool", lambda e, hh=hh: e.memset(QT[(1 - hh) * 64:(2 - hh) * 64, hh:8:2, :], 0.0), writes=[r_QT])

        with ExitStack() as es2:
            wa = [es2.enter_context(nc.sbuf_tensor("wa%d" % i, [128, 8, 512], F32)) for i in range(3)]
            scbc = es2.enter_context(nc.sbuf_tensor("scbc", [128, 8, 128], F32))
            r_wa = [Res("wa%d" % i) for i in range(3)]
            r_scbc = Res("scbc")

            S.op("dve", lambda e: e.tensor_copy(scbc[:, :, :], sc[:, :].unsqueeze(2).to_broadcast([128, 8, 128])),
                 reads=[r_sc], writes=[r_scbc])
            psA = banks[7]
            for j in range(6):
                buf = wa[j % 3]
                S.dma("sp", lambda e, buf=buf, j=j: e.dma_start(out=buf[:, :, :], in_=w_ada_v[:, :, j * 512:(j + 1) * 512]),
                      d_wa[j % 3], writes=[r_wa[j % 3]])
                if j < 4:
                    def mm(e, buf=buf, j=j):
                        ins = None
                        for i in range(4):
                            fc = j * 4 + i
                            for kc in range(8):
                                ins = e.matmul(psA[:, fc:fc + 1], buf[:, kc, i * 128:(i + 1) * 128], sc[:, kc:kc + 1],
                                               start=(kc == 0), stop=(kc == 7))
                        return ins
                    S.op("pe", mm, reads=[r_wa[j % 3], r_sc], writes=[bres[7]])
                else:
                    bk = 5 + (j - 4)

                    def mm(e, buf=buf, bk=bk):
                        ins = None
                        for kc in range(8):
                            ins = e.matmul(banks[bk][:, :], scbc[:, kc, :], buf[:, kc, :], start=(kc == 0), stop=(kc == 7))
                        return ins
                    S.op("pe", mm, reads=[r_wa[j % 3], r_scbc], writes=[bres[bk]])
                    half = j - 4
                    S.op("dve", lambda e, bk=bk, half=half: e.tensor_tensor(
                        gate_bc[:, half * 512:(half + 1) * 512], banks[bk][:, :],
                        gate_bc[:, half * 512:(half + 1) * 512], ALU.add),
                        reads=[bres[bk], r_const], writes=[r_gate])
                if j == 3:
                    S.op("dve", lambda e: e.tensor_tensor(adaT[:, :], psA[:, 0:16], b_adaT[:, :], ALU.add),
                         reads=[bres[7], r_c0], writes=[r_ada])
                    S.op("dve", lambda e: e.tensor_scalar_add(scale1[:, :], adaT[:, 8:16], 1.0), writes=[r_ada])
            S.op("pool", lambda e: e.tensor_tensor(gb[:, :], gb[:, :], gate_bc[:, :], ALU.mult),
                 reads=[r_gate, r_const], writes=[r_gb])
            t_end_ada = S.op("pe", lambda e: e.matmul(banks[7][:, 0:1], onesf[:, 0:128], onesf[:, 0:1], start=True, stop=True),
                             reads=[r_const], writes=[bres[7]])

        shiftT = adaT
        KT = sb("KT", [128, 4, SEQ], BF16)
        V = sb("V", [128, NBLK, 8, 66], BF16)
        S.op("pool", lambda e: e.memset(V[:, :, :, 64], 1.0), writes=[r_V], extra=[t_end_ada])

        xb_rot = [0]

        def load_x(src_rows, q="sp"):
            i = xb_rot[0] % NXB
            xb_rot[0] += 1
            S.dma(q, lambda e, i=i, src_rows=src_rows: e.dma_start(out=xb[i][:, :], in_=src_rows), d_xb[i],
                  writes=[r_xb[i]])
            return i

        def load_feat(src_v, tok0, ntok, q="sp"):
            per = 1024 // ntok
            bufs = []
            for kc0 in range(0, 8, per):
                i = xb_rot[0] % NXB
                xb_rot[0] += 1
                S.dma(q, lambda e, i=i, kc0=kc0: e.dma_start(
                    out=xb[i][:, :].rearrange("p (a b) -> p a b", b=ntok), in_=src_v[:, kc0:kc0 + per, tok0:tok0 + ntok]),
                    d_xb[i], writes=[r_xb[i]])
                bufs.append(i)
            return bufs

        def modulate_sb(kc, bufs, ntok, ut, eng):
            per = 1024 // ntok
            xi = bufs[kc // per]
            off = (kc % per) * ntok
            if eng == "act":
                S.op("act", lambda e: e.activation(uT[ut][:, kc, 0:ntok], xb[xi][:, off:off + ntok], AF.Identity,
                                                   bias=shiftT[:, kc:kc + 1], scale=scale1[:, kc:kc + 1]),
                     reads=[r_xb[xi], r_ada], writes=[r_uT[ut]])
            else:
                S.op(eng, lambda e: e.tensor_scalar(uT[ut][:, kc, 0:ntok], xb[xi][:, off:off + ntok],
                                                    scale1[:, kc:kc + 1], shiftT[:, kc:kc + 1], ALU.mult, ALU.add),
                     reads=[r_xb[xi], r_ada], writes=[r_uT[ut]])

        ENG_MIX = ["dve", "act", "pool", "act", "dve", "act", "pool", "act"]

        rotT = [0]
        rotKV = [0]
        KVB = [0, 1, 2, 3, 4, 5, 7]
        def emit_T(ch):
            bufs = load_feat(xpT_v, ch * 512, 512, "act" if ch == 0 else "sp")
            for kc in range(8):
                modulate_sb(kc, bufs, 512, ch % 2, "act" if ch == 0 else ENG_MIX[kc])

        def emit_K(ch):
            ut = ch % 2
            for pair in range(4):
                bank = KVB[rotKV[0] % len(KVB)]
                rotKV[0] += 1

                def mm(e, pair=pair, bank=bank, ut=ut):
                    ins = None
                    for kc in range(8):
                        ins = e.matmul(banks[bank][:, :], wA[:, kc, pair * 128:(pair + 1) * 128], uT[ut][:, kc, :],
                                       start=(kc == 0), stop=(kc == 7))
                    return ins
                S.op("pe", mm, reads=[r_wA, r_uT[ut]], writes=[bres[bank]])
                S.op("act", lambda e, pair=pair, bank=bank, ch=ch: e.activation(
                    KT[:, pair, ch * 512:(ch + 1) * 512], banks[bank][:, :], AF.Identity, bias=b_kT[:, pair:pair + 1], scale=1.0),
                    reads=[bres[bank], r_const], writes=[r_KT], extra=[t_end_ada])

        def emit_V(ch):
            ut = ch % 2
            for bi in range(4):
                pos = ch * 4 + bi
                bank = KVB[rotKV[0] % len(KVB)]
                rotKV[0] += 1

                def mm(e, bi=bi, bank=bank, ut=ut):
                    ins = None
                    for kc in range(8):
                        ins = e.matmul(banks[bank][:, :], uT[ut][:, kc, bi * 128:(bi + 1) * 128], wA[:, kc, 512:1024],
                                       start=(kc == 0), stop=(kc == 7))
                    return ins
                S.op("pe", mm, reads=[r_wA, r_uT[ut]], writes=[bres[bank]])
                S.op("dve", lambda e, pos=pos, bank=bank: e.tensor_tensor(
                    V[:, pos, :, 0:64], banks[bank][:, :].rearrange("p (h d) -> p h d", d=64),
                    bvfp[:, 0:512].rearrange("p (h d) -> p h d", d=64), ALU.add),
                    reads=[bres[bank], r_const], writes=[r_V], extra=[t_end_ada])

            def mmf(e, ut=ut):
                ins = None
                for bi in range(4):
                    for kc in range(8):
                        ins = e.matmul(banks[6][:, bi * 8:(bi + 1) * 8], uT[ut][:, kc, bi * 128:(bi + 1) * 128],
                                       wA[:, kc, 1024:1032], start=(kc == 0), stop=(kc == 7))
                return ins
            S.op("pe", mmf, reads=[r_wA, r_uT[ut]], writes=[bres[6]])
            S.op("dve", lambda e: e.tensor_tensor(
                fl[:, :, :], banks[6][:, 0:32].rearrange("p (b h) -> p b h", h=8),
                bvfp[:, 512:520].unsqueeze(1).to_broadcast([128, 4, 8]), ALU.add),
                reads=[bres[6], r_const], writes=[r_fl])
            S.op("act", lambda e: e.activation(ex[:, :, :], fl[:, :, :], AF.Exp, scale=-1.0), reads=[r_fl], writes=[r_small])
            S.op("act", lambda e, ch=ch: e.activation(nlf[:, ch * 4:(ch + 1) * 4, :], ex[:, :, :], AF.Ln, bias=1.0, scale=1.0),
                 reads=[r_small], writes=[r_nlf, r_fl])

        emit_T(0)
        for ch in range(8):
            emit_K(ch)
            if ch == 2:
                for c0, pc in ((0, 0), (2056, 1), (1544, 3), (2568, 2)):
                    S.dma("pool", lambda e, c0=c0, pc=pc: e.dma_start(out=wbf[pc, :, :, :], in_=w_in_v[:, :, c0:c0 + 512]),
                          d_wbf[pc], writes=[r_wbf[pc]], extra=[r_KT.w])
            if ch + 1 < 8:
                emit_T(ch + 1)
            emit_V(ch)

        def mmtot(e):
            ins = None
            for h in range(8):
                ins = e.matmul(banks[0][0:32, 256 + h:257 + h], nlf[:, :, h], onesf[:, 0:1], start=True, stop=True)
            return ins
        def cum1():
            S.op("pe", mmtot, reads=[r_nlf, r_const], writes=[bres[0]])
            S.op("dve", lambda e: e.tensor_copy(tot[:, :], banks[0][0:32, 256:264]), reads=[bres[0]], writes=[r_tot])
            S.op("dve", lambda e: e.tensor_tensor(
                Z[:, :].rearrange("p (b h) -> p b h", h=8), tot[:, :].unsqueeze(1).to_broadcast([32, 32, 8]),
                predf[:, :].unsqueeze(2).to_broadcast([32, 32, 8]), ALU.mult), reads=[r_tot, r_const], writes=[r_Z])

        def mmcum(e):
            e.matmul(banks[0][:, 0:256], Uf[:, :], nlf[:, :, :].rearrange("p b h -> p (b h)"), start=True, stop=False)
            return e.matmul(banks[0][:, 0:256], onesf[0:32, :], Z[:, :], start=False, stop=True)
        def cum2():
            S.op("pe", mmcum, reads=[r_nlf, r_Z, r_const], writes=[bres[0]])
            S.op("dve", lambda e: e.tensor_copy(Cpos[:, :, :].rearrange("p b h -> p (b h)"), banks[0][:, 0:256]),
                 reads=[bres[0]], writes=[r_Cpos])

        SB_ = [0, 1, 2, 3]
        OB_ = [4, 5]
        MB_ = [6, 7]
        rotM = [0]
        rotW = [0]

        def mbank():
            b = MB_[rotM[0] % len(MB_)]
            rotM[0] += 1
            return b

        PIECE = {0: 0, 2056: 1, 2568: 2, 1544: 3}

        def load_ws(c0, first=False):
            i = rotW[0] % 2
            rotW[0] += 1
            pc = PIECE[c0]
            S.dma("pool", lambda e, i=i, pc=pc: e.dma_start(out=wS[i][:, :, :], in_=wbf[pc, :, :, :]),
                  d_wS[i], reads=[r_wbf[pc]], writes=[r_wS[i]])
            return i

        PCOL = 1544
        QCOL = 0
        GACOL = 2056
        GPCOL = 2568

        On = hb[0][:, :].bitcast(BF16).rearrange("p (a b) -> p a b", b=512)
        PTv = hb[1][:, :].bitcast(BF16).rearrange("p (a b) -> p a b", b=512)
        r_On = Res("On")
        r_pt = [Res("pt%d" % i) for i in range(4)]
        QTb = [QT, uT[1]]
        r_QTb = [r_QT, r_uT[1]]
        gattb = [gatt, gatt1]
        r_gattb = [r_gatt, Res("gatt1")]
        nb_gT = nbg
        S.op("dve", lambda e: e.tensor_scalar(nb_gT[:, :], b_gT[:, :], 0.5, None, ALU.mult), reads=[r_const], writes=[r_small2])

        tm_rot = [0]

        def silu_evac(bank, c8, out_ap, out_res):
            ti = tm_rot[0] % 2
            tm_rot[0] += 1
            tm, r_tm = tmpfs[ti], r_tmpfs[ti]
            S.op("act", lambda e: e.activation(tm[:, :], banks[bank][:, :], AF.Tanh, bias=nb_gT[:, c8:c8 + 1], scale=0.5),
                 reads=[bres[bank], r_small2], writes=[r_tm])
            S.op("pool", lambda e: e.tensor_scalar(tm[:, :], tm[:, :], 0.5, 0.5, ALU.mult, ALU.add), reads=[r_tm], writes=[r_tm])
            S.op("dve", lambda e: e.scalar_tensor_tensor(out_ap, banks[bank][:, :], b_gT[:, c8:c8 + 1], tm[:, :], ALU.add, ALU.mult),
                 reads=[bres[bank], r_tm, r_const], writes=out_res)

        def group2(lhs_fn, rhs_fn, rd, bank):
            for part in range(2):
                def mm(e, part=part):
                    ins = None
                    for kc in range(part * 4, part * 4 + 4):
                        ins = e.matmul(banks[bank][:, :], lhs_fn(kc), rhs_fn(kc), start=(kc == 0), stop=(kc == 7))
                    return ins
                S.op("pe", mm, reads=rd, writes=[bres[bank]])
                yield

        mods_done = [False]
        GORD = [3, 2, 1, 0]

        def emit_biasG(G):
            bG = G % 2
            bank = mbank()
            S.op("pe", lambda e, bank=bank, G=G: e.matmul(banks[bank][:, 0:8], onesf[:, :], Cpos[:, 4 * G + 2, :], start=True, stop=True),
                 reads=[r_Cpos, r_const], writes=[bres[bank]])
            S.op("dve", lambda e, bank=bank, bG=bG: e.scalar_tensor_tensor(
                biasG[bG][:, :, :], banks[bank][:, 0:8].unsqueeze(1).to_broadcast([128, NBLK, 8]), -1.0 / 128.0,
                Cpos[:, :, :], ALU.mult, ALU.add), reads=[bres[bank], r_Cpos], writes=[r_biasG[bG]])

        def prelude(G, overlapped, n_sp=14):
            qb = (G + 1) % 2
            X, Y = 4 * ((G + 2) % 3), 4 * (G % 3)
            wq = load_ws(QCOL, G == 0)
            wga = load_ws(GACOL, G == 0)
            bufs = load_feat(xpT_v, G * 512, 512)
            for _ in range(n_sp):
                yield
            for kc in range(8):
                modulate_sb(kc, bufs, 512, 0, "pool" if overlapped else ENG_MIX[kc])
                if kc % 2 == 1:
                    yield
            mods_done[0] = True
            if G == GORD[0]:
                hbufs = load_feat(xhT_v, 0, 256)
                for kc in range(8):
                    modulate_sb(kc, hbufs, 256, 1, ENG_MIX[kc])
            if G == GORD[1]:
                for hh in range(2):
                    S.op("pool", lambda e, hh=hh: e.memset(uT[1][(1 - hh) * 64:(2 - hh) * 64, hh:8:2, :], 0.0), writes=[r_uT[1]])
            for c in range(4):
                bank = mbank()
                for _ in group2(lambda kc, c=c: wS[wq][:, kc, c * 128:(c + 1) * 128], lambda kc: uT[0][:, kc, :],
                                  [r_wS[wq], r_uT[0]], bank):
                    pass
                for hh in range(2):
                    S.op("dve", lambda e, hh=hh, c=c, bank=bank: e.tensor_scalar(
                        QTb[qb][hh * 64:(hh + 1) * 64, 2 * c + hh, :], banks[bank][hh * 64:(hh + 1) * 64, :],
                        b_qT[hh * 64:(hh + 1) * 64, c:c + 1], None, ALU.add),
                        reads=[bres[bank], r_const], writes=[r_QTb[qb]])
                yield
            wp = load_ws(PCOL, G == 0)
            for c in range(4):
                bank = mbank()
                for _ in group2(lambda kc, c=c: wS[wga][:, kc, c * 128:(c + 1) * 128], lambda kc: uT[0][:, kc, :],
                                  [r_wS[wga], r_uT[0]], bank):
                    pass
                silu_evac(bank, c, gattb[qb][:, c, :], [r_gattb[qb]])
                yield
            wgp = load_ws(GPCOL, G == 0)
            for mi in range(4):
                bank = mbank()
                for _ in group2(lambda kc, mi=mi: uT[0][:, kc, mi * 128:(mi + 1) * 128], lambda kc: wS[wp][:, kc, :],
                                  [r_wS[wp], r_uT[0]], bank):
                    pass
                S.op("dve", lambda e, mi=mi, bank=bank: e.tensor_tensor(scr[:, X + mi, :], banks[bank][:, :], bvfp[:, 520:1032], ALU.add),
                     reads=[bres[bank], r_const], writes=[r_scr[X + mi]])
                yield
            if G == GORD[0]:
                for hbk in range(2):
                    bank = mbank()
                    for _ in group2(lambda kc, hbk=hbk: uT[1][:, kc, hbk * 128:(hbk + 1) * 128], lambda kc: wS[wp][:, kc, :],
                                    [r_wS[wp], r_uT[1]], bank):
                        pass
                    S.op("dve", lambda e, hbk=hbk, bank=bank: e.tensor_tensor(phalo[:, hbk, :], banks[bank][:, :], bvfp[:, 520:1032], ALU.add),
                         reads=[bres[bank], r_const], writes=[r_phalo])
            for mi in range(4):
                m = 4 * G + mi
                bank = mbank()

                def mm(e, mi=mi, m=m, bank=bank):
                    ins = None
                    for g in range(4):
                        bmi = g if m > 0 else 4 + g
                        bhi = ((m % 8) * 4 + g) if m > 0 else 32 + g
                        e.matmul(banks[bank][:, g * 128:(g + 1) * 128], scr[:, X + mi, g * 128:(g + 1) * 128], bm[:, bmi, :],
                                 start=True, stop=False)
                        ins = e.matmul(banks[bank][:, g * 128:g * 128 + 16], phalo[:, m // 8, g * 128:(g + 1) * 128],
                                       bh[:, bhi, :], start=False, stop=True)
                    return ins
                S.op("pe", mm, reads=[r_scr[X + mi], r_phalo, r_constb], writes=[bres[bank]])
                S.op("dve", lambda e, mi=mi, bank=bank: e.tensor_copy(
                    scr[:, Y:Y + 4, mi * 128:(mi + 1) * 128], banks[bank][:, :].rearrange("p (g t) -> p g t", t=128)),
                    reads=[bres[bank]], writes=[r_scr[Y + g] for g in range(4)])
                yield
            for c in range(4):
                bank = mbank()
                for _ in group2(lambda kc, c=c: wS[wgp][:, kc, c * 128:(c + 1) * 128], lambda kc: uT[0][:, kc, :],
                                  [r_wS[wgp], r_uT[0]], bank):
                    pass
                silu_evac(bank, 4 + c, scr[:, X + c, :], [r_scr[X + c]])
                yield
            for g in range(4):
                bank = mbank()
                S.op("pe", lambda e, g=g, bank=bank: e.matmul(banks[bank][:, :], wpm[:, g, :], scr[:, Y + g, :], start=True, stop=True),
                     reads=[r_scr[Y + g], r_constb], writes=[bres[bank]])
                S.op("dve", lambda e, g=g, bank=bank: e.tensor_scalar(
                    tmpf[:, :], banks[bank][:, :], b_pmT[:, g:g + 1], pscT[:, g:g + 1], ALU.add, ALU.mult),
                    reads=[bres[bank], r_const], writes=[r_tmpf])
                S.op("pool", lambda e, g=g: e.tensor_tensor(scr[:, Y + g, :], tmpf[:, :], scr[:, X + g, :], ALU.mult),
                     reads=[r_tmpf, r_scr[X + g]], writes=[r_scr[Y + g]])
                yield
            if overlapped:
                emit_biasG(G)

        pg = prelude(GORD[0], False, 0)
        for _ in range(4):
            next(pg, None)
        cum1()
        for _ in range(4):
            next(pg, None)
        cum2()


        epg = [None]
        for gi, G in enumerate(GORD):
            Gn = GORD[gi + 1] if gi + 1 < len(GORD) else None
            qb = (G + 1) % 2
            Y = 4 * (G % 3)
            nit_g = 8 * (8 * G + 8)
            sp_items = 40 if gi == 0 else 45
            stride = max(1, nit_g // (50 + sp_items))
            if gi == 0:
                sp_items, stride = 24, 4
            gen = prelude(Gn, True, -(-sp_items // stride)) if Gn is not None else None
            if gi == 0:
                for half in range(2):
                    S.dma("pool", lambda e, half=half: e.dma_start(out=wA[:, half * 4:(half + 1) * 4, 0:1024],
                                                                  in_=w_out_v[:, half * 4:(half + 1) * 4, :]),
                          d_wA, writes=[r_wA])
            bG = G % 2
            if gi == 0:
                emit_biasG(G)

            nk = 4 * G + 4
            kblocks = [(i, i) for i in range(nk)] + [(16 + i, i) for i in range(nk)]
            items = []
            for h in range(8):
                for j, (pos, i) in enumerate(kblocks):
                    items.append((h, pos, i, j == 0, j == len(kblocks) - 1))
            LA = 3
            prev_ep = epg[0]
            first_rest = pg if gi == 0 else None
            ep_xi = None
            ep_stt = False
            ep_at = 0
            mods_done[0] = False
            nit = len(items)
            for idx in range(nit + LA):
                if idx < nit:
                    h, pos, i, first, last = items[idx]
                    pair, hh = h // 2, h % 2
                    own = pos < 16
                    col0 = max(0, i - 4 * G) * 128
                    masked = i >= 4 * G
                    sbk = SB_[idx % len(SB_)]
                    pti = idx % 4

                    def mms(e, pos=pos, col0=col0, masked=masked, sbk=sbk, own=own, pair=pair, h=h, qb=qb):
                        ins = e.matmul(banks[sbk][:, col0:512], KT[:, pair, pos * 128:(pos + 1) * 128],
                                       QTb[qb][:, h, col0:512], start=True, stop=(not masked))
                        if masked:
                            ins = e.matmul(banks[sbk][:, col0:col0 + 128], identb[:, :], masks[:, 0 if own else 1, :],
                                           start=False, stop=True)
                        return ins
                    S.op("pe", mms, reads=[r_KT, r_QTb[qb], r_constb], writes=[bres[sbk]])
                    S.op("act", lambda e, pos=pos, col0=col0, sbk=sbk, pti=pti, h=h, bG=bG: e.activation(
                        PTv[:, pti, col0:512], banks[sbk][:, col0:512], AF.Exp, bias=biasG[bG][:, pos, h:h + 1], scale=0.125),
                        reads=[bres[sbk], r_biasG[bG]], writes=[r_pt[pti]])
                if idx >= LA:
                    jdx = idx - LA
                    h, pos, i, first, last = items[jdx]
                    own = pos < 16
                    col0 = max(0, i - 4 * G) * 128
                    pti = jdx % 4
                    ob = OB_[h % 2]

                    def mmpv(e, pos=pos, col0=col0, pti=pti, h=h, ob=ob, first=first, i=i, own=own, G=G):
                        ins = None
                        for mi in range(col0 // 128, 4):
                            st = first and mi == 0
                            sp_ = (not own) and (i == 4 * G + mi)
                            ins = e.matmul(banks[ob][:, mi * 65:(mi + 1) * 65], PTv[:, pti, mi * 128:(mi + 1) * 128],
                                           V[:, pos, h, 0:65], start=st, stop=sp_, skip_group_check=True)
                        return ins
                    S.op("pe", mmpv, reads=[r_pt[pti], r_V], writes=[bres[ob]])
                    if last:
                        S.op("dve", lambda e, ob=ob: e.reciprocal(
                            rl[:, :], banks[ob][:, 0:260].rearrange("p (a b) -> p a b", b=65)[:, :, 64]),
                            reads=[bres[ob]], writes=[r_rl])
                        S.op("dve", lambda e, ob=ob, h=h: e.tensor_tensor(
                            On[:, :, h * 64:(h + 1) * 64], banks[ob][:, 0:260].rearrange("p (a b) -> p a b", b=65)[:, :, 0:64],
                            rl[:, :].unsqueeze(2).to_broadcast([128, 4, 64]), ALU.mult),
                            reads=[bres[ob], r_rl], writes=[r_On])
                if prev_ep is not None and idx % 2 == 1:
                    try:
                        next(prev_ep)
                    except StopIteration:
                        prev_ep = None
                if prev_ep is None and idx % stride == stride - 1:
                    if first_rest is not None:
                        try:
                            next(first_rest)
                        except StopIteration:
                            first_rest = None
                    elif gen is not None:
                        next(gen, None)
                if ep_xi is None and idx >= nit - 40 and prev_ep is None and (gen is None or mods_done[0]):
                    ep_xi = [load_x(xp[(4 * G + mi) * 128:(4 * G + mi + 1) * 128, :]) for mi in range(4)]
                    ep_at = idx
                if ep_xi is not None and not ep_stt and idx >= max(nit - 20, ep_at + 10):
                    ep_stt = True
                    for xi in ep_xi:
                        S.op("dve", lambda e, xi=xi: e.scalar_tensor_tensor(xb[xi][:, :], xb[xi][:, :], ALPHA, gb[:, :], ALU.mult, ALU.add),
                             reads=[r_gb], writes=[r_xb[xi]])
            if first_rest is not None:
                for _ in first_rest:
                    pass
            if gen is not None:
                for _ in gen:
                    pass
            if prev_ep is not None:
                for _ in prev_ep:
                    pass
            if ep_xi is None:
                ep_xi = [load_x(xp[(4 * G + mi) * 128:(4 * G + mi + 1) * 128, :]) for mi in range(4)]
            if not ep_stt:
                for xi in ep_xi:
                    S.op("dve", lambda e, xi=xi: e.scalar_tensor_tensor(xb[xi][:, :], xb[xi][:, :], ALPHA, gb[:, :], ALU.mult, ALU.add),
                         reads=[r_gb], writes=[r_xb[xi]])
            for cpair in range(2):
                bank = mbank()
                bview = banks[bank][:, :].bitcast(BF16)

                def tr(e, cpair=cpair, bview=bview):
                    ins = None
                    for cc in range(2):
                        c = cpair * 2 + cc
                        for mi in range(4):
                            ins = e.transpose(bview[:, cc * 512 + mi * 128: cc * 512 + (mi + 1) * 128],
                                              On[:, mi, c * 128:(c + 1) * 128], identb[:, :])
                    return ins
                S.op("pe", tr, reads=[r_On, r_constb], writes=[bres[bank]])
                for cc in range(2):
                    c = cpair * 2 + cc
                    S.op("dve", lambda e, c=c, cc=cc, bview=bview, qb=qb: e.tensor_tensor(
                        yT[:, c, :], bview[:, cc * 512:(cc + 1) * 512], gattb[qb][:, c, :], ALU.mult),
                        reads=[bres[bank], r_gattb[qb]], writes=[r_yT[c]])
            def epilogue(G=G, Y=Y, ep_xi=ep_xi, last=(gi == len(GORD) - 1)):
                lb = [6, 7, 0, 1, 2, 3, 6, 7]
                for mi in range(4):
                    m = 4 * G + mi
                    xi = ep_xi[mi]
                    sp2 = m % 2
                    bks = [lb[2 * mi], lb[2 * mi + 1]] if last else [mbank(), mbank()]
                    for half in range(2):
                        def mm(e, half=half, mi=mi, bank=bks[half], Y=Y):
                            ins = None
                            for kc in range(8):
                                lhs = yT[:, kc, mi * 128:(mi + 1) * 128] if kc < 4 else scr[:, Y + kc - 4, mi * 128:(mi + 1) * 128]
                                ins = e.matmul(banks[bank][:, :], lhs, wA[:, kc, half * 512:(half + 1) * 512],
                                               start=(kc == 0), stop=(kc == 7))
                            return ins
                        S.op("pe", mm, reads=r_yT + [r_scr[Y + g] for g in range(4)] + [r_wA], writes=[bres[bks[half]]])
                        tk = tm_rot[0] % 2
                        tm_rot[0] += 1
                        S.op("dve", lambda e, half=half, bank=bks[half], tk=tk: e.tensor_tensor(
                            tmpfs[tk][:, :], banks[bank][:, :], gate_bc[:, half * 512:(half + 1) * 512], ALU.mult),
                            reads=[bres[bks[half]], r_gate], writes=[r_tmpfs[tk]])
                        S.op("dve", lambda e, half=half, xi=xi, tk=tk: e.tensor_tensor(
                            xb[xi][:, half * 512:(half + 1) * 512], xb[xi][:, half * 512:(half + 1) * 512], tmpfs[tk][:, :], ALU.add),
                            reads=[r_tmpfs[tk]], writes=[r_xb[xi]])
                        S.op("dve", lambda e, half=half, xi=xi, sp2=sp2: e.bn_stats(stats2[sp2][:, half, :], xb[xi][:, half * 512:(half + 1) * 512]),
                             reads=[r_xb[xi]], writes=[r_sm[sp2]])
                        yield
                    S.op("dve", lambda e, sp2=sp2: e.bn_aggr(mv2[sp2][:, :], stats2[sp2][:, :, :].rearrange("p a b -> p (a b)")),
                         writes=[r_sm[sp2]])
                    S.op("pool", lambda e, sp2=sp2: e.tensor_scalar(ve2[sp2][:, :], mv2[sp2][:, 1:2], EPS, 0.0, ALU.add, ALU.add),
                         reads=[r_sm[sp2]], writes=[r_sm[sp2]])
                    S.op("pool", lambda e, sp2=sp2: e.tensor_tensor(rstd2[sp2][:, :], ve2[sp2][:, :], mhalf[:, :], ALU.pow),
                         reads=[r_small], writes=[r_sm[sp2]])
                    S.op("dve", lambda e, xi=xi, sp2=sp2: e.tensor_scalar(xb[xi][:, :], xb[xi][:, :], mv2[sp2][:, 0:1], rstd2[sp2][:, 0:1],
                                                                         ALU.subtract, ALU.mult),
                         reads=[r_sm[sp2]], writes=[r_xb[xi], r_sm[sp2]])
                    S.op("pool", lambda e, xi=xi: e.tensor_tensor(xb[xi][:, :], xb[xi][:, :], lng[:, :], ALU.mult),
                         reads=[r_const], writes=[r_xb[xi]])
                    S.op("pool", lambda e, xi=xi: e.tensor_tensor(xb[xi][:, :], xb[xi][:, :], lnb[:, :], ALU.add),
                         reads=[r_const], writes=[r_xb[xi]])
                    S.dma("pool", lambda e, xi=xi, m=m: e.dma_start(out=out_d[m * 128:(m + 1) * 128, :], in_=xb[xi][:, :]),
                          d_out[xi], reads=[r_xb[xi]])
                    yield

            epg[0] = epilogue()
            if gi == len(GORD) - 1:
                for _ in epg[0]:
                    pass
        S.wait_only("pool", [Tok(d.sem, d.n * 16, "dma", d.key) for d in d_out])
        if debug:
            S.wait_only("sp", [Tok(S.sem[en], S.cnt[en], en, en) for en in ("pe", "act", "dve", "pool")]
                        + [Tok(d.sem, d.n * 16, "dma", d.key) for d in d_out])
            dumps = {"adaT": (adaT, F32), "scale1": (scale1, F32), "gate_bc": (gate_bc, F32), "gb": (gb, F32),
                     "KT": (KT, BF16), "V": (V, BF16), "Cpos": (Cpos, F32), "nlf": (nlf, F32), "biasG1": (biasG[1], F32),
                     "QT": (QT, BF16), "gatt": (gatt, BF16), "yT": (yT, BF16), "scr": (scr, BF16), "gatt1": (gatt1, BF16), "phalo": (phalo, BF16),
                     "wA": (wA, BF16), "hb1": (hb[1], F32), "uT1": (uT[1], BF16), "uT0": (uT[0], BF16), "biasG0": (biasG[0], F32), "sc": (sc, F32), "tot": (tot, F32),
                     "Z": (Z, F32), "bvfp": (bvfp, F32), "masks": (masks, BF16), "bm": (bm, BF16)}
            for nm, (tl, dt) in dumps.items():
                shp = list(tl.shape)
                dd = nc.dram_tensor("dbg_" + nm, shp, dt, kind="ExternalOutput").ap()
                full = tuple(slice(None) for _ in shp)
                S.dma("sp", lambda e, dd=dd, tl=tl, full=full: e.dma_start(out=dd[full], in_=tl[full]), d_tmp)
            S.wait_only("sp", [Tok(d_tmp.sem, d_tmp.n * 16, "dma", d_tmp.key)])

        with nc.Block() as block:
            @block.tensor
            def _(e):
                S.replay("pe", e)

            @block.scalar
            def _(e):
                S.replay("act", e)

            @block.vector
            def _(e):
                S.replay("dve", e)

            @block.gpsimd
            def _(e):
                S.replay("pool", e)

            @block.sync
            def _(e):
                S.replay("sp", e)
    return nc


def _consts(par):
    ident = np.eye(128, dtype=np.float32)
    ones = np.ones((128, 128), np.float32)
    s = np.arange(128)[:, None]
    t = np.arange(128)[None, :]
    U = (s <= t).astype(np.float32)
    glob = np.array([2 * p + par if p < 16 else 2 * (p - 16) + 1 - par for p in range(32)])
    pred = (glob[:, None] < glob[None, :]).astype(np.float32)
    masks = np.zeros((128, 2, 128), np.float32)
    masks[:, 0, :] = np.where(s <= t, 0.0, NEG)
    masks[:, 1, :] = 0.0 if par == 1 else NEG
    bm = np.zeros((128, 8, 128), np.float32)
    bh = np.zeros((128, 36, 16), np.float32)
    eye = np.eye(128, dtype=np.float32)
    for g, w in enumerate(WINS):
        inwin = ((t - s) >= 0) & ((t - s) < w)
        bm[:, g, :] = np.where(inwin, 1.0 / w, 0.0) - eye
        if par == 0:
            cnt = np.minimum(t + 1, w).astype(np.float32)
            bm[:, 4 + g, :] = np.where(inwin, 1.0 / cnt, 0.0) - eye
        else:
            bm[:, 4 + g, :] = bm[:, g, :]
        for j in range(8):
            for i in range(16):
                for tt in range(16):
                    if tt + 16 - i < w:
                        bh[j * 16 + i, j * 4 + g, tt] = 1.0 / w
        if par == 1:
            bh[:, 32 + g, :] = bh[:, 0 * 4 + g, :]
    return ident, ones, U, pred, masks, bm, bh


def _colT(v, n):
    return np.ascontiguousarray(np.asarray(v, np.float32).reshape(n, 128).T)


_NC_CACHE = {}
_DEBUG = [False]
_NG = [4]


def kernel(x, c, w_ada, b_ada, w_in, b_in, w_pool_mix, b_pool_mix, pool_scale, w_out, b_out, ln_g, ln_b):
    x = np.asarray(x, np.float32)
    c = np.asarray(c, np.float32)
    w_ada = np.ascontiguousarray(np.asarray(w_ada, np.float32)[0])
    b_ada = np.asarray(b_ada, np.float32)[0]
    w_in = np.ascontiguousarray(np.asarray(w_in, np.float32)[0])
    b_in = np.asarray(b_in, np.float32)[0]
    w_pm = np.ascontiguousarray(np.asarray(w_pool_mix, np.float32)[0])
    b_pm = np.asarray(b_pool_mix, np.float32)[0]
    psc = np.asarray(pool_scale, np.float32)[0]
    w_out = np.ascontiguousarray(np.asarray(w_out, np.float32)[0])
    b_out = np.asarray(b_out, np.float32)[0]
    ln_g = np.asarray(ln_g, np.float32)[0]
    ln_b = np.asarray(ln_b, np.float32)[0]

    common = {
        "w_ada": w_ada,
        "b_adaT": _colT(b_ada[0:2048], 16),
        "b_gate": np.ascontiguousarray(b_ada[2048:3072].reshape(1, D)),
        "w_in": w_in,
        "b_qT": _colT(b_in[0:512], 4),
        "b_kT": _colT(b_in[512:1024], 4),
        "b_gT": _colT(b_in[2056:3080], 8),
        "b_vfp": np.ascontiguousarray(b_in[1024:2056].reshape(1, 1032)),
        "w_pm": w_pm,
        "b_pmT": np.ascontiguousarray(b_pm.T),
        "pscT": _colT(psc, 4),
        "w_out": w_out,
        "b_out": np.ascontiguousarray(b_out.reshape(1, D)),
        "ln_g": np.ascontiguousarray(ln_g.reshape(1, D)),
        "ln_b": np.ascontiguousarray(ln_b.reshape(1, D)),
    }
    in_maps = []
    for core in range(NCORES):
        b, par = core // 2, core % 2
        xb_ = x[b].reshape(NBLK, 128, D)
        own = [2 * m + par for m in range(16)]
        oth = [2 * m + 1 - par for m in range(16)]
        xp = np.ascontiguousarray(xb_[own + oth].reshape(SEQ, D))
        xh = np.zeros((256, D), np.float32)
        for m in range(16):
            g = own[m]
            if g > 0:
                xh[m * 16:(m + 1) * 16] = x[b, g * 128 - 16:g * 128]
        ident, ones, U, pred, masks, bm, bh = _consts(par)
        mp = dict(common)
        mp.update({"xp": np.ascontiguousarray(xp[:2048]), "xpT": np.ascontiguousarray(xp.T), "xhT": np.ascontiguousarray(xh.T), "cT": _colT(c[b], 8), "ident": ident, "ones": ones, "U": U,
                   "pred": pred, "masks": masks, "bm": bm, "bh": bh})
        in_maps.append(mp)

    if "nc" not in _NC_CACHE:
        _NC_CACHE["nc"] = build_nc(_DEBUG[0])
    nc = _NC_CACHE["nc"]
    res = run_bass_kernel_spmd(nc, in_maps, core_ids=list(range(NCORES)))
    if _DEBUG[0]:
        _DEBUG.append(res.results)
    out = np.empty((4, SEQ, D), np.float32)
    for core in range(NCORES):
        b, par = core // 2, core % 2
        o = np.asarray(res.results[core]["out"], np.float32).reshape(16, 128, D)
        for m in range(16):
            g = 2 * m + par
            out[b, g * 128:(g + 1) * 128] = o[m]
    return out
```

```python
import numpy as np
from contextlib import ExitStack
import concourse.bass as bass
import concourse.mybir as mybir
from concourse.bass_utils import run_bass_kernel_spmd

F32 = mybir.dt.float32
BF16 = mybir.dt.bfloat16
AF = mybir.ActivationFunctionType
ALU = mybir.AluOpType

NCORES = 8
SEQ = 4096
D = 1024
NBLK = 32
ALPHA = float(2.0 ** 0.25)
EPS = 1e-5
NEG = -30000.0
WINS = (2, 4, 8, 16)


class Tok:
    __slots__ = ("sem", "val", "eng", "key")

    def __init__(self, sem, val, eng, key):
        self.sem, self.val, self.eng, self.key = sem, val, eng, key


class Res:
    def __init__(self, name, track_reads=True):
        self.name = name
        self.w = None
        self.r = []
        self.track = track_reads


class DSem:
    def __init__(self, sem, key):
        self.sem, self.n, self.key = sem, 0, key


class Sched:
    ENGS = ("pe", "act", "dve", "pool", "sp")

    def __init__(self, nc, es):
        self.nc = nc
        self.es = es
        self.q = {e: [] for e in self.ENGS}
        self.sem = {e: es.enter_context(nc.semaphore("s_" + e)) for e in self.ENGS}
        self.cnt = {e: 0 for e in self.ENGS}
        self.seen = {e: {} for e in self.ENGS}
        self.nd = 0

    def dsem(self, name):
        self.nd += 1
        return DSem(self.es.enter_context(self.nc.semaphore("d_" + name)), "d%d" % self.nd)

    def _waits(self, eng, reads, writes, extra, is_dma=False):
        need = {}

        def add(t):
            if t is None:
                return
            if t.eng == eng and eng == "pe" and not is_dma:
                return
            if self.seen[eng].get(t.key, 0) >= t.val:
                return
            if need.get(t.key, (None, 0))[1] < t.val:
                need[t.key] = (t.sem, t.val)

        for r in reads:
            add(r.w)
        for w in writes:
            add(w.w)
            for t in w.r:
                add(t)
        for t in extra:
            add(t)
        for k, (s, v) in need.items():
            self.seen[eng][k] = v
        return list(need.values())

    def _commit(self, tok, reads, writes):
        for w in writes:
            w.w = tok
            w.r = []
        for r in reads:
            if r.track:
                r.r.append(tok)

    def op(self, eng, fn, reads=(), writes=(), extra=()):
        waits = self._waits(eng, reads, writes, extra)
        self.cnt[eng] += 1
        tok = Tok(self.sem[eng], self.cnt[eng], eng, eng)
        self.q[eng].append((waits, fn, (self.sem[eng], 1)))
        self._commit(tok, reads, writes)
        return tok

    def dma(self, eng, fn, ds, reads=(), writes=(), extra=()):
        waits = self._waits(eng, reads, writes, extra, is_dma=True)
        ds.n += 1
        tok = Tok(ds.sem, ds.n * 16, "dma", ds.key)
        self.q[eng].append((waits, fn, (ds.sem, 16)))
        self._commit(tok, reads, writes)
        return tok

    def wait_only(self, eng, toks):
        waits = self._waits(eng, (), (), toks)
        self.q[eng].append((waits, None, None))

    def replay(self, eng, e):
        for waits, fn, inc in self.q[eng]:
            for s, v in waits:
                e.wait_ge(s, v)
            if fn is not None:
                ins = fn(e)
                ins.then_inc(inc[0], inc[1])


def build_nc(debug=False):
    nc = bass.Bass("TRN2", target_bir_lowering=False)

    def din(name, shape):
        return nc.dram_tensor(name, list(shape), F32, kind="ExternalInput").ap()

    xp = din("xp", [2048, D])
    xpT = din("xpT", [D, SEQ])
    xhT = din("xhT", [D, 256])
    cT_d = din("cT", [128, 8])
    w_ada = din("w_ada", [D, 3 * D])
    b_adaT_d = din("b_adaT", [128, 16])
    b_gate_d = din("b_gate", [1, D])
    w_in = din("w_in", [D, 3080])
    b_qT_d = din("b_qT", [128, 4])
    b_kT_d = din("b_kT", [128, 4])
    b_gT_d = din("b_gT", [128, 8])
    b_vfp_d = din("b_vfp", [1, 1032])
    w_pm_d = din("w_pm", [4, 128, 128])
    b_pmT_d = din("b_pmT", [128, 4])
    pscT_d = din("pscT", [128, 4])
    w_out_d = din("w_out", [D, D])
    b_out_d = din("b_out", [1, D])
    ln_g_d = din("ln_g", [1, D])
    ln_b_d = din("ln_b", [1, D])
    ident_d = din("ident", [128, 128])
    ones_d = din("ones", [128, 128])
    U_d = din("U", [128, 128])
    pred_d = din("pred", [32, 32])
    masks_d = din("masks", [128, 2, 128])
    bm_d = din("bm", [128, 8, 128])
    bh_d = din("bh", [128, 36, 16])
    out_d = nc.dram_tensor("out", [2048, D], F32, kind="ExternalOutput").ap()

    wbf = nc.dram_tensor("wbf", [4, 128, 8, 512], BF16, kind="Internal").ap()
    xpT_v = xpT.rearrange("(kc p) t -> p kc t", p=128)
    xhT_v = xhT.rearrange("(kc p) t -> p kc t", p=128)
    w_ada_v = w_ada.rearrange("(kc p) e -> p kc e", p=128)
    w_in_v = w_in.rearrange("(kc p) e -> p kc e", p=128)
    w_out_v = w_out_d.rearrange("(kc p) e -> p kc e", p=128)

    with ExitStack() as es:
        S = Sched(nc, es)

        def sb(name, shape, dt=F32):
            return es.enter_context(nc.sbuf_tensor("sb_" + name, list(shape), dt))

        banks = [es.enter_context(nc.psum_tensor("ps%d" % i, [128, 512], F32)) for i in range(8)]
        bres = [Res("bank%d" % i) for i in range(8)]

        nlf = sb("nlf", [128, NBLK, 8])
        Cpos = sb("Cpos", [128, NBLK, 8])
        biasG = [sb("biasG%d" % i, [128, NBLK, 8]) for i in range(2)]
        ident = sb("ident", [128, 128])
        onesf = sb("onesf", [128, 128])
        Uf = sb("Uf", [128, 128])
        predf = sb("predf", [32, 32])
        tot = sb("tot", [32, 8])
        Z = sb("Z", [32, 256])
        identb = sb("identb", [128, 128], BF16)
        masks = sb("masks", [128, 2, 128], BF16)
        bm = sb("bm", [128, 8, 128], BF16)
        bh = sb("bh", [128, 36, 16], BF16)
        gate_bc = sb("gate_bc", [128, D])
        gb = sb("gb", [128, D])
        lng = sb("lng", [128, D])
        lnb = sb("lnb", [128, D])
        bvfp = sb("bvfp", [128, 1032])
        adaT = sb("adaT", [128, 16])
        scale1 = sb("scale1", [128, 8])
        b_adaT = sb("b_adaT", [128, 16])
        b_qT = sb("b_qT", [128, 4])
        b_kT = sb("b_kT", [128, 4])
        b_gT = sb("b_gT", [128, 8])
        b_pmT = sb("b_pmT", [128, 4])
        pscT = sb("pscT", [128, 4])
        cT = sb("cT", [128, 8])
        sc = sb("sc", [128, 8])
        phalo = sb("phalo", [128, 2, 512], BF16)
        wpm = sb("wpm", [128, 4, 128], BF16)
        NXB = 4
        xb = [sb("xb%d" % i, [128, D]) for i in range(NXB)]
        uT = [sb("uT%d" % i, [128, 8, 512], BF16) for i in range(2)]
        wA = sb("wA", [128, 8, 1032], BF16)
        wS = [sb("wS%d" % i, [128, 8, 512], BF16) for i in range(2)]
        QT = sb("QT", [128, 8, 512], BF16)
        gatt = sb("gatt", [128, 4, 512], BF16)
        scr = sb("scr", [128, 12, 512], BF16)
        yT = sb("yT", [128, 4, 512], BF16)
        gatt1 = sb("gatt1", [128, 4, 512], BF16)
        stats2 = [sb("stats2_%d" % i, [128, 2, 6]) for i in range(2)]
        mv2 = [sb("mv2_%d" % i, [128, 2]) for i in range(2)]
        ve2 = [sb("ve2_%d" % i, [128, 1]) for i in range(2)]
        rstd2 = [sb("rstd2_%d" % i, [128, 1]) for i in range(2)]
        mhalf = sb("mhalf", [128, 1])
        r_sm = [Res("sm0"), Res("sm1")]
        nbg = sb("nbg", [128, 8])
        tmpf = sb("tmpf", [128, 512])
        tmpf2 = sb("tmpf2", [128, 512])
        hb = [sb("hb%d" % i, [128, D]) for i in range(2)]
        rl = sb("rl", [128, 4])
        fl = sb("fl", [128, 4, 8])
        ex = sb("ex", [128, 4, 8])

        r_KT = Res("KT", False)
        r_V = Res("V", False)
        r_nlf = Res("nlf")
        r_Cpos = Res("Cpos", False)
        r_biasG = [Res("biasG0"), Res("biasG1")]
        r_const = Res("const", False)
        r_constb = Res("constb", False)
        r_ada = Res("ada", False)
        r_gate = Res("gate", False)
        r_gb = Res("gb", False)
        r_sc = Res("sc", False)
        r_xb = [Res("xb%d" % i) for i in range(NXB)]
        r_uT = [Res("uT0"), Res("uT1")]
        r_wA = Res("wA")
        r_wS = [Res("wS0"), Res("wS1")]
        r_QT = Res("QT")
        r_gatt = Res("gatt")
        r_scr = [Res("scr%d" % i) for i in range(12)]
        r_yT = [Res("yT%d" % i) for i in range(4)]
        r_small2 = Res("small2", False)
        r_tmpf = Res("tmpf")
        r_tmpf2 = Res("tmpf2")
        tmpfs = [tmpf, tmpf2]
        r_tmpfs = [r_tmpf, r_tmpf2]
        r_hb = [Res("hb0"), Res("hb1")]
        r_small = Res("small")
        r_rl = Res("rl")
        r_phalo = Res("phalo", False)
        r_fl = Res("fl")
        r_tot = Res("tot")
        r_Z = Res("Z")

        d_const = S.dsem("const")
        d_constb = S.dsem("constb")
        d_xb = [S.dsem("xb%d" % i) for i in range(NXB)]
        d_wa = [S.dsem("wa%d" % i) for i in range(3)]
        d_wA = S.dsem("wA")
        d_wS = [S.dsem("wS0"), S.dsem("wS1")]
        d_out = [S.dsem("out%d" % i) for i in range(4)]
        d_tmp = S.dsem("tmp")

        def cload(dst, src, eng="act"):
            r_const.w = S.dma(eng, lambda e, dst=dst, src=src: e.dma_start(out=dst, in_=src), d_const)

        r_c0 = Res("c0", False)
        d_c0 = S.dsem("c0")
        S.dma("act", lambda e: e.dma_start(out=cT[:, :], in_=cT_d[:, :]), d_c0)
        r_c0.w = S.dma("act", lambda e: e.dma_start(out=b_adaT[:, :], in_=b_adaT_d[:, :]), d_c0)
        S.op("act", lambda e: e.activation(sc[:, :], cT[:, :], AF.Silu), reads=[r_c0], writes=[r_sc])
        cload(ident[:, :], ident_d[:, :])
        cload(onesf[:, :], ones_d[:, :])
        cload(Uf[:, :], U_d[:, :])
        cload(predf[:, :], pred_d[:, :])
        cload(b_qT[:, :], b_qT_d[:, :])
        cload(b_kT[:, :], b_kT_d[:, :])
        cload(b_gT[:, :], b_gT_d[:, :])
        cload(b_pmT[:, :], b_pmT_d[:, :])
        cload(pscT[:, :], pscT_d[:, :])
        cload(bvfp[:, :], b_vfp_d.partition_broadcast(128))
        cload(lng[:, :], ln_g_d.partition_broadcast(128))
        cload(lnb[:, :], ln_b_d.partition_broadcast(128))
        cload(gb[:, :], b_out_d.partition_broadcast(128))
        cload(gate_bc[:, :], b_gate_d.partition_broadcast(128))

        def cloadb(dst, src):
            r_constb.w = S.dma("pool", lambda e, dst=dst, src=src: e.dma_start(out=dst, in_=src), d_constb)

        cloadb(identb[:, :], ident_d[:, :])
        cloadb(masks[:, :, :], masks_d[:, :, :])
        cloadb(bm[:, :, :], bm_d[:, :, :])
        cloadb(bh[:, :, :], bh_d[:, :, :])
        cloadb(wpm[:, :, :], w_pm_d.rearrange("g c e -> c g e"))
        for half in range(2):
            r_wA.w = S.dma("pool", lambda e, half=half: e.dma_start(out=wA[:, half * 4:(half + 1) * 4, :],
                                                                   in_=w_in_v[:, half * 4:(half + 1) * 4, 512:1544]),
                           d_wA)

        S.op("dve", lambda e: e.memset(mhalf[:, :], -0.5), writes=[r_small])
        r_wbf = [Res("wbf%d" % i, False) for i in range(4)]
        d_wbf = [S.dsem("wbf%d" % i) for i in range(4)]
        for hh in range(2):
            S.op("pool", lambda e, hh=hh: e.memset(QT[(1 - hh) * 64:(2 - hh) * 64, hh:8:2, :], 0.0), writes=[r_QT])

        with ExitStack() as es2:
            wa = [es2.enter_context(nc.sbuf_tensor("wa%d" % i, [128, 8, 512], F32)) for i in range(3)]
            scbc = es2.enter_context(nc.sbuf_tensor("scbc", [128, 8, 128], F32))
            r_wa = [Res("wa%d" % i) for i in range(3)]
            r_scbc = Res("scbc")

            S.op("dve", lambda e: e.tensor_copy(scbc[:, :, :], sc[:, :].unsqueeze(2).to_broadcast([128, 8, 128])),
                 reads=[r_sc], writes=[r_scbc])
            psA = banks[7]
            for j in range(6):
                buf = wa[j % 3]
                S.dma("sp", lambda e, buf=buf, j=j: e.dma_start(out=buf[:, :, :], in_=w_ada_v[:, :, j * 512:(j + 1) * 512]),
                      d_wa[j % 3], writes=[r_wa[j % 3]])
                if j < 4:
                    def mm(e, buf=buf, j=j):
                        ins = None
                        for i in range(4):
                            fc = j * 4 + i
                            for kc in range(8):
                                ins = e.matmul(psA[:, fc:fc + 1], buf[:, kc, i * 128:(i + 1) * 128], sc[:, kc:kc + 1],
                                               start=(kc == 0), stop=(kc == 7))
                        return ins
                    S.op("pe", mm, reads=[r_wa[j % 3], r_sc], writes=[bres[7]])
                else:
                    bk = 5 + (j - 4)

                    def mm(e, buf=buf, bk=bk):
                        ins = None
                        for kc in range(8):
                            ins = e.matmul(banks[bk][:, :], scbc[:, kc, :], buf[:, kc, :], start=(kc == 0), stop=(kc == 7))
                        return ins
                    S.op("pe", mm, reads=[r_wa[j % 3], r_scbc], writes=[bres[bk]])
                    half = j - 4
                    S.op("dve", lambda e, bk=bk, half=half: e.tensor_tensor(
                        gate_bc[:, half * 512:(half + 1) * 512], banks[bk][:, :],
                        gate_bc[:, half * 512:(half + 1) * 512], ALU.add),
                        reads=[bres[bk], r_const], writes=[r_gate])
                if j == 3:
                    S.op("dve", lambda e: e.tensor_tensor(adaT[:, :], psA[:, 0:16], b_adaT[:, :], ALU.add),
                         reads=[bres[7], r_c0], writes=[r_ada])
                    S.op("dve", lambda e: e.tensor_scalar_add(scale1[:, :], adaT[:, 8:16], 1.0), writes=[r_ada])
            S.op("pool", lambda e: e.tensor_tensor(gb[:, :], gb[:, :], gate_bc[:, :], ALU.mult),
                 reads=[r_gate, r_const], writes=[r_gb])
            t_end_ada = S.op("pe", lambda e: e.matmul(banks[7][:, 0:1], onesf[:, 0:128], onesf[:, 0:1], start=True, stop=True),
                             reads=[r_const], writes=[bres[7]])

        shiftT = adaT
        KT = sb("KT", [128, 4, SEQ], BF16)
        V = sb("V", [128, NBLK, 8, 66], BF16)
        S.op("pool", lambda e: e.memset(V[:, :, :, 64], 1.0), writes=[r_V], extra=[t_end_ada])

        xb_rot = [0]

        def load_x(src_rows, q="sp"):
            i = xb_rot[0] % NXB
            xb_rot[0] += 1
            S.dma(q, lambda e, i=i, src_rows=src_rows: e.dma_start(out=xb[i][:, :], in_=src_rows), d_xb[i],
                  writes=[r_xb[i]])
            return i

        def load_feat(src_v, tok0, ntok, q="sp"):
            per = 1024 // ntok
            bufs = []
            for kc0 in range(0, 8, per):
                i = xb_rot[0] % NXB
                xb_rot[0] += 1
                S.dma(q, lambda e, i=i, kc0=kc0: e.dma_start(
                    out=xb[i][:, :].rearrange("p (a b) -> p a b", b=ntok), in_=src_v[:, kc0:kc0 + per, tok0:tok0 + ntok]),
                    d_xb[i], writes=[r_xb[i]])
                bufs.append(i)
            return bufs

        def modulate_sb(kc, bufs, ntok, ut, eng):
            per = 1024 // ntok
            xi = bufs[kc // per]
            off = (kc % per) * ntok
            if eng == "act":
                S.op("act", lambda e: e.activation(uT[ut][:, kc, 0:ntok], xb[xi][:, off:off + ntok], AF.Identity,
                                                   bias=shiftT[:, kc:kc + 1], scale=scale1[:, kc:kc + 1]),
                     reads=[r_xb[xi], r_ada], writes=[r_uT[ut]])
            else:
                S.op(eng, lambda e: e.tensor_scalar(uT[ut][:, kc, 0:ntok], xb[xi][:, off:off + ntok],
                                                    scale1[:, kc:kc + 1], shiftT[:, kc:kc + 1], ALU.mult, ALU.add),
                     reads=[r_xb[xi], r_ada], writes=[r_uT[ut]])

        ENG_MIX = ["dve", "act", "pool", "act", "dve", "act", "pool", "act"]

        rotT = [0]
        rotKV = [0]
        KVB = [0, 1, 2, 3, 4, 5, 7]
        def emit_T(ch):
            bufs = load_feat(xpT_v, ch * 512, 512, "act" if ch == 0 else "sp")
            for kc in range(8):
                modulate_sb(kc, bufs, 512, ch % 2, "act" if ch == 0 else ENG_MIX[kc])

        def emit_K(ch):
            ut = ch % 2
            for pair in range(4):
                bank = KVB[rotKV[0] % len(KVB)]
                rotKV[0] += 1

                def mm(e, pair=pair, bank=bank, ut=ut):
                    ins = None
                    for kc in range(8):
                        ins = e.matmul(banks[bank][:, :], wA[:, kc, pair * 128:(pair + 1) * 128], uT[ut][:, kc, :],
                                       start=(kc == 0), stop=(kc == 7))
                    return ins
                S.op("pe", mm, reads=[r_wA, r_uT[ut]], writes=[bres[bank]])
                S.op("act", lambda e, pair=pair, bank=bank, ch=ch: e.activation(
                    KT[:, pair, ch * 512:(ch + 1) * 512], banks[bank][:, :], AF.Identity, bias=b_kT[:, pair:pair + 1], scale=1.0),
                    reads=[bres[bank], r_const], writes=[r_KT], extra=[t_end_ada])

        def emit_V(ch):
            ut = ch % 2
            for bi in range(4):
                pos = ch * 4 + bi
                bank = KVB[rotKV[0] % len(KVB)]
                rotKV[0] += 1

                def mm(e, bi=bi, bank=bank, ut=ut):
                    ins = None
                    for kc in range(8):
                        ins = e.matmul(banks[bank][:, :], uT[ut][:, kc, bi * 128:(bi + 1) * 128], wA[:, kc, 512:1024],
                                       start=(kc == 0), stop=(kc == 7))
                    return ins
                S.op("pe", mm, reads=[r_wA, r_uT[ut]], writes=[bres[bank]])
                S.op("dve", lambda e, pos=pos, bank=bank: e.tensor_tensor(
                    V[:, pos, :, 0:64], banks[bank][:, :].rearrange("p (h d) -> p h d", d=64),
                    bvfp[:, 0:512].rearrange("p (h d) -> p h d", d=64), ALU.add),
                    reads=[bres[bank], r_const], writes=[r_V], extra=[t_end_ada])

            def mmf(e, ut=ut):
                ins = None
                for bi in range(4):
                    for kc in range(8):
                        ins = e.matmul(banks[6][:, bi * 8:(bi + 1) * 8], uT[ut][:, kc, bi * 128:(bi + 1) * 128],
                                       wA[:, kc, 1024:1032], start=(kc == 0), stop=(kc == 7))
                return ins
            S.op("pe", mmf, reads=[r_wA, r_uT[ut]], writes=[bres[6]])
            S.op("dve", lambda e: e.tensor_tensor(
                fl[:, :, :], banks[6][:, 0:32].rearrange("p (b h) -> p b h", h=8),
                bvfp[:, 512:520].unsqueeze(1).to_broadcast([128, 4, 8]), ALU.add),
                reads=[bres[6], r_const], writes=[r_fl])
            S.op("act", lambda e: e.activation(ex[:, :, :], fl[:, :, :], AF.Exp, scale=-1.0), reads=[r_fl], writes=[r_small])
            S.op("act", lambda e, ch=ch: e.activation(nlf[:, ch * 4:(ch + 1) * 4, :], ex[:, :, :], AF.Ln, bias=1.0, scale=1.0),
                 reads=[r_small], writes=[r_nlf, r_fl])

        emit_T(0)
        for ch in range(8):
            emit_K(ch)
            if ch == 2:
                for c0, pc in ((0, 0), (2056, 1), (1544, 3), (2568, 2)):
                    S.dma("pool", lambda e, c0=c0, pc=pc: e.dma_start(out=wbf[pc, :, :, :], in_=w_in_v[:, :, c0:c0 + 512]),
                          d_wbf[pc], writes=[r_wbf[pc]], extra=[r_KT.w])
            if ch + 1 < 8:
                emit_T(ch + 1)
            emit_V(ch)

        def mmtot(e):
            ins = None
            for h in range(8):
                ins = e.matmul(banks[0][0:32, 256 + h:257 + h], nlf[:, :, h], onesf[:, 0:1], start=True, stop=True)
            return ins
        def cum1():
            S.op("pe", mmtot, reads=[r_nlf, r_const], writes=[bres[0]])
            S.op("dve", lambda e: e.tensor_copy(tot[:, :], banks[0][0:32, 256:264]), reads=[bres[0]], writes=[r_tot])
            S.op("dve", lambda e: e.tensor_tensor(
                Z[:, :].rearrange("p (b h) -> p b h", h=8), tot[:, :].unsqueeze(1).to_broadcast([32, 32, 8]),
                predf[:, :].unsqueeze(2).to_broadcast([32, 32, 8]), ALU.mult), reads=[r_tot, r_const], writes=[r_Z])

        def mmcum(e):
            e.matmul(banks[0][:, 0:256], Uf[:, :], nlf[:, :, :].rearrange("p b h -> p (b h)"), start=True, stop=False)
            return e.matmul(banks[0][:, 0:256], onesf[0:32, :], Z[:, :], start=False, stop=True)
        def cum2():
            S.op("pe", mmcum, reads=[r_nlf, r_Z, r_const], writes=[bres[0]])
            S.op("dve", lambda e: e.tensor_copy(Cpos[:, :, :].rearrange("p b h -> p (b h)"), banks[0][:, 0:256]),
                 reads=[bres[0]], writes=[r_Cpos])

        SB_ = [0, 1, 2, 3]
        OB_ = [4, 5]
        MB_ = [6, 7]
        rotM = [0]
        rotW = [0]

        def mbank():
            b = MB_[rotM[0] % len(MB_)]
            rotM[0] += 1
            return b

        PIECE = {0: 0, 2056: 1, 2568: 2, 1544: 3}

        def load_ws(c0, first=False):
            i = rotW[0] % 2
            rotW[0] += 1
            pc = PIECE[c0]
            S.dma("pool", lambda e, i=i, pc=pc: e.dma_start(out=wS[i][:, :, :], in_=wbf[pc, :, :, :]),
                  d_wS[i], reads=[r_wbf[pc]], writes=[r_wS[i]])
            return i

        PCOL = 1544
        QCOL = 0
        GACOL = 2056
        GPCOL = 2568

        On = hb[0][:, :].bitcast(BF16).rearrange("p (a b) -> p a b", b=512)
        PTv = hb[1][:, :].bitcast(BF16).rearrange("p (a b) -> p a b", b=512)
        r_On = Res("On")
        r_pt = [Res("pt%d" % i) for i in range(4)]
        QTb = [QT, uT[1]]
        r_QTb = [r_QT, r_uT[1]]
        gattb = [gatt, gatt1]
        r_gattb = [r_gatt, Res("gatt1")]
        nb_gT = nbg
        S.op("dve", lambda e: e.tensor_scalar(nb_gT[:, :], b_gT[:, :], 0.5, None, ALU.mult), reads=[r_const], writes=[r_small2])

        tm_rot = [0]

        def silu_evac(bank, c8, out_ap, out_res):
            ti = tm_rot[0] % 2
            tm_rot[0] += 1
            tm, r_tm = tmpfs[ti], r_tmpfs[ti]
            S.op("act", lambda e: e.activation(tm[:, :], banks[bank][:, :], AF.Tanh, bias=nb_gT[:, c8:c8 + 1], scale=0.5),
                 reads=[bres[bank], r_small2], writes=[r_tm])
            S.op("pool", lambda e: e.tensor_scalar(tm[:, :], tm[:, :], 0.5, 0.5, ALU.mult, ALU.add), reads=[r_tm], writes=[r_tm])
            S.op("dve", lambda e: e.scalar_tensor_tensor(out_ap, banks[bank][:, :], b_gT[:, c8:c8 + 1], tm[:, :], ALU.add, ALU.mult),
                 reads=[bres[bank], r_tm, r_const], writes=out_res)

        def group2(lhs_fn, rhs_fn, rd, bank):
            for part in range(2):
                def mm(e, part=part):
                    ins = None
                    for kc in range(part * 4, part * 4 + 4):
                        ins = e.matmul(banks[bank][:, :], lhs_fn(kc), rhs_fn(kc), start=(kc == 0), stop=(kc == 7))
                    return ins
                S.op("pe", mm, reads=rd, writes=[bres[bank]])
                yield

        mods_done = [False]
        GORD = [3, 2, 1, 0]

        def emit_biasG(G):
            bG = G % 2
            bank = mbank()
            S.op("pe", lambda e, bank=bank, G=G: e.matmul(banks[bank][:, 0:8], onesf[:, :], Cpos[:, 4 * G + 2, :], start=True, stop=True),
                 reads=[r_Cpos, r_const], writes=[bres[bank]])
            S.op("dve", lambda e, bank=bank, bG=bG: e.scalar_tensor_tensor(
                biasG[bG][:, :, :], banks[bank][:, 0:8].unsqueeze(1).to_broadcast([128, NBLK, 8]), -1.0 / 128.0,
                Cpos[:, :, :], ALU.mult, ALU.add), reads=[bres[bank], r_Cpos], writes=[r_biasG[bG]])

        def prelude(G, overlapped, n_sp=14):
            qb = (G + 1) % 2
            X, Y = 4 * ((G + 2) % 3), 4 * (G % 3)
            wq = load_ws(QCOL, G == 0)
            wga = load_ws(GACOL, G == 0)
            bufs = load_feat(xpT_v, G * 512, 512)
            for _ in range(n_sp):
                yield
            for kc in range(8):
                modulate_sb(kc, bufs, 512, 0, "pool" if overlapped else ENG_MIX[kc])
                if kc % 2 == 1:
                    yield
            mods_done[0] = True
            if G == GORD[0]:
                hbufs = load_feat(xhT_v, 0, 256)
                for kc in range(8):
                    modulate_sb(kc, hbufs, 256, 1, ENG_MIX[kc])
            if G == GORD[1]:
                for hh in range(2):
                    S.op("pool", lambda e, hh=hh: e.memset(uT[1][(1 - hh) * 64:(2 - hh) * 64, hh:8:2, :], 0.0), writes=[r_uT[1]])
            for c in range(4):
                bank = mbank()
                for _ in group2(lambda kc, c=c: wS[wq][:, kc, c * 128:(c + 1) * 128], lambda kc: uT[0][:, kc, :],
                                  [r_wS[wq], r_uT[0]], bank):
                    pass
                for hh in range(2):
                    S.op("dve", lambda e, hh=hh, c=c, bank=bank: e.tensor_scalar(
                        QTb[qb][hh * 64:(hh + 1) * 64, 2 * c + hh, :], banks[bank][hh * 64:(hh + 1) * 64, :],
                        b_qT[hh * 64:(hh + 1) * 64, c:c + 1], None, ALU.add),
                        reads=[bres[bank], r_const], writes=[r_QTb[qb]])
                yield
            wp = load_ws(PCOL, G == 0)
            for c in range(4):
                bank = mbank()
                for _ in group2(lambda kc, c=c: wS[wga][:, kc, c * 128:(c + 1) * 128], lambda kc: uT[0][:, kc, :],
                                  [r_wS[wga], r_uT[0]], bank):
                    pass
                silu_evac(bank, c, gattb[qb][:, c, :], [r_gattb[qb]])
                yield
            wgp = load_ws(GPCOL, G == 0)
            for mi in range(4):
                bank = mbank()
                for _ in group2(lambda kc, mi=mi: uT[0][:, kc, mi * 128:(mi + 1) * 128], lambda kc: wS[wp][:, kc, :],
                                  [r_wS[wp], r_uT[0]], bank):
                    pass
                S.op("dve", lambda e, mi=mi, bank=bank: e.tensor_tensor(scr[:, X + mi, :], banks[bank][:, :], bvfp[:, 520:1032], ALU.add),
                     reads=[bres[bank], r_const], writes=[r_scr[X + mi]])
                yield
            if G == GORD[0]:
                for hbk in range(2):
                    bank = mbank()
                    for _ in group2(lambda kc, hbk=hbk: uT[1][:, kc, hbk * 128:(hbk + 1) * 128], lambda kc: wS[wp][:, kc, :],
                                    [r_wS[wp], r_uT[1]], bank):
                        pass
                    S.op("dve", lambda e, hbk=hbk, bank=bank: e.tensor_tensor(phalo[:, hbk, :], banks[bank][:, :], bvfp[:, 520:1032], ALU.add),
                         reads=[bres[bank], r_const], writes=[r_phalo])
            for mi in range(4):
                m = 4 * G + mi
                bank = mbank()

                def mm(e, mi=mi, m=m, bank=bank):
                    ins = None
                    for g in range(4):
                        bmi = g if m > 0 else 4 + g
                        bhi = ((m % 8) * 4 + g) if m > 0 else 32 + g
                        e.matmul(banks[bank][:, g * 128:(g + 1) * 128], scr[:, X + mi, g * 128:(g + 1) * 128], bm[:, bmi, :],
                                 start=True, stop=False)
                        ins = e.matmul(banks[bank][:, g * 128:g * 128 + 16], phalo[:, m // 8, g * 128:(g + 1) * 128],
                                       bh[:, bhi, :], start=False, stop=True)
                    return ins
                S.op("pe", mm, reads=[r_scr[X + mi], r_phalo, r_constb], writes=[bres[bank]])
                S.op("dve", lambda e, mi=mi, bank=bank: e.tensor_copy(
                    scr[:, Y:Y + 4, mi * 128:(mi + 1) * 128], banks[bank][:, :].rearrange("p (g t) -> p g t", t=128)),
                    reads=[bres[bank]], writes=[r_scr[Y + g] for g in range(4)])
                yield
            for c in range(4):
                bank = mbank()
                for _ in group2(lambda kc, c=c: wS[wgp][:, kc, c * 128:(c + 1) * 128], lambda kc: uT[0][:, kc, :],
                                  [r_wS[wgp], r_uT[0]], bank):
                    pass
                silu_evac(bank, 4 + c, scr[:, X + c, :], [r_scr[X + c]])
                yield
            for g in range(4):
                bank = mbank()
                S.op("pe", lambda e, g=g, bank=bank: e.matmul(banks[bank][:, :], wpm[:, g, :], scr[:, Y + g, :], start=True, stop=True),
                     reads=[r_scr[Y + g], r_constb], writes=[bres[bank]])
                S.op("dve", lambda e, g=g, bank=bank: e.tensor_scalar(
                    tmpf[:, :], banks[bank][:, :], b_pmT[:, g:g + 1], pscT[:, g:g + 1], ALU.add, ALU.mult),
                    reads=[bres[bank], r_const], writes=[r_tmpf])
                S.op("pool", lambda e, g=g: e.tensor_tensor(scr[:, Y + g, :], tmpf[:, :], scr[:, X + g, :], ALU.mult),
                     reads=[r_tmpf, r_scr[X + g]], writes=[r_scr[Y + g]])
                yield
            if overlapped:
                emit_biasG(G)

        pg = prelude(GORD[0], False, 0)
        for _ in range(4):
            next(pg, None)
        cum1()
        for _ in range(4):
            next(pg, None)
        cum2()


        epg = [None]
        for gi, G in enumerate(GORD):
            Gn = GORD[gi + 1] if gi + 1 < len(GORD) else None
            qb = (G + 1) % 2
            Y = 4 * (G % 3)
            nit_g = 8 * (8 * G + 8)
            sp_items = 40 if gi == 0 else 45
            stride = max(1, nit_g // (50 + sp_items))
            if gi == 0:
                sp_items, stride = 24, 4
            gen = prelude(Gn, True, -(-sp_items // stride)) if Gn is not None else None
            if gi == 0:
                for half in range(2):
                    S.dma("pool", lambda e, half=half: e.dma_start(out=wA[:, half * 4:(half + 1) * 4, 0:1024],
                                                                  in_=w_out_v[:, half * 4:(half + 1) * 4, :]),
                          d_wA, writes=[r_wA])
            bG = G % 2
            if gi == 0:
                emit_biasG(G)

            nk = 4 * G + 4
            kblocks = [(i, i) for i in range(nk)] + [(16 + i, i) for i in range(nk)]
            items = []
            for h in range(8):
                for j, (pos, i) in enumerate(kblocks):
                    items.append((h, pos, i, j == 0, j == len(kblocks) - 1))
            LA = 3
            prev_ep = epg[0]
            first_rest = pg if gi == 0 else None
            ep_xi = None
            ep_stt = False
            ep_at = 0
            mods_done[0] = False
            nit = len(items)
            for idx in range(nit + LA):
                if idx < nit:
                    h, pos, i, first, last = items[idx]
                    pair, hh = h // 2, h % 2
                    own = pos < 16
                    col0 = max(0, i - 4 * G) * 128
                    masked = i >= 4 * G
                    sbk = SB_[idx % len(SB_)]
                    pti = idx % 4

                    def mms(e, pos=pos, col0=col0, masked=masked, sbk=sbk, own=own, pair=pair, h=h, qb=qb):
                        ins = e.matmul(banks[sbk][:, col0:512], KT[:, pair, pos * 128:(pos + 1) * 128],
                                       QTb[qb][:, h, col0:512], start=True, stop=(not masked))
                        if masked:
                            ins = e.matmul(banks[sbk][:, col0:col0 + 128], identb[:, :], masks[:, 0 if own else 1, :],
                                           start=False, stop=True)
                        return ins
                    S.op("pe", mms, reads=[r_KT, r_QTb[qb], r_constb], writes=[bres[sbk]])
                    S.op("act", lambda e, pos=pos, col0=col0, sbk=sbk, pti=pti, h=h, bG=bG: e.activation(
                        PTv[:, pti, col0:512], banks[sbk][:, col0:512], AF.Exp, bias=biasG[bG][:, pos, h:h + 1], scale=0.125),
                        reads=[bres[sbk], r_biasG[bG]], writes=[r_pt[pti]])
                if idx >= LA:
                    jdx = idx - LA
                    h, pos, i, first, last = items[jdx]
                    own = pos < 16
                    col0 = max(0, i - 4 * G) * 128
                    pti = jdx % 4
                    ob = OB_[h % 2]

                    def mmpv(e, pos=pos, col0=col0, pti=pti, h=h, ob=ob, first=first, i=i, own=own, G=G):
                        ins = None
                        for mi in range(col0 // 128, 4):
                            st = first and mi == 0
                            sp_ = (not own) and (i == 4 * G + mi)
                            ins = e.matmul(banks[ob][:, mi * 65:(mi + 1) * 65], PTv[:, pti, mi * 128:(mi + 1) * 128],
                                           V[:, pos, h, 0:65], start=st, stop=sp_, skip_group_check=True)
                        return ins
                    S.op("pe", mmpv, reads=[r_pt[pti], r_V], writes=[bres[ob]])
                    if last:
                        S.op("dve", lambda e, ob=ob: e.reciprocal(
                            rl[:, :], banks[ob][:, 0:260].rearrange("p (a b) -> p a b", b=65)[:, :, 64]),
                            reads=[bres[ob]], writes=[r_rl])
                        S.op("dve", lambda e, ob=ob, h=h: e.tensor_tensor(
                            On[:, :, h * 64:(h + 1) * 64], banks[ob][:, 0:260].rearrange("p (a b) -> p a b", b=65)[:, :, 0:64],
                            rl[:, :].unsqueeze(2).to_broadcast([128, 4, 64]), ALU.mult),
                            reads=[bres[ob], r_rl], writes=[r_On])
                if prev_ep is not None and idx % 2 == 1:
                    try:
                        next(prev_ep)
                    except StopIteration:
                        prev_ep = None
                if prev_ep is None and idx % stride == stride - 1:
                    if first_rest is not None:
                        try:
                            next(first_rest)
                        except StopIteration:
                            first_rest = None
                    elif gen is not None:
                        next(gen, None)
                if ep_xi is None and idx >= nit - 40 and prev_ep is None and (gen is None or mods_done[0]):
                    ep_xi = [load_x(xp[(4 * G + mi) * 128:(4 * G + mi + 1) * 128, :]) for mi in range(4)]
                    ep_at = idx
                if ep_xi is not None and not ep_stt and idx >= max(nit - 20, ep_at + 10):
                    ep_stt = True
                    for xi in ep_xi:
                        S.op("dve", lambda e, xi=xi: e.scalar_tensor_tensor(xb[xi][:, :], xb[xi][:, :], ALPHA, gb[:, :], ALU.mult, ALU.add),
                             reads=[r_gb], writes=[r_xb[xi]])
            if first_rest is not None:
                for _ in first_rest:
                    pass
            if gen is not None:
                for _ in gen:
                    pass
            if prev_ep is not None:
                for _ in prev_ep:
                    pass
            if ep_xi is None:
                ep_xi = [load_x(xp[(4 * G + mi) * 128:(4 * G + mi + 1) * 128, :]) for mi in range(4)]
            if not ep_stt:
                for xi in ep_xi:
                    S.op("dve", lambda e, xi=xi: e.scalar_tensor_tensor(xb[xi][:, :], xb[xi][:, :], ALPHA, gb[:, :], ALU.mult, ALU.add),
                         reads=[r_gb], writes=[r_xb[xi]])
            for cpair in range(2):
                bank = mbank()
                bview = banks[bank][:, :].bitcast(BF16)

                def tr(e, cpair=cpair, bview=bview):
                    ins = None
                    for cc in range(2):
                        c = cpair * 2 + cc
                        for mi in range(4):
                            ins = e.transpose(bview[:, cc * 512 + mi * 128: cc * 512 + (mi + 1) * 128],
                                              On[:, mi, c * 128:(c + 1) * 128], identb[:, :])
                    return ins
                S.op("pe", tr, reads=[r_On, r_constb], writes=[bres[bank]])
                for cc in range(2):
                    c = cpair * 2 + cc
                    S.op("dve", lambda e, c=c, cc=cc, bview=bview, qb=qb: e.tensor_tensor(
                        yT[:, c, :], bview[:, cc * 512:(cc + 1) * 512], gattb[qb][:, c, :], ALU.mult),
                        reads=[bres[bank], r_gattb[qb]], writes=[r_yT[c]])
            def epilogue(G=G, Y=Y, ep_xi=ep_xi):
                for mi in range(4):
                    m = 4 * G + mi
                    xi = ep_xi[mi]
                    sp2 = m % 2
                    bks = [mbank(), mbank()]
                    for half in range(2):
                        def mm(e, half=half, mi=mi, bank=bks[half], Y=Y):
                            ins = None
                            for kc in range(8):
                                lhs = yT[:, kc, mi * 128:(mi + 1) * 128] if kc < 4 else scr[:, Y + kc - 4, mi * 128:(mi + 1) * 128]
                                ins = e.matmul(banks[bank][:, :], lhs, wA[:, kc, half * 512:(half + 1) * 512],
                                               start=(kc == 0), stop=(kc == 7))
                            return ins
                        S.op("pe", mm, reads=r_yT + [r_scr[Y + g] for g in range(4)] + [r_wA], writes=[bres[bks[half]]])
                        tk = tm_rot[0] % 2
                        tm_rot[0] += 1
                        S.op("dve", lambda e, half=half, bank=bks[half], tk=tk: e.tensor_tensor(
                            tmpfs[tk][:, :], banks[bank][:, :], gate_bc[:, half * 512:(half + 1) * 512], ALU.mult),
                            reads=[bres[bks[half]], r_gate], writes=[r_tmpfs[tk]])
                        S.op("dve", lambda e, half=half, xi=xi, tk=tk: e.tensor_tensor(
                            xb[xi][:, half * 512:(half + 1) * 512], xb[xi][:, half * 512:(half + 1) * 512], tmpfs[tk][:, :], ALU.add),
                            reads=[r_tmpfs[tk]], writes=[r_xb[xi]])
                        S.op("dve", lambda e, half=half, xi=xi, sp2=sp2: e.bn_stats(stats2[sp2][:, half, :], xb[xi][:, half * 512:(half + 1) * 512]),
                             reads=[r_xb[xi]], writes=[r_sm[sp2]])
                        yield
                    S.op("dve", lambda e, sp2=sp2: e.bn_aggr(mv2[sp2][:, :], stats2[sp2][:, :, :].rearrange("p a b -> p (a b)")),
                         writes=[r_sm[sp2]])
                    S.op("pool", lambda e, sp2=sp2: e.tensor_scalar(ve2[sp2][:, :], mv2[sp2][:, 1:2], EPS, 0.0, ALU.add, ALU.add),
                         reads=[r_sm[sp2]], writes=[r_sm[sp2]])
                    S.op("pool", lambda e, sp2=sp2: e.tensor_tensor(rstd2[sp2][:, :], ve2[sp2][:, :], mhalf[:, :], ALU.pow),
                         reads=[r_small], writes=[r_sm[sp2]])
                    S.op("dve", lambda e, xi=xi, sp2=sp2: e.tensor_scalar(xb[xi][:, :], xb[xi][:, :], mv2[sp2][:, 0:1], rstd2[sp2][:, 0:1],
                                                                         ALU.subtract, ALU.mult),
                         reads=[r_sm[sp2]], writes=[r_xb[xi], r_sm[sp2]])
                    S.op("pool", lambda e, xi=xi: e.tensor_tensor(xb[xi][:, :], xb[xi][:, :], lng[:, :], ALU.mult),
                         reads=[r_const], writes=[r_xb[xi]])
                    S.op("pool", lambda e, xi=xi: e.tensor_tensor(xb[xi][:, :], xb[xi][:, :], lnb[:, :], ALU.add),
                         reads=[r_const], writes=[r_xb[xi]])
                    S.dma("pool", lambda e, xi=xi, m=m: e.dma_start(out=out_d[m * 128:(m + 1) * 128, :], in_=xb[xi][:, :]),
                          d_out[xi], reads=[r_xb[xi]])
                    yield

            def last_epilogue(G=G, Y=Y, ep_xi=ep_xi):
                def part_a(mi):
                    m = 4 * G + mi
                    xi = ep_xi[mi]
                    sp2 = m % 2
                    bks = [mbank(), mbank()]
                    for half in range(2):
                        def mm(e, half=half, mi=mi, bank=bks[half], Y=Y):
                            ins = None
                            for kc in range(8):
                                lhs = yT[:, kc, mi * 128:(mi + 1) * 128] if kc < 4 else scr[:, Y + kc - 4, mi * 128:(mi + 1) * 128]
                                ins = e.matmul(banks[bank][:, :], lhs, wA[:, kc, half * 512:(half + 1) * 512],
                                               start=(kc == 0), stop=(kc == 7))
                            return ins
                        S.op("pe", mm, reads=r_yT + [r_scr[Y + g] for g in range(4)] + [r_wA], writes=[bres[bks[half]]])
                        tk = tm_rot[0] % 2
                        tm_rot[0] += 1
                        S.op("dve", lambda e, half=half, bank=bks[half], tk=tk: e.tensor_tensor(
                            tmpfs[tk][:, :], banks[bank][:, :], gate_bc[:, half * 512:(half + 1) * 512], ALU.mult),
                            reads=[bres[bks[half]], r_gate], writes=[r_tmpfs[tk]])
                        S.op("dve", lambda e, half=half, xi=xi, tk=tk: e.tensor_tensor(
                            xb[xi][:, half * 512:(half + 1) * 512], xb[xi][:, half * 512:(half + 1) * 512], tmpfs[tk][:, :], ALU.add),
                            reads=[r_tmpfs[tk]], writes=[r_xb[xi]])
                        S.op("dve", lambda e, half=half, xi=xi, sp2=sp2: e.bn_stats(stats2[sp2][:, half, :], xb[xi][:, half * 512:(half + 1) * 512]),
                             reads=[r_xb[xi]], writes=[r_sm[sp2]])
                    S.op("dve", lambda e, sp2=sp2: e.bn_aggr(mv2[sp2][:, :], stats2[sp2][:, :, :].rearrange("p a b -> p (a b)")),
                         writes=[r_sm[sp2]])
                    S.op("pool", lambda e, sp2=sp2: e.tensor_scalar(ve2[sp2][:, :], mv2[sp2][:, 1:2], EPS, 0.0, ALU.add, ALU.add),
                         reads=[r_sm[sp2]], writes=[r_sm[sp2]])
                    S.op("pool", lambda e, sp2=sp2: e.tensor_tensor(rstd2[sp2][:, :], ve2[sp2][:, :], mhalf[:, :], ALU.pow),
                         reads=[r_small], writes=[r_sm[sp2]])

                def part_b(mi):
                    m = 4 * G + mi
                    xi = ep_xi[mi]
                    sp2 = m % 2
                    S.op("dve", lambda e, xi=xi, sp2=sp2: e.tensor_scalar(xb[xi][:, :], xb[xi][:, :], mv2[sp2][:, 0:1], rstd2[sp2][:, 0:1],
                                                                         ALU.subtract, ALU.mult),
                         reads=[r_sm[sp2]], writes=[r_xb[xi], r_sm[sp2]])
                    S.op("pool", lambda e, xi=xi: e.tensor_tensor(xb[xi][:, :], xb[xi][:, :], lng[:, :], ALU.mult),
                         reads=[r_const], writes=[r_xb[xi]])
                    S.op("pool", lambda e, xi=xi: e.tensor_tensor(xb[xi][:, :], xb[xi][:, :], lnb[:, :], ALU.add),
                         reads=[r_const], writes=[r_xb[xi]])
                    S.dma("pool", lambda e, xi=xi, m=m: e.dma_start(out=out_d[m * 128:(m + 1) * 128, :], in_=xb[xi][:, :]),
                          d_out[xi], reads=[r_xb[xi]])

                part_a(0)
                for mi in range(1, 4):
                    part_a(mi)
                    part_b(mi - 1)
                part_b(3)

            if gi == len(GORD) - 1:
                last_epilogue()
            else:
                epg[0] = epilogue()
        S.wait_only("pool", [Tok(d.sem, d.n * 16, "dma", d.key) for d in d_out])
        if debug:
            S.wait_only("sp", [Tok(S.sem[en], S.cnt[en], en, en) for en in ("pe", "act", "dve", "pool")]
                        + [Tok(d.sem, d.n * 16, "dma", d.key) for d in d_out])
            dumps = {"adaT": (adaT, F32), "scale1": (scale1, F32), "gate_bc": (gate_bc, F32), "gb": (gb, F32),
                     "KT": (KT, BF16), "V": (V, BF16), "Cpos": (Cpos, F32), "nlf": (nlf, F32), "biasG1": (biasG[1], F32),
                     "QT": (QT, BF16), "gatt": (gatt, BF16), "yT": (yT, BF16), "scr": (scr, BF16), "gatt1": (gatt1, BF16), "phalo": (phalo, BF16),
                     "wA": (wA, BF16), "hb1": (hb[1], F32), "uT1": (uT[1], BF16), "uT0": (uT[0], BF16), "biasG0": (biasG[0], F32), "sc": (sc, F32), "tot": (tot, F32),
                     "Z": (Z, F32), "bvfp": (bvfp, F32), "masks": (masks, BF16), "bm": (bm, BF16)}
            for nm, (tl, dt) in dumps.items():
                shp = list(tl.shape)
                dd = nc.dram_tensor("dbg_" + nm, shp, dt, kind="ExternalOutput").ap()
                full = tuple(slice(None) for _ in shp)
                S.dma("sp", lambda e, dd=dd, tl=tl, full=full: e.dma_start(out=dd[full], in_=tl[full]), d_tmp)
            S.wait_only("sp", [Tok(d_tmp.sem, d_tmp.n * 16, "dma", d_tmp.key)])

        with nc.Block() as block:
            @block.tensor
            def _(e):
                S.replay("pe", e)

            @block.scalar
            def _(e):
                S.replay("act", e)

            @block.vector
            def _(e):
                S.replay("dve", e)

            @block.gpsimd
            def _(e):
                S.replay("pool", e)

            @block.sync
            def _(e):
                S.replay("sp", e)
    return nc


def _consts(par):
    ident = np.eye(128, dtype=np.float32)
    ones = np.ones((128, 128), np.float32)
    s = np.arange(128)[:, None]
    t = np.arange(128)[None, :]
    U = (s <= t).astype(np.float32)
    glob = np.array([2 * p + par if p < 16 else 2 * (p - 16) + 1 - par for p in range(32)])
    pred = (glob[:, None] < glob[None, :]).astype(np.float32)
    masks = np.zeros((128, 2, 128), np.float32)
    masks[:, 0, :] = np.where(s <= t, 0.0, NEG)
    masks[:, 1, :] = 0.0 if par == 1 else NEG
    bm = np.zeros((128, 8, 128), np.float32)
    bh = np.zeros((128, 36, 16), np.float32)
    eye = np.eye(128, dtype=np.float32)
    for g, w in enumerate(WINS):
        inwin = ((t - s) >= 0) & ((t - s) < w)
        bm[:, g, :] = np.where(inwin, 1.0 / w, 0.0) - eye
        if par == 0:
            cnt = np.minimum(t + 1, w).astype(np.float32)
            bm[:, 4 + g, :] = np.where(inwin, 1.0 / cnt, 0.0) - eye
        else:
            bm[:, 4 + g, :] = bm[:, g, :]
        for j in range(8):
            for i in range(16):
                for tt in range(16):
                    if tt + 16 - i < w:
                        bh[j * 16 + i, j * 4 + g, tt] = 1.0 / w
        if par == 1:
            bh[:, 32 + g, :] = bh[:, 0 * 4 + g, :]
    return ident, ones, U, pred, masks, bm, bh


def _colT(v, n):
    return np.ascontiguousarray(np.asarray(v, np.float32).reshape(n, 128).T)


_NC_CACHE = {}
_DEBUG = [False]
_NG = [4]


def kernel(x, c, w_ada, b_ada, w_in, b_in, w_pool_mix, b_pool_mix, pool_scale, w_out, b_out, ln_g, ln_b):
    x = np.asarray(x, np.float32)
    c = np.asarray(c, np.float32)
    w_ada = np.ascontiguousarray(np.asarray(w_ada, np.float32)[0])
    b_ada = np.asarray(b_ada, np.float32)[0]
    w_in = np.ascontiguousarray(np.asarray(w_in, np.float32)[0])
    b_in = np.asarray(b_in, np.float32)[0]
    w_pm = np.ascontiguousarray(np.asarray(w_pool_mix, np.float32)[0])
    b_pm = np.asarray(b_pool_mix, np.float32)[0]
    psc = np.asarray(pool_scale, np.float32)[0]
    w_out = np.ascontiguousarray(np.asarray(w_out, np.float32)[0])
    b_out = np.asarray(b_out, np.float32)[0]
    ln_g = np.asarray(ln_g, np.float32)[0]
    ln_b = np.asarray(ln_b, np.float32)[0]

    common = {
        "w_ada": w_ada,
        "b_adaT": _colT(b_ada[0:2048], 16),
        "b_gate": np.ascontiguousarray(b_ada[2048:3072].reshape(1, D)),
        "w_in": w_in,
        "b_qT": _colT(b_in[0:512], 4),
        "b_kT": _colT(b_in[512:1024], 4),
        "b_gT": _colT(b_in[2056:3080], 8),
        "b_vfp": np.ascontiguousarray(b_in[1024:2056].reshape(1, 1032)),
        "w_pm": w_pm,
        "b_pmT": np.ascontiguousarray(b_pm.T),
        "pscT": _colT(psc, 4),
        "w_out": w_out,
        "b_out": np.ascontiguousarray(b_out.reshape(1, D)),
        "ln_g": np.ascontiguousarray(ln_g.reshape(1, D)),
        "ln_b": np.ascontiguousarray(ln_b.reshape(1, D)),
    }
    in_maps = []
    for core in range(NCORES):
        b, par = core // 2, core % 2
        xb_ = x[b].reshape(NBLK, 128, D)
        own = [2 * m + par for m in range(16)]
        oth = [2 * m + 1 - par for m in range(16)]
        xp = np.ascontiguousarray(xb_[own + oth].reshape(SEQ, D))
        xh = np.zeros((256, D), np.float32)
        for m in range(16):
            g = own[m]
            if g > 0:
                xh[m * 16:(m + 1) * 16] = x[b, g * 128 - 16:g * 128]
        ident, ones, U, pred, masks, bm, bh = _consts(par)
        mp = dict(common)
        mp.update({"xp": np.ascontiguousarray(xp[:2048]), "xpT": np.ascontiguousarray(xp.T), "xhT": np.ascontiguousarray(xh.T), "cT": _colT(c[b], 8), "ident": ident, "ones": ones, "U": U,
                   "pred": pred, "masks": masks, "bm": bm, "bh": bh})
        in_maps.append(mp)

    if "nc" not in _NC_CACHE:
        _NC_CACHE["nc"] = build_nc(_DEBUG[0])
    nc = _NC_CACHE["nc"]
    res = run_bass_kernel_spmd(nc, in_maps, core_ids=list(range(NCORES)))
    if _DEBUG[0]:
        _DEBUG.append(res.results)
    out = np.empty((4, SEQ, D), np.float32)
    for core in range(NCORES):
        b, par = core // 2, core % 2
        o = np.asarray(res.results[core]["out"], np.float32).reshape(16, 128, D)
        for m in range(16):
            g = 2 * m + par
            out[b, g * 128:(g + 1) * 128] = o[m]
    return out
```

```python
import numpy as np
from contextlib import ExitStack
import concourse.bass as bass
import concourse.mybir as mybir
from concourse.bass_utils import run_bass_kernel_spmd

F32 = mybir.dt.float32
BF16 = mybir.dt.bfloat16
AF = mybir.ActivationFunctionType
ALU = mybir.AluOpType

NCORES = 8
SEQ = 4096
D = 1024
NBLK = 32
ALPHA = float(2.0 ** 0.25)
EPS = 1e-5
NEG = -30000.0
WINS = (2, 4, 8, 16)


class Tok:
    __slots__ = ("sem", "val", "eng", "key")

    def __init__(self, sem, val, eng, key):
        self.sem, self.val, self.eng, self.key = sem, val, eng, key


class Res:
    def __init__(self, name, track_reads=True):
        self.name = name
        self.w = None
        self.r = []
        self.track = track_reads


class DSem:
    def __init__(self, sem, key):
        self.sem, self.n, self.key = sem, 0, key


class Sched:
    ENGS = ("pe", "act", "dve", "pool", "sp")

    def __init__(self, nc, es):
        self.nc = nc
        self.es = es
        self.q = {e: [] for e in self.ENGS}
        self.sem = {e: es.enter_context(nc.semaphore("s_" + e)) for e in self.ENGS}
        self.cnt = {e: 0 for e in self.ENGS}
        self.seen = {e: {} for e in self.ENGS}
        self.nd = 0

    def dsem(self, name):
        self.nd += 1
        return DSem(self.es.enter_context(self.nc.semaphore("d_" + name)), "d%d" % self.nd)

    def _waits(self, eng, reads, writes, extra, is_dma=False):
        need = {}

        def add(t):
            if t is None:
                return
            if t.eng == eng and eng == "pe" and not is_dma:
                return
            if self.seen[eng].get(t.key, 0) >= t.val:
                return
            if need.get(t.key, (None, 0))[1] < t.val:
                need[t.key] = (t.sem, t.val)

        for r in reads:
            add(r.w)
        for w in writes:
            add(w.w)
            for t in w.r:
                add(t)
        for t in extra:
            add(t)
        for k, (s, v) in need.items():
            self.seen[eng][k] = v
        return list(need.values())

    def _commit(self, tok, reads, writes):
        for w in writes:
            w.w = tok
            w.r = []
        for r in reads:
            if r.track:
                r.r.append(tok)

    def op(self, eng, fn, reads=(), writes=(), extra=()):
        waits = self._waits(eng, reads, writes, extra)
        self.cnt[eng] += 1
        tok = Tok(self.sem[eng], self.cnt[eng], eng, eng)
        self.q[eng].append((waits, fn, (self.sem[eng], 1)))
        self._commit(tok, reads, writes)
        return tok

    def dma(self, eng, fn, ds, reads=(), writes=(), extra=()):
        waits = self._waits(eng, reads, writes, extra, is_dma=True)
        ds.n += 1
        tok = Tok(ds.sem, ds.n * 16, "dma", ds.key)
        self.q[eng].append((waits, fn, (ds.sem, 16)))
        self._commit(tok, reads, writes)
        return tok

    def wait_only(self, eng, toks):
        waits = self._waits(eng, (), (), toks)
        self.q[eng].append((waits, None, None))

    def replay(self, eng, e):
        for waits, fn, inc in self.q[eng]:
            for s, v in waits:
                e.wait_ge(s, v)
            if fn is not None:
                ins = fn(e)
                ins.then_inc(inc[0], inc[1])


def build_nc(debug=False):
    nc = bass.Bass("TRN2", target_bir_lowering=False)

    def din(name, shape):
        return nc.dram_tensor(name, list(shape), F32, kind="ExternalInput").ap()

    xp = din("xp", [2048, D])
    xpT = din("xpT", [D, SEQ])
    xhT = din("xhT", [D, 256])
    cT_d = din("cT", [128, 8])
    w_ada = din("w_ada", [D, 3 * D])
    b_adaT_d = din("b_adaT", [128, 16])
    b_gate_d = din("b_gate", [1, D])
    w_in = din("w_in", [D, 3080])
    b_qT_d = din("b_qT", [128, 4])
    b_kT_d = din("b_kT", [128, 4])
    b_gT_d = din("b_gT", [128, 8])
    b_vfp_d = din("b_vfp", [1, 1032])
    w_pm_d = din("w_pm", [4, 128, 128])
    b_pmT_d = din("b_pmT", [128, 4])
    pscT_d = din("pscT", [128, 4])
    w_out_d = din("w_out", [D, D])
    b_out_d = din("b_out", [1, D])
    ln_g_d = din("ln_g", [1, D])
    ln_b_d = din("ln_b", [1, D])
    ident_d = din("ident", [128, 128])
    ones_d = din("ones", [128, 128])
    U_d = din("U", [128, 128])
    pred_d = din("pred", [32, 32])
    masks_d = din("masks", [128, 2, 128])
    bm_d = din("bm", [128, 8, 128])
    bh_d = din("bh", [128, 36, 16])
    out_d = nc.dram_tensor("out", [2048, D], F32, kind="ExternalOutput").ap()

    wbf = nc.dram_tensor("wbf", [4, 128, 8, 512], BF16, kind="Internal").ap()
    xpT_v = xpT.rearrange("(kc p) t -> p kc t", p=128)
    xhT_v = xhT.rearrange("(kc p) t -> p kc t", p=128)
    w_ada_v = w_ada.rearrange("(kc p) e -> p kc e", p=128)
    w_in_v = w_in.rearrange("(kc p) e -> p kc e", p=128)
    w_out_v = w_out_d.rearrange("(kc p) e -> p kc e", p=128)

    with ExitStack() as es:
        S = Sched(nc, es)

        def sb(name, shape, dt=F32):
            return es.enter_context(nc.sbuf_tensor("sb_" + name, list(shape), dt))

        banks = [es.enter_context(nc.psum_tensor("ps%d" % i, [128, 512], F32)) for i in range(8)]
        bres = [Res("bank%d" % i) for i in range(8)]

        nlf = sb("nlf", [128, NBLK, 8])
        Cpos = sb("Cpos", [128, NBLK, 8])
        biasG = [sb("biasG%d" % i, [128, NBLK, 8]) for i in range(2)]
        ident = sb("ident", [128, 128])
        onesf = sb("onesf", [128, 128])
        Uf = sb("Uf", [128, 128])
        predf = sb("predf", [32, 32])
        tot = sb("tot", [32, 8])
        Z = sb("Z", [32, 256])
        identb = sb("identb", [128, 128], BF16)
        masks = sb("masks", [128, 2, 128], BF16)
        bm = sb("bm", [128, 8, 128], BF16)
        bh = sb("bh", [128, 36, 16], BF16)
        gate_bc = sb("gate_bc", [128, D])
        gb = sb("gb", [128, D])
        lng = sb("lng", [128, D])
        lnb = sb("lnb", [128, D])
        bvfp = sb("bvfp", [128, 1032])
        adaT = sb("adaT", [128, 16])
        scale1 = sb("scale1", [128, 8])
        b_adaT = sb("b_adaT", [128, 16])
        b_qT = sb("b_qT", [128, 4])
        b_kT = sb("b_kT", [128, 4])
        b_gT = sb("b_gT", [128, 8])
        b_pmT = sb("b_pmT", [128, 4])
        pscT = sb("pscT", [128, 4])
        cT = sb("cT", [128, 8])
        sc = sb("sc", [128, 8])
        phalo = sb("phalo", [128, 2, 512], BF16)
        wpm = sb("wpm", [128, 4, 128], BF16)
        NXB = 4
        xb = [sb("xb%d" % i, [128, D]) for i in range(NXB)]
        uT = [sb("uT%d" % i, [128, 8, 512], BF16) for i in range(2)]
        wA = sb("wA", [128, 8, 1032], BF16)
        wS = [sb("wS%d" % i, [128, 8, 512], BF16) for i in range(2)]
        QT = sb("QT", [128, 8, 512], BF16)
        gatt = sb("gatt", [128, 4, 512], BF16)
        scr = sb("scr", [128, 12, 512], BF16)
        yT = sb("yT", [128, 4, 512], BF16)
        gatt1 = sb("gatt1", [128, 4, 512], BF16)
        stats2 = [sb("stats2_%d" % i, [128, 2, 6]) for i in range(2)]
        mv2 = [sb("mv2_%d" % i, [128, 2]) for i in range(2)]
        ve2 = [sb("ve2_%d" % i, [128, 1]) for i in range(2)]
        rstd2 = [sb("rstd2_%d" % i, [128, 1]) for i in range(2)]
        mhalf = sb("mhalf", [128, 1])
        r_sm = [Res("sm0"), Res("sm1")]
        nbg = sb("nbg", [128, 8])
        tmpf = sb("tmpf", [128, 512])
        tmpf2 = sb("tmpf2", [128, 512])
        hb = [sb("hb%d" % i, [128, D]) for i in range(2)]
        rl = sb("rl", [128, 4])
        fl = sb("fl", [128, 4, 8])
        ex = sb("ex", [128, 4, 8])

        r_KT = Res("KT", False)
        r_V = Res("V", False)
        r_nlf = Res("nlf")
        r_Cpos = Res("Cpos", False)
        r_biasG = [Res("biasG0"), Res("biasG1")]
        r_const = Res("const", False)
        r_constb = Res("constb", False)
        r_ada = Res("ada", False)
        r_gate = Res("gate", False)
        r_gb = Res("gb", False)
        r_sc = Res("sc", False)
        r_xb = [Res("xb%d" % i) for i in range(NXB)]
        r_uT = [Res("uT0"), Res("uT1")]
        r_wA = Res("wA")
        r_wS = [Res("wS0"), Res("wS1")]
        r_QT = Res("QT")
        r_gatt = Res("gatt")
        r_scr = [Res("scr%d" % i) for i in range(12)]
        r_yT = [Res("yT%d" % i) for i in range(4)]
        r_small2 = Res("small2", False)
        r_tmpf = Res("tmpf")
        r_tmpf2 = Res("tmpf2")
        tmpfs = [tmpf, tmpf2]
        r_tmpfs = [r_tmpf, r_tmpf2]
        r_hb = [Res("hb0"), Res("hb1")]
        r_small = Res("small")
        r_rl = Res("rl")
        r_phalo = Res("phalo", False)
        r_fl = Res("fl")
        r_tot = Res("tot")
        r_Z = Res("Z")

        d_const = S.dsem("const")
        d_constb = S.dsem("constb")
        d_xb = [S.dsem("xb%d" % i) for i in range(NXB)]
        d_wa = [S.dsem("wa%d" % i) for i in range(3)]
        d_wA = S.dsem("wA")
        d_wS = [S.dsem("wS0"), S.dsem("wS1")]
        d_out = [S.dsem("out%d" % i) for i in range(4)]
        d_tmp = S.dsem("tmp")

        def cload(dst, src, eng="act"):
            r_const.w = S.dma(eng, lambda e, dst=dst, src=src: e.dma_start(out=dst, in_=src), d_const)

        r_c0 = Res("c0", False)
        d_c0 = S.dsem("c0")
        S.dma("act", lambda e: e.dma_start(out=cT[:, :], in_=cT_d[:, :]), d_c0)
        r_c0.w = S.dma("act", lambda e: e.dma_start(out=b_adaT[:, :], in_=b_adaT_d[:, :]), d_c0)
        S.op("act", lambda e: e.activation(sc[:, :], cT[:, :], AF.Silu), reads=[r_c0], writes=[r_sc])
        cload(ident[:, :], ident_d[:, :])
        cload(onesf[:, :], ones_d[:, :])
        cload(Uf[:, :], U_d[:, :])
        cload(predf[:, :], pred_d[:, :])
        cload(b_qT[:, :], b_qT_d[:, :])
        cload(b_kT[:, :], b_kT_d[:, :])
        cload(b_gT[:, :], b_gT_d[:, :])
        cload(b_pmT[:, :], b_pmT_d[:, :])
        cload(pscT[:, :], pscT_d[:, :])
        cload(bvfp[:, :], b_vfp_d.partition_broadcast(128))
        cload(lng[:, :], ln_g_d.partition_broadcast(128))
        cload(lnb[:, :], ln_b_d.partition_broadcast(128))
        cload(gb[:, :], b_out_d.partition_broadcast(128))
        cload(gate_bc[:, :], b_gate_d.partition_broadcast(128))

        def cloadb(dst, src):
            r_constb.w = S.dma("pool", lambda e, dst=dst, src=src: e.dma_start(out=dst, in_=src), d_constb)

        cloadb(identb[:, :], ident_d[:, :])
        cloadb(masks[:, :, :], masks_d[:, :, :])
        cloadb(bm[:, :, :], bm_d[:, :, :])
        cloadb(bh[:, :, :], bh_d[:, :, :])
        cloadb(wpm[:, :, :], w_pm_d.rearrange("g c e -> c g e"))
        for half in range(2):
            r_wA.w = S.dma("pool", lambda e, half=half: e.dma_start(out=wA[:, half * 4:(half + 1) * 4, :],
                                                                   in_=w_in_v[:, half * 4:(half + 1) * 4, 512:1544]),
                           d_wA)

        S.op("dve", lambda e: e.memset(mhalf[:, :], -0.5), writes=[r_small])
        r_wbf = [Res("wbf%d" % i, False) for i in range(4)]
        d_wbf = [S.dsem("wbf%d" % i) for i in range(4)]
        for hh in range(2):
            S.op("pool", lambda e, hh=hh: e.memset(QT[(1 - hh) * 64:(2 - hh) * 64, hh:8:2, :], 0.0), writes=[r_QT])

        with ExitStack() as es2:
            wa = [es2.enter_context(nc.sbuf_tensor("wa%d" % i, [128, 8, 512], F32)) for i in range(3)]
            scbc = es2.enter_context(nc.sbuf_tensor("scbc", [128, 8, 128], F32))
            r_wa = [Res("wa%d" % i) for i in range(3)]
            r_scbc = Res("scbc")

            S.op("dve", lambda e: e.tensor_copy(scbc[:, :, :], sc[:, :].unsqueeze(2).to_broadcast([128, 8, 128])),
                 reads=[r_sc], writes=[r_scbc])
            psA = banks[7]
            for j in range(6):
                buf = wa[j % 3]
                S.dma("sp", lambda e, buf=buf, j=j: e.dma_start(out=buf[:, :, :], in_=w_ada_v[:, :, j * 512:(j + 1) * 512]),
                      d_wa[j % 3], writes=[r_wa[j % 3]])
                if j < 4:
                    def mm(e, buf=buf, j=j):
                        ins = None
                        for i in range(4):
                            fc = j * 4 + i
                            for kc in range(8):
                                ins = e.matmul(psA[:, fc:fc + 1], buf[:, kc, i * 128:(i + 1) * 128], sc[:, kc:kc + 1],
                                               start=(kc == 0), stop=(kc == 7))
                        return ins
                    S.op("pe", mm, reads=[r_wa[j % 3], r_sc], writes=[bres[7]])
                else:
                    bk = 5 + (j - 4)

                    def mm(e, buf=buf, bk=bk):
                        ins = None
                        for kc in range(8):
                            ins = e.matmul(banks[bk][:, :], scbc[:, kc, :], buf[:, kc, :], start=(kc == 0), stop=(kc == 7))
                        return ins
                    S.op("pe", mm, reads=[r_wa[j % 3], r_scbc], writes=[bres[bk]])
                    half = j - 4
                    S.op("dve", lambda e, bk=bk, half=half: e.tensor_tensor(
                        gate_bc[:, half * 512:(half + 1) * 512], banks[bk][:, :],
                        gate_bc[:, half * 512:(half + 1) * 512], ALU.add),
                        reads=[bres[bk], r_const], writes=[r_gate])
                if j == 3:
                    S.op("dve", lambda e: e.tensor_tensor(adaT[:, :], psA[:, 0:16], b_adaT[:, :], ALU.add),
                         reads=[bres[7], r_c0], writes=[r_ada])
                    S.op("dve", lambda e: e.tensor_scalar_add(scale1[:, :], adaT[:, 8:16], 1.0), writes=[r_ada])
            S.op("pool", lambda e: e.tensor_tensor(gb[:, :], gb[:, :], gate_bc[:, :], ALU.mult),
                 reads=[r_gate, r_const], writes=[r_gb])
            t_end_ada = S.op("pe", lambda e: e.matmul(banks[7][:, 0:1], onesf[:, 0:128], onesf[:, 0:1], start=True, stop=True),
                             reads=[r_const], writes=[bres[7]])

        shiftT = adaT
        KT = sb("KT", [128, 4, SEQ], BF16)
        V = sb("V", [128, NBLK, 8, 66], BF16)
        S.op("pool", lambda e: e.memset(V[:, :, :, 64], 1.0), writes=[r_V], extra=[t_end_ada])

        xb_rot = [0]

        def load_x(src_rows, q="sp"):
            i = xb_rot[0] % NXB
            xb_rot[0] += 1
            S.dma(q, lambda e, i=i, src_rows=src_rows: e.dma_start(out=xb[i][:, :], in_=src_rows), d_xb[i],
                  writes=[r_xb[i]])
            return i

        def load_feat(src_v, tok0, ntok, q="sp"):
            per = 1024 // ntok
            bufs = []
            for kc0 in range(0, 8, per):
                i = xb_rot[0] % NXB
                xb_rot[0] += 1
                S.dma(q, lambda e, i=i, kc0=kc0: e.dma_start(
                    out=xb[i][:, :].rearrange("p (a b) -> p a b", b=ntok), in_=src_v[:, kc0:kc0 + per, tok0:tok0 + ntok]),
                    d_xb[i], writes=[r_xb[i]])
                bufs.append(i)
            return bufs

        def modulate_sb(kc, bufs, ntok, ut, eng):
            per = 1024 // ntok
            xi = bufs[kc // per]
            off = (kc % per) * ntok
            if eng == "act":
                S.op("act", lambda e: e.activation(uT[ut][:, kc, 0:ntok], xb[xi][:, off:off + ntok], AF.Identity,
                                                   bias=shiftT[:, kc:kc + 1], scale=scale1[:, kc:kc + 1]),
                     reads=[r_xb[xi], r_ada], writes=[r_uT[ut]])
            else:
                S.op(eng, lambda e: e.tensor_scalar(uT[ut][:, kc, 0:ntok], xb[xi][:, off:off + ntok],
                                                    scale1[:, kc:kc + 1], shiftT[:, kc:kc + 1], ALU.mult, ALU.add),
                     reads=[r_xb[xi], r_ada], writes=[r_uT[ut]])

        ENG_MIX = ["dve", "act", "pool", "act", "dve", "act", "pool", "act"]

        rotT = [0]
        rotKV = [0]
        KVB = [0, 1, 2, 3, 4, 5, 7]
        def emit_T(ch):
            bufs = load_feat(xpT_v, ch * 512, 512, "act" if ch == 0 else "sp")
            for kc in range(8):
                modulate_sb(kc, bufs, 512, ch % 2, "act" if ch == 0 else ENG_MIX[kc])

        def emit_K(ch):
            ut = ch % 2
            for pair in range(4):
                bank = KVB[rotKV[0] % len(KVB)]
                rotKV[0] += 1

                def mm(e, pair=pair, bank=bank, ut=ut):
                    ins = None
                    for kc in range(8):
                        ins = e.matmul(banks[bank][:, :], wA[:, kc, pair * 128:(pair + 1) * 128], uT[ut][:, kc, :],
                                       start=(kc == 0), stop=(kc == 7))
                    return ins
                S.op("pe", mm, reads=[r_wA, r_uT[ut]], writes=[bres[bank]])
                S.op("act", lambda e, pair=pair, bank=bank, ch=ch: e.activation(
                    KT[:, pair, ch * 512:(ch + 1) * 512], banks[bank][:, :], AF.Identity, bias=b_kT[:, pair:pair + 1], scale=1.0),
                    reads=[bres[bank], r_const], writes=[r_KT], extra=[t_end_ada])

        def emit_V(ch):
            ut = ch % 2
            for bi in range(4):
                pos = ch * 4 + bi
                bank = KVB[rotKV[0] % len(KVB)]
                rotKV[0] += 1

                def mm(e, bi=bi, bank=bank, ut=ut):
                    ins = None
                    for kc in range(8):
                        ins = e.matmul(banks[bank][:, :], uT[ut][:, kc, bi * 128:(bi + 1) * 128], wA[:, kc, 512:1024],
                                       start=(kc == 0), stop=(kc == 7))
                    return ins
                S.op("pe", mm, reads=[r_wA, r_uT[ut]], writes=[bres[bank]])
                S.op("dve", lambda e, pos=pos, bank=bank: e.tensor_tensor(
                    V[:, pos, :, 0:64], banks[bank][:, :].rearrange("p (h d) -> p h d", d=64),
                    bvfp[:, 0:512].rearrange("p (h d) -> p h d", d=64), ALU.add),
                    reads=[bres[bank], r_const], writes=[r_V], extra=[t_end_ada])

            def mmf(e, ut=ut):
                ins = None
                for bi in range(4):
                    for kc in range(8):
                        ins = e.matmul(banks[6][:, bi * 8:(bi + 1) * 8], uT[ut][:, kc, bi * 128:(bi + 1) * 128],
                                       wA[:, kc, 1024:1032], start=(kc == 0), stop=(kc == 7))
                return ins
            S.op("pe", mmf, reads=[r_wA, r_uT[ut]], writes=[bres[6]])
            S.op("dve", lambda e: e.tensor_tensor(
                fl[:, :, :], banks[6][:, 0:32].rearrange("p (b h) -> p b h", h=8),
                bvfp[:, 512:520].unsqueeze(1).to_broadcast([128, 4, 8]), ALU.add),
                reads=[bres[6], r_const], writes=[r_fl])
            S.op("act", lambda e: e.activation(ex[:, :, :], fl[:, :, :], AF.Exp, scale=-1.0), reads=[r_fl], writes=[r_small])
            S.op("act", lambda e, ch=ch: e.activation(nlf[:, ch * 4:(ch + 1) * 4, :], ex[:, :, :], AF.Ln, bias=1.0, scale=1.0),
                 reads=[r_small], writes=[r_nlf, r_fl])

        emit_T(0)
        for ch in range(8):
            emit_K(ch)
            if ch == 2:
                for c0, pc in ((0, 0), (2056, 1), (1544, 3), (2568, 2)):
                    S.dma("pool", lambda e, c0=c0, pc=pc: e.dma_start(out=wbf[pc, :, :, :], in_=w_in_v[:, :, c0:c0 + 512]),
                          d_wbf[pc], writes=[r_wbf[pc]], extra=[r_KT.w])
            if ch + 1 < 8:
                emit_T(ch + 1)
            emit_V(ch)

        def mmtot(e):
            ins = None
            for h in range(8):
                ins = e.matmul(banks[0][0:32, 256 + h:257 + h], nlf[:, :, h], onesf[:, 0:1], start=True, stop=True)
            return ins
        def cum1():
            S.op("pe", mmtot, reads=[r_nlf, r_const], writes=[bres[0]])
            S.op("dve", lambda e: e.tensor_copy(tot[:, :], banks[0][0:32, 256:264]), reads=[bres[0]], writes=[r_tot])
            S.op("dve", lambda e: e.tensor_tensor(
                Z[:, :].rearrange("p (b h) -> p b h", h=8), tot[:, :].unsqueeze(1).to_broadcast([32, 32, 8]),
                predf[:, :].unsqueeze(2).to_broadcast([32, 32, 8]), ALU.mult), reads=[r_tot, r_const], writes=[r_Z])

        def mmcum(e):
            e.matmul(banks[0][:, 0:256], Uf[:, :], nlf[:, :, :].rearrange("p b h -> p (b h)"), start=True, stop=False)
            return e.matmul(banks[0][:, 0:256], onesf[0:32, :], Z[:, :], start=False, stop=True)
        def cum2():
            S.op("pe", mmcum, reads=[r_nlf, r_Z, r_const], writes=[bres[0]])
            S.op("dve", lambda e: e.tensor_copy(Cpos[:, :, :].rearrange("p b h -> p (b h)"), banks[0][:, 0:256]),
                 reads=[bres[0]], writes=[r_Cpos])

        SB_ = [0, 1, 2, 3]
        OB_ = [4, 5]
        MB_ = [6, 7]
        rotM = [0]
        rotW = [0]

        def mbank():
            b = MB_[rotM[0] % len(MB_)]
            rotM[0] += 1
            return b

        PIECE = {0: 0, 2056: 1, 2568: 2, 1544: 3}

        def load_ws(c0, first=False):
            i = rotW[0] % 2
            rotW[0] += 1
            pc = PIECE[c0]
            S.dma("pool", lambda e, i=i, pc=pc: e.dma_start(out=wS[i][:, :, :], in_=wbf[pc, :, :, :]),
                  d_wS[i], reads=[r_wbf[pc]], writes=[r_wS[i]])
            return i

        PCOL = 1544
        QCOL = 0
        GACOL = 2056
        GPCOL = 2568

        On = hb[0][:, :].bitcast(BF16).rearrange("p (a b) -> p a b", b=512)
        PTv = hb[1][:, :].bitcast(BF16).rearrange("p (a b) -> p a b", b=512)
        r_On = Res("On")
        r_pt = [Res("pt%d" % i) for i in range(4)]
        QTb = [QT, uT[1]]
        r_QTb = [r_QT, r_uT[1]]
        gattb = [gatt, gatt1]
        r_gattb = [r_gatt, Res("gatt1")]
        nb_gT = nbg
        S.op("dve", lambda e: e.tensor_scalar(nb_gT[:, :], b_gT[:, :], 0.5, None, ALU.mult), reads=[r_const], writes=[r_small2])

        tm_rot = [0]

        def silu_evac(bank, c8, out_ap, out_res):
            ti = tm_rot[0] % 2
            tm_rot[0] += 1
            tm, r_tm = tmpfs[ti], r_tmpfs[ti]
            S.op("act", lambda e: e.activation(tm[:, :], banks[bank][:, :], AF.Tanh, bias=nb_gT[:, c8:c8 + 1], scale=0.5),
                 reads=[bres[bank], r_small2], writes=[r_tm])
            S.op("pool", lambda e: e.tensor_scalar(tm[:, :], tm[:, :], 0.5, 0.5, ALU.mult, ALU.add), reads=[r_tm], writes=[r_tm])
            S.op("dve", lambda e: e.scalar_tensor_tensor(out_ap, banks[bank][:, :], b_gT[:, c8:c8 + 1], tm[:, :], ALU.add, ALU.mult),
                 reads=[bres[bank], r_tm, r_const], writes=out_res)

        def group2(lhs_fn, rhs_fn, rd, bank):
            for part in range(2):
                def mm(e, part=part):
                    ins = None
                    for kc in range(part * 4, part * 4 + 4):
                        ins = e.matmul(banks[bank][:, :], lhs_fn(kc), rhs_fn(kc), start=(kc == 0), stop=(kc == 7))
                    return ins
                S.op("pe", mm, reads=rd, writes=[bres[bank]])
                yield

        mods_done = [False]
        GORD = [3, 2, 1, 0]

        def emit_biasG(G):
            bG = G % 2
            bank = mbank()
            S.op("pe", lambda e, bank=bank, G=G: e.matmul(banks[bank][:, 0:8], onesf[:, :], Cpos[:, 4 * G + 2, :], start=True, stop=True),
                 reads=[r_Cpos, r_const], writes=[bres[bank]])
            S.op("dve", lambda e, bank=bank, bG=bG: e.scalar_tensor_tensor(
                biasG[bG][:, :, :], banks[bank][:, 0:8].unsqueeze(1).to_broadcast([128, NBLK, 8]), -1.0 / 128.0,
                Cpos[:, :, :], ALU.mult, ALU.add), reads=[bres[bank], r_Cpos], writes=[r_biasG[bG]])

        def prelude(G, overlapped, n_sp=14):
            qb = (G + 1) % 2
            X, Y = 4 * ((G + 2) % 3), 4 * (G % 3)
            wq = load_ws(QCOL, G == 0)
            wga = load_ws(GACOL, G == 0)
            bufs = load_feat(xpT_v, G * 512, 512)
            for _ in range(n_sp):
                yield
            for kc in range(8):
                modulate_sb(kc, bufs, 512, 0, "pool" if overlapped else ENG_MIX[kc])
                if kc % 2 == 1:
                    yield
            mods_done[0] = True
            if G == GORD[0]:
                hbufs = load_feat(xhT_v, 0, 256)
                for kc in range(8):
                    modulate_sb(kc, hbufs, 256, 1, ENG_MIX[kc])
            if G == GORD[1]:
                for hh in range(2):
                    S.op("pool", lambda e, hh=hh: e.memset(uT[1][(1 - hh) * 64:(2 - hh) * 64, hh:8:2, :], 0.0), writes=[r_uT[1]])
            for c in range(4):
                bank = mbank()
                for _ in group2(lambda kc, c=c: wS[wq][:, kc, c * 128:(c + 1) * 128], lambda kc: uT[0][:, kc, :],
                                  [r_wS[wq], r_uT[0]], bank):
                    pass
                for hh in range(2):
                    S.op("dve", lambda e, hh=hh, c=c, bank=bank: e.tensor_scalar(
                        QTb[qb][hh * 64:(hh + 1) * 64, 2 * c + hh, :], banks[bank][hh * 64:(hh + 1) * 64, :],
                        b_qT[hh * 64:(hh + 1) * 64, c:c + 1], None, ALU.add),
                        reads=[bres[bank], r_const], writes=[r_QTb[qb]])
                yield
            wp = load_ws(PCOL, G == 0)
            for c in range(4):
                bank = mbank()
                for _ in group2(lambda kc, c=c: wS[wga][:, kc, c * 128:(c + 1) * 128], lambda kc: uT[0][:, kc, :],
                                  [r_wS[wga], r_uT[0]], bank):
                    pass
                silu_evac(bank, c, gattb[qb][:, c, :], [r_gattb[qb]])
                yield
            wgp = load_ws(GPCOL, G == 0)
            for mi in range(4):
                bank = mbank()
                for _ in group2(lambda kc, mi=mi: uT[0][:, kc, mi * 128:(mi + 1) * 128], lambda kc: wS[wp][:, kc, :],
                                  [r_wS[wp], r_uT[0]], bank):
                    pass
                S.op("dve", lambda e, mi=mi, bank=bank: e.tensor_tensor(scr[:, X + mi, :], banks[bank][:, :], bvfp[:, 520:1032], ALU.add),
                     reads=[bres[bank], r_const], writes=[r_scr[X + mi]])
                yield
            if G == GORD[0]:
                for hbk in range(2):
                    bank = mbank()
                    for _ in group2(lambda kc, hbk=hbk: uT[1][:, kc, hbk * 128:(hbk + 1) * 128], lambda kc: wS[wp][:, kc, :],
                                    [r_wS[wp], r_uT[1]], bank):
                        pass
                    S.op("dve", lambda e, hbk=hbk, bank=bank: e.tensor_tensor(phalo[:, hbk, :], banks[bank][:, :], bvfp[:, 520:1032], ALU.add),
                         reads=[bres[bank], r_const], writes=[r_phalo])
            for mi in range(4):
                m = 4 * G + mi
                bank = mbank()

                def mm(e, mi=mi, m=m, bank=bank):
                    ins = None
                    for g in range(4):
                        bmi = g if m > 0 else 4 + g
                        bhi = ((m % 8) * 4 + g) if m > 0 else 32 + g
                        e.matmul(banks[bank][:, g * 128:(g + 1) * 128], scr[:, X + mi, g * 128:(g + 1) * 128], bm[:, bmi, :],
                                 start=True, stop=False)
                        ins = e.matmul(banks[bank][:, g * 128:g * 128 + 16], phalo[:, m // 8, g * 128:(g + 1) * 128],
                                       bh[:, bhi, :], start=False, stop=True)
                    return ins
                S.op("pe", mm, reads=[r_scr[X + mi], r_phalo, r_constb], writes=[bres[bank]])
                S.op("dve", lambda e, mi=mi, bank=bank: e.tensor_copy(
                    scr[:, Y:Y + 4, mi * 128:(mi + 1) * 128], banks[bank][:, :].rearrange("p (g t) -> p g t", t=128)),
                    reads=[bres[bank]], writes=[r_scr[Y + g] for g in range(4)])
                yield
            for c in range(4):
                bank = mbank()
                for _ in group2(lambda kc, c=c: wS[wgp][:, kc, c * 128:(c + 1) * 128], lambda kc: uT[0][:, kc, :],
                                  [r_wS[wgp], r_uT[0]], bank):
                    pass
                silu_evac(bank, 4 + c, scr[:, X + c, :], [r_scr[X + c]])
                yield
            for g in range(4):
                bank = mbank()
                S.op("pe", lambda e, g=g, bank=bank: e.matmul(banks[bank][:, :], wpm[:, g, :], scr[:, Y + g, :], start=True, stop=True),
                     reads=[r_scr[Y + g], r_constb], writes=[bres[bank]])
                S.op("dve", lambda e, g=g, bank=bank: e.tensor_scalar(
                    tmpf[:, :], banks[bank][:, :], b_pmT[:, g:g + 1], pscT[:, g:g + 1], ALU.add, ALU.mult),
                    reads=[bres[bank], r_const], writes=[r_tmpf])
                S.op("pool", lambda e, g=g: e.tensor_tensor(scr[:, Y + g, :], tmpf[:, :], scr[:, X + g, :], ALU.mult),
                     reads=[r_tmpf, r_scr[X + g]], writes=[r_scr[Y + g]])
                yield
            if overlapped:
                emit_biasG(G)

        pg = prelude(GORD[0], False, 0)
        for _ in range(4):
            next(pg, None)
        cum1()
        for _ in range(4):
            next(pg, None)
        cum2()


        epg = [None]
        for gi, G in enumerate(GORD):
            Gn = GORD[gi + 1] if gi + 1 < len(GORD) else None
            qb = (G + 1) % 2
            Y = 4 * (G % 3)
            nit_g = 8 * (8 * G + 8)
            sp_items = 40 if gi == 0 else 45
            stride = max(1, nit_g // (50 + sp_items))
            if gi == 0:
                sp_items, stride = 24, 4
            gen = prelude(Gn, True, -(-sp_items // stride)) if Gn is not None else None
            if gi == 0:
                for half in range(2):
                    S.dma("pool", lambda e, half=half: e.dma_start(out=wA[:, half * 4:(half + 1) * 4, 0:1024],
                                                                  in_=w_out_v[:, half * 4:(half + 1) * 4, :]),
                          d_wA, writes=[r_wA])
            bG = G % 2
            if gi == 0:
                emit_biasG(G)

            nk = 4 * G + 4
            kblocks = [(i, i) for i in range(nk)] + [(16 + i, i) for i in range(nk)]
            items = []
            for h in range(8):
                for j, (pos, i) in enumerate(kblocks):
                    items.append((h, pos, i, j == 0, j == len(kblocks) - 1))
            LA = 3
            prev_ep = epg[0]
            first_rest = pg if gi == 0 else None
            ep_xi = None
            ep_stt = False
            ep_at = 0
            mods_done[0] = False
            nit = len(items)
            for idx in range(nit + LA):
                if idx < nit:
                    h, pos, i, first, last = items[idx]
                    pair, hh = h // 2, h % 2
                    own = pos < 16
                    col0 = max(0, i - 4 * G) * 128
                    masked = i >= 4 * G
                    sbk = SB_[idx % len(SB_)]
                    pti = idx % 4

                    def mms(e, pos=pos, col0=col0, masked=masked, sbk=sbk, own=own, pair=pair, h=h, qb=qb):
                        ins = e.matmul(banks[sbk][:, col0:512], KT[:, pair, pos * 128:(pos + 1) * 128],
                                       QTb[qb][:, h, col0:512], start=True, stop=(not masked))
                        if masked:
                            ins = e.matmul(banks[sbk][:, col0:col0 + 128], identb[:, :], masks[:, 0 if own else 1, :],
                                           start=False, stop=True)
                        return ins
                    S.op("pe", mms, reads=[r_KT, r_QTb[qb], r_constb], writes=[bres[sbk]])
                    S.op("act", lambda e, pos=pos, col0=col0, sbk=sbk, pti=pti, h=h, bG=bG: e.activation(
                        PTv[:, pti, col0:512], banks[sbk][:, col0:512], AF.Exp, bias=biasG[bG][:, pos, h:h + 1], scale=0.125),
                        reads=[bres[sbk], r_biasG[bG]], writes=[r_pt[pti]])
                if idx >= LA:
                    jdx = idx - LA
                    h, pos, i, first, last = items[jdx]
                    own = pos < 16
                    col0 = max(0, i - 4 * G) * 128
                    pti = jdx % 4
                    ob = OB_[h % 2]

                    def mmpv(e, pos=pos, col0=col0, pti=pti, h=h, ob=ob, first=first, i=i, own=own, G=G):
                        ins = None
                        for mi in range(col0 // 128, 4):
                            st = first and mi == 0
                            sp_ = (not own) and (i == 4 * G + mi)
                            ins = e.matmul(banks[ob][:, mi * 65:(mi + 1) * 65], PTv[:, pti, mi * 128:(mi + 1) * 128],
                                           V[:, pos, h, 0:65], start=st, stop=sp_, skip_group_check=True)
                        return ins
                    S.op("pe", mmpv, reads=[r_pt[pti], r_V], writes=[bres[ob]])
                    if last:
                        S.op("dve", lambda e, ob=ob: e.reciprocal(
                            rl[:, :], banks[ob][:, 0:260].rearrange("p (a b) -> p a b", b=65)[:, :, 64]),
                            reads=[bres[ob]], writes=[r_rl])
                        S.op("dve", lambda e, ob=ob, h=h: e.tensor_tensor(
                            On[:, :, h * 64:(h + 1) * 64], banks[ob][:, 0:260].rearrange("p (a b) -> p a b", b=65)[:, :, 0:64],
                            rl[:, :].unsqueeze(2).to_broadcast([128, 4, 64]), ALU.mult),
                            reads=[bres[ob], r_rl], writes=[r_On])
                if prev_ep is not None and idx % 2 == 1:
                    try:
                        next(prev_ep)
                    except StopIteration:
                        prev_ep = None
                if prev_ep is None and idx % stride == stride - 1:
                    if first_rest is not None:
                        try:
                            next(first_rest)
                        except StopIteration:
                            first_rest = None
                    elif gen is not None:
                        next(gen, None)
                if ep_xi is None and idx >= nit - 40 and prev_ep is None and (gen is None or mods_done[0]):
                    ep_xi = [load_x(xp[(4 * G + mi) * 128:(4 * G + mi + 1) * 128, :]) for mi in range(4)]
                    ep_at = idx
                if ep_xi is not None and not ep_stt and idx >= max(nit - 20, ep_at + 10):
                    ep_stt = True
                    for xi in ep_xi:
                        S.op("dve", lambda e, xi=xi: e.scalar_tensor_tensor(xb[xi][:, :], xb[xi][:, :], ALPHA, gb[:, :], ALU.mult, ALU.add),
                             reads=[r_gb], writes=[r_xb[xi]])
            if first_rest is not None:
                for _ in first_rest:
                    pass
            if gen is not None:
                for _ in gen:
                    pass
            if prev_ep is not None:
                for _ in prev_ep:
                    pass
            if ep_xi is None:
                ep_xi = [load_x(xp[(4 * G + mi) * 128:(4 * G + mi + 1) * 128, :]) for mi in range(4)]
            if not ep_stt:
                for xi in ep_xi:
                    S.op("dve", lambda e, xi=xi: e.scalar_tensor_tensor(xb[xi][:, :], xb[xi][:, :], ALPHA, gb[:, :], ALU.mult, ALU.add),
                         reads=[r_gb], writes=[r_xb[xi]])
            for cpair in range(2):
                bank = mbank()
                bview = banks[bank][:, :].bitcast(BF16)

                def tr(e, cpair=cpair, bview=bview):
                    ins = None
                    for cc in range(2):
                        c = cpair * 2 + cc
                        for mi in range(4):
                            ins = e.transpose(bview[:, cc * 512 + mi * 128: cc * 512 + (mi + 1) * 128],
                                              On[:, mi, c * 128:(c + 1) * 128], identb[:, :])
                    return ins
                S.op("pe", tr, reads=[r_On, r_constb], writes=[bres[bank]])
                for cc in range(2):
                    c = cpair * 2 + cc
                    S.op("dve", lambda e, c=c, cc=cc, bview=bview, qb=qb: e.tensor_tensor(
                        yT[:, c, :], bview[:, cc * 512:(cc + 1) * 512], gattb[qb][:, c, :], ALU.mult),
                        reads=[bres[bank], r_gattb[qb]], writes=[r_yT[c]])
            def epilogue(G=G, Y=Y, ep_xi=ep_xi):
                for mi in range(4):
                    m = 4 * G + mi
                    xi = ep_xi[mi]
                    sp2 = m % 2
                    bks = [mbank(), mbank()]
                    for half in range(2):
                        def mm(e, half=half, mi=mi, bank=bks[half], Y=Y):
                            ins = None
                            for kc in range(8):
                                lhs = yT[:, kc, mi * 128:(mi + 1) * 128] if kc < 4 else scr[:, Y + kc - 4, mi * 128:(mi + 1) * 128]
                                ins = e.matmul(banks[bank][:, :], lhs, wA[:, kc, half * 512:(half + 1) * 512],
                                               start=(kc == 0), stop=(kc == 7))
                            return ins
                        S.op("pe", mm, reads=r_yT + [r_scr[Y + g] for g in range(4)] + [r_wA], writes=[bres[bks[half]]])
                        tk = tm_rot[0] % 2
                        tm_rot[0] += 1
                        S.op("dve", lambda e, half=half, bank=bks[half], tk=tk: e.tensor_tensor(
                            tmpfs[tk][:, :], banks[bank][:, :], gate_bc[:, half * 512:(half + 1) * 512], ALU.mult),
                            reads=[bres[bks[half]], r_gate], writes=[r_tmpfs[tk]])
                        S.op("dve", lambda e, half=half, xi=xi, tk=tk: e.tensor_tensor(
                            xb[xi][:, half * 512:(half + 1) * 512], xb[xi][:, half * 512:(half + 1) * 512], tmpfs[tk][:, :], ALU.add),
                            reads=[r_tmpfs[tk]], writes=[r_xb[xi]])
                        S.op("dve", lambda e, half=half, xi=xi, sp2=sp2: e.bn_stats(stats2[sp2][:, half, :], xb[xi][:, half * 512:(half + 1) * 512]),
                             reads=[r_xb[xi]], writes=[r_sm[sp2]])
                        yield
                    S.op("dve", lambda e, sp2=sp2: e.bn_aggr(mv2[sp2][:, :], stats2[sp2][:, :, :].rearrange("p a b -> p (a b)")),
                         writes=[r_sm[sp2]])
                    S.op("pool", lambda e, sp2=sp2: e.tensor_scalar(ve2[sp2][:, :], mv2[sp2][:, 1:2], EPS, 0.0, ALU.add, ALU.add),
                         reads=[r_sm[sp2]], writes=[r_sm[sp2]])
                    S.op("pool", lambda e, sp2=sp2: e.tensor_tensor(rstd2[sp2][:, :], ve2[sp2][:, :], mhalf[:, :], ALU.pow),
                         reads=[r_small], writes=[r_sm[sp2]])
                    S.op("dve", lambda e, xi=xi, sp2=sp2: e.tensor_scalar(xb[xi][:, :], xb[xi][:, :], mv2[sp2][:, 0:1], rstd2[sp2][:, 0:1],
                                                                         ALU.subtract, ALU.mult),
                         reads=[r_sm[sp2]], writes=[r_xb[xi], r_sm[sp2]])
                    S.op("pool", lambda e, xi=xi: e.tensor_tensor(xb[xi][:, :], xb[xi][:, :], lng[:, :], ALU.mult),
                         reads=[r_const], writes=[r_xb[xi]])
                    S.op("pool", lambda e, xi=xi: e.tensor_tensor(xb[xi][:, :], xb[xi][:, :], lnb[:, :], ALU.add),
                         reads=[r_const], writes=[r_xb[xi]])
                    S.dma("pool", lambda e, xi=xi, m=m: e.dma_start(out=out_d[m * 128:(m + 1) * 128, :], in_=xb[xi][:, :]),
                          d_out[xi], reads=[r_xb[xi]])
                    yield

            def last_epilogue(G=G, Y=Y, ep_xi=ep_xi):
                def part_a(mi):
                    m = 4 * G + mi
                    xi = ep_xi[mi]
                    sp2 = m % 2
                    bks = [mbank(), mbank()]
                    for half in range(2):
                        def mm(e, half=half, mi=mi, bank=bks[half], Y=Y):
                            ins = None
                            for kc in range(8):
                                lhs = yT[:, kc, mi * 128:(mi + 1) * 128] if kc < 4 else scr[:, Y + kc - 4, mi * 128:(mi + 1) * 128]
                                ins = e.matmul(banks[bank][:, :], lhs, wA[:, kc, half * 512:(half + 1) * 512],
                                               start=(kc == 0), stop=(kc == 7))
                            return ins
                        S.op("pe", mm, reads=r_yT + [r_scr[Y + g] for g in range(4)] + [r_wA], writes=[bres[bks[half]]])
                        tk = tm_rot[0] % 2
                        tm_rot[0] += 1
                        S.op("dve", lambda e, half=half, bank=bks[half], tk=tk: e.tensor_tensor(
                            tmpfs[tk][:, :], banks[bank][:, :], gate_bc[:, half * 512:(half + 1) * 512], ALU.mult),
                            reads=[bres[bks[half]], r_gate], writes=[r_tmpfs[tk]])
                        S.op("dve", lambda e, half=half, xi=xi, tk=tk: e.tensor_tensor(
                            xb[xi][:, half * 512:(half + 1) * 512], xb[xi][:, half * 512:(half + 1) * 512], tmpfs[tk][:, :], ALU.add),
                            reads=[r_tmpfs[tk]], writes=[r_xb[xi]])
                        S.op("dve", lambda e, half=half, xi=xi, sp2=sp2: e.bn_stats(stats2[sp2][:, half, :], xb[xi][:, half * 512:(half + 1) * 512]),
                             reads=[r_xb[xi]], writes=[r_sm[sp2]])
                    S.op("dve", lambda e, sp2=sp2: e.bn_aggr(mv2[sp2][:, :], stats2[sp2][:, :, :].rearrange("p a b -> p (a b)")),
                         writes=[r_sm[sp2]])
                    S.op("pool", lambda e, sp2=sp2: e.tensor_scalar(ve2[sp2][:, :], mv2[sp2][:, 1:2], EPS, 0.0, ALU.add, ALU.add),
                         reads=[r_sm[sp2]], writes=[r_sm[sp2]])
                    S.op("pool", lambda e, sp2=sp2: e.tensor_tensor(rstd2[sp2][:, :], ve2[sp2][:, :], mhalf[:, :], ALU.pow),
                         reads=[r_small], writes=[r_sm[sp2]])

                def part_b(mi):
                    m = 4 * G + mi
                    xi = ep_xi[mi]
                    sp2 = m % 2
                    S.op("dve", lambda e, xi=xi, sp2=sp2: e.tensor_scalar(xb[xi][:, :], xb[xi][:, :], mv2[sp2][:, 0:1], rstd2[sp2][:, 0:1],
                                                                         ALU.subtract, ALU.mult),
                         reads=[r_sm[sp2]], writes=[r_xb[xi], r_sm[sp2]])
                    S.op("dve", lambda e, xi=xi: e.tensor_tensor(xb[xi][:, :], xb[xi][:, :], lng[:, :], ALU.mult),
                         reads=[r_const], writes=[r_xb[xi]])
                    S.op("pool", lambda e, xi=xi: e.tensor_tensor(xb[xi][:, :], xb[xi][:, :], lnb[:, :], ALU.add),
                         reads=[r_const], writes=[r_xb[xi]])
                    S.dma("pool", lambda e, xi=xi, m=m: e.dma_start(out=out_d[m * 128:(m + 1) * 128, :], in_=xb[xi][:, :]),
                          d_out[xi], reads=[r_xb[xi]])

                part_a(0)
                for mi in range(1, 4):
                    part_a(mi)
                    part_b(mi - 1)
                part_b(3)

            if gi == len(GORD) - 1:
                last_epilogue()
            else:
                epg[0] = epilogue()
        S.wait_only("pool", [Tok(d.sem, d.n * 16, "dma", d.key) for d in d_out])
        if debug:
            S.wait_only("sp", [Tok(S.sem[en], S.cnt[en], en, en) for en in ("pe", "act", "dve", "pool")]
                        + [Tok(d.sem, d.n * 16, "dma", d.key) for d in d_out])
            dumps = {"adaT": (adaT, F32), "scale1": (scale1, F32), "gate_bc": (gate_bc, F32), "gb": (gb, F32),
                     "KT": (KT, BF16), "V": (V, BF16), "Cpos": (Cpos, F32), "nlf": (nlf, F32), "biasG1": (biasG[1], F32),
                     "QT": (QT, BF16), "gatt": (gatt, BF16), "yT": (yT, BF16), "scr": (scr, BF16), "gatt1": (gatt1, BF16), "phalo": (phalo, BF16),
                     "wA": (wA, BF16), "hb1": (hb[1], F32), "uT1": (uT[1], BF16), "uT0": (uT[0], BF16), "biasG0": (biasG[0], F32), "sc": (sc, F32), "tot": (tot, F32),
                     "Z": (Z, F32), "bvfp": (bvfp, F32), "masks": (masks, BF16), "bm": (bm, BF16)}
            for nm, (tl, dt) in dumps.items():
                shp = list(tl.shape)
                dd = nc.dram_tensor("dbg_" + nm, shp, dt, kind="ExternalOutput").ap()
                full = tuple(slice(None) for _ in shp)
                S.dma("sp", lambda e, dd=dd, tl=tl, full=full: e.dma_start(out=dd[full], in_=tl[full]), d_tmp)
            S.wait_only("sp", [Tok(d_tmp.sem, d_tmp.n * 16, "dma", d_tmp.key)])

        with nc.Block() as block:
            @block.tensor
            def _(e):
                S.replay("pe", e)

            @block.scalar
            def _(e):
                S.replay("act", e)

            @block.vector
            def _(e):
                S.replay("dve", e)

            @block.gpsimd
            def _(e):
                S.replay("pool", e)

            @block.sync
            def _(e):
                S.replay("sp", e)
    return nc


def _consts(par):
    ident = np.eye(128, dtype=np.float32)
    ones = np.ones((128, 128), np.float32)
    s = np.arange(128)[:, None]
    t = np.arange(128)[None, :]
    U = (s <= t).astype(np.float32)
    glob = np.array([2 * p + par if p < 16 else 2 * (p - 16) + 1 - par for p in range(32)])
    pred = (glob[:, None] < glob[None, :]).astype(np.float32)
    masks = np.zeros((128, 2, 128), np.float32)
    masks[:, 0, :] = np.where(s <= t, 0.0, NEG)
    masks[:, 1, :] = 0.0 if par == 1 else NEG
    bm = np.zeros((128, 8, 128), np.float32)
    bh = np.zeros((128, 36, 16), np.float32)
    eye = np.eye(128, dtype=np.float32)
    for g, w in enumerate(WINS):
        inwin = ((t - s) >= 0) & ((t - s) < w)
        bm[:, g, :] = np.where(inwin, 1.0 / w, 0.0) - eye
        if par == 0:
            cnt = np.minimum(t + 1, w).astype(np.float32)
            bm[:, 4 + g, :] = np.where(inwin, 1.0 / cnt, 0.0) - eye
        else:
            bm[:, 4 + g, :] = bm[:, g, :]
        for j in range(8):
            for i in range(16):
                for tt in range(16):
                    if tt + 16 - i < w:
                        bh[j * 16 + i, j * 4 + g, tt] = 1.0 / w
        if par == 1:
            bh[:, 32 + g, :] = bh[:, 0 * 4 + g, :]
    return ident, ones, U, pred, masks, bm, bh


def _colT(v, n):
    return np.ascontiguousarray(np.asarray(v, np.float32).reshape(n, 128).T)


_NC_CACHE = {}
_DEBUG = [False]
_NG = [4]


def kernel(x, c, w_ada, b_ada, w_in, b_in, w_pool_mix, b_pool_mix, pool_scale, w_out, b_out, ln_g, ln_b):
    x = np.asarray(x, np.float32)
    c = np.asarray(c, np.float32)
    w_ada = np.ascontiguousarray(np.asarray(w_ada, np.float32)[0])
    b_ada = np.asarray(b_ada, np.float32)[0]
    w_in = np.ascontiguousarray(np.asarray(w_in, np.float32)[0])
    b_in = np.asarray(b_in, np.float32)[0]
    w_pm = np.ascontiguousarray(np.asarray(w_pool_mix, np.float32)[0])
    b_pm = np.asarray(b_pool_mix, np.float32)[0]
    psc = np.asarray(pool_scale, np.float32)[0]
    w_out = np.ascontiguousarray(np.asarray(w_out, np.float32)[0])
    b_out = np.asarray(b_out, np.float32)[0]
    ln_g = np.asarray(ln_g, np.float32)[0]
    ln_b = np.asarray(ln_b, np.float32)[0]

    common = {
        "w_ada": w_ada,
        "b_adaT": _colT(b_ada[0:2048], 16),
        "b_gate": np.ascontiguousarray(b_ada[2048:3072].reshape(1, D)),
        "w_in": w_in,
        "b_qT": _colT(b_in[0:512], 4),
        "b_kT": _colT(b_in[512:1024], 4),
        "b_gT": _colT(b_in[2056:3080], 8),
        "b_vfp": np.ascontiguousarray(b_in[1024:2056].reshape(1, 1032)),
        "w_pm": w_pm,
        "b_pmT": np.ascontiguousarray(b_pm.T),
        "pscT": _colT(psc, 4),
        "w_out": w_out,
        "b_out": np.ascontiguousarray(b_out.reshape(1, D)),
        "ln_g": np.ascontiguousarray(ln_g.reshape(1, D)),
        "ln_b": np.ascontiguousarray(ln_b.reshape(1, D)),
    }
    in_maps = []
    for core in range(NCORES):
        b, par = core // 2, core % 2
        xb_ = x[b].reshape(NBLK, 128, D)
        own = [2 * m + par for m in range(16)]
        oth = [2 * m + 1 - par for m in range(16)]
        xp = np.ascontiguousarray(xb_[own + oth].reshape(SEQ, D))
        xh = np.zeros((256, D), np.float32)
        for m in range(16):
            g = own[m]
            if g > 0:
                xh[m * 16:(m + 1) * 16] = x[b, g * 128 - 16:g * 128]
        ident, ones, U, pred, masks, bm, bh = _consts(par)
        mp = dict(common)
        mp.update({"xp": np.ascontiguousarray(xp[:2048]), "xpT": np.ascontiguousarray(xp.T), "xhT": np.ascontiguousarray(xh.T), "cT": _colT(c[b], 8), "ident": ident, "ones": ones, "U": U,
                   "pred": pred, "masks": masks, "bm": bm, "bh": bh})
        in_maps.append(mp)

    if "nc" not in _NC_CACHE:
        _NC_CACHE["nc"] = build_nc(_DEBUG[0])
    nc = _NC_CACHE["nc"]
    res = run_bass_kernel_spmd(nc, in_maps, core_ids=list(range(NCORES)))
    if _DEBUG[0]:
        _DEBUG.append(res.results)
    out = np.empty((4, SEQ, D), np.float32)
    for core in range(NCORES):
        b, par = core // 2, core % 2
        o = np.asarray(res.results[core]["out"], np.float32).reshape(16, 128, D)
        for m in range(16):
            g = 2 * m + par
            out[b, g * 128:(g + 1) * 128] = o[m]
    return out
```
